# Optimizing a Trainium2 kernel written in Bass

```python
import jax, jax.numpy as jnp
from jax import lax
import numpy as np

D_MODEL = 4096
BATCH = 8
SEQ = 2048
DEPTH = 2

M_WIDTH = 4096
M_HEADS = 8
M_HEAD_DIM = M_WIDTH // M_HEADS
M_QKV_BLOCK = 4
M_CONV = 4
M_CHUNK = 64
A_HEADS = 32
A_HEAD_DIM = 128
A_WIDTH = A_HEADS * A_HEAD_DIM
A_LATENT = 512
IDX_HEADS = 32
IDX_DIM = 128
IDX_TOPK = 256
Q_BLOCK = 128
EPS = 1e-6

SPLITS = (M_WIDTH, M_WIDTH, M_WIDTH, M_HEADS, M_HEADS,
          A_WIDTH, A_LATENT, A_WIDTH,
          IDX_HEADS * IDX_DIM, IDX_DIM, IDX_HEADS,
          D_MODEL, D_MODEL)
N_IN = sum(SPLITS)

kernel_name = "hybrid_mlstm_dsa_gated_block"


def rmsnorm(x, g):
    x32 = x.astype(jnp.float32)
    y = x32 * lax.rsqrt(jnp.mean(x32 * x32, axis=-1, keepdims=True) + EPS)
    return (y * g.astype(jnp.float32)).astype(x.dtype)


def layernorm(x, g, b):
    x32 = x.astype(jnp.float32)
    mu = jnp.mean(x32, axis=-1, keepdims=True)
    xc = x32 - mu
    y = xc * lax.rsqrt(jnp.mean(xc * xc, axis=-1, keepdims=True) + EPS)
    return y * g.astype(jnp.float32) + b.astype(jnp.float32)


def head_layernorm(h, g):
    mu = jnp.mean(h, axis=-1, keepdims=True)
    hc = h - mu
    y = hc * lax.rsqrt(jnp.mean(hc * hc, axis=-1, keepdims=True) + EPS)
    return y * g.astype(jnp.float32).reshape(M_HEADS, M_HEAD_DIM)


def causal_dwconv(x, w, b):
    y = lax.conv_general_dilated(x, w[:, None, :].astype(x.dtype), window_strides=(1,),
                                 padding=[(M_CONV - 1, 0)],
                                 dimension_numbers=('NWC', 'WIO', 'NWC'),
                                 feature_group_count=x.shape[-1])
    return y + b.astype(x.dtype)


def blockdiag(x, w):
    xs = x.reshape(*x.shape[:-1], w.shape[0], w.shape[1])
    return jnp.einsum('...gi,gio->...go', xs, w).reshape(x.shape)


def mlstm_chunkwise(q, k, v, i_pre, f_pre):
    B, S, H, d = q.shape
    nc = S // M_CHUNK
    f32 = jnp.float32

    def to_chunks(a):
        a = a.astype(f32).reshape(B, nc, M_CHUNK, H, *a.shape[3:])
        return jnp.moveaxis(a, (1, 3), (0, 2))

    qc = to_chunks(q)
    kc = to_chunks(k) * (d ** -0.5)
    vc = to_chunks(v)
    ic = to_chunks(i_pre)
    fc = to_chunks(jax.nn.log_sigmoid(f_pre.astype(f32)))
    tril = jnp.tril(jnp.ones((M_CHUNK, M_CHUNK), dtype=bool))

    def step(carry, inp):
        C, n, m = carry
        qb, kb, vb, ib, fb = inp
        b = jnp.cumsum(fb, axis=-1)
        dmat = jnp.where(tril, b[..., :, None] - b[..., None, :] + ib[..., None, :], -jnp.inf)
        m_inter = b + m[..., None]
        m_t = jnp.maximum(m_inter, jnp.max(dmat, axis=-1))
        a = jnp.einsum('bhtd,bhsd->bhts', qb, kb) * jnp.exp(dmat - m_t[..., None])
        inter = jnp.exp(m_inter - m_t)
        num = jnp.einsum('bhts,bhsd->bhtd', a, vb) + inter[..., None] * jnp.einsum('bhtd,bhde->bhte', qb, C)
        den = jnp.sum(a, axis=-1) + inter * jnp.einsum('bhtd,bhd->bht', qb, n)
        h = num / jnp.maximum(jnp.abs(den), jnp.exp(-m_t))[..., None]
        b_last = b[..., -1]
        g = b_last[..., None] - b + ib
        m_new = jnp.maximum(b_last + m, jnp.max(g, axis=-1))
        w = jnp.exp(g - m_new[..., None])
        decay = jnp.exp(b_last + m - m_new)
        C_new = decay[..., None, None] * C + jnp.einsum('bhsd,bhse->bhde', kb * w[..., None], vb)
        n_new = decay[..., None] * n + jnp.einsum('bhs,bhsd->bhd', w, kb)
        return (C_new, n_new, m_new), h

    init = (jnp.zeros((B, H, d, d), f32), jnp.zeros((B, H, d), f32), jnp.zeros((B, H), f32))
    _, hs = lax.scan(step, init, (qc, kc, vc, ic, fc))
    return jnp.moveaxis(hs, (0, 2), (1, 3)).reshape(B, S, H, d)


def mlstm_branch(xm, om, zm, ip, fp, conv_w, conv_b, wq, wk, wv, b_i, b_f, g_head, skip):
    B, S, _ = xm.shape
    xc = jax.nn.silu(causal_dwconv(xm, conv_w, conv_b))
    q = blockdiag(xc, wq).reshape(B, S, M_HEADS, M_HEAD_DIM)
    k = blockdiag(xc, wk).reshape(B, S, M_HEADS, M_HEAD_DIM)
    v = blockdiag(xm, wv).reshape(B, S, M_HEADS, M_HEAD_DIM)
    h = mlstm_chunkwise(q, k, v, ip + b_i, fp + b_f)
    h = head_layernorm(h, g_head).reshape(B, S, M_WIDTH).astype(xm.dtype)
    h = jax.nn.sigmoid(om) * h
    return (h + skip * xc) * jax.nn.silu(zm)


def dsa_branch(qa, ckv, za, qi, ki, wi, g_ckv, w_uk, w_uv, g_kidx, b_kidx):
    B, S, _ = qa.shape
    nb = S // Q_BLOCK
    topk = min(IDX_TOPK, S // 4)
    f32 = jnp.float32
    c = rmsnorm(ckv, g_ckv)
    kidx = layernorm(ki, g_kidx, b_kidx)
    qa = qa.reshape(B, S, A_HEADS, A_HEAD_DIM)
    qi = qi.reshape(B, S, IDX_HEADS, IDX_DIM)
    wi = wi.astype(f32) * (IDX_HEADS ** -0.5) * (IDX_DIM ** -0.5)
    s_pos = jnp.arange(S)

    def blocks(a):
        return jnp.moveaxis(a.reshape(B, nb, Q_BLOCK, *a.shape[2:]), 1, 0)

    def one_block(args):
        blk, q_b, qi_b, wi_b = args
        t_pos = blk * Q_BLOCK + jnp.arange(Q_BLOCK)
        logits = jnp.einsum('bthi,bsi->bths', qi_b.astype(f32), kidx)
        score = jnp.einsum('bths,bth->bts', jax.nn.relu(logits), wi_b)
        score = jnp.where(s_pos[None, :] <= t_pos[:, None], score, -jnp.inf)
        _, idx = lax.top_k(score, topk)
        valid = idx <= t_pos[None, :, None]
        c_sel = jax.vmap(lambda cb, ib: cb[ib])(c, idx)
        q_lat = jnp.einsum('bthd,hcd->bthc', q_b, w_uk)
        att = jnp.einsum('bthc,btkc->bthk', q_lat, c_sel).astype(f32) * (A_HEAD_DIM ** -0.5)
        att = jnp.where(valid[:, :, None, :], att, -jnp.inf)
        p = jax.nn.softmax(att, axis=-1).astype(c.dtype)
        o_lat = jnp.einsum('bthk,btkc->bthc', p, c_sel)
        return jnp.einsum('bthc,hcd->bthd', o_lat, w_uv)

    out = lax.map(one_block, (jnp.arange(nb), blocks(qa), blocks(qi), blocks(wi)))
    out = jnp.moveaxis(out, 0, 1).reshape(B, S, A_WIDTH)
    return out * jax.nn.silu(za)


def setup_inputs(seed: int = 0) -> dict:
    key = jax.random.key(seed)
    ks = jax.random.split(key, 21)
    nrm = lambda k, shape, s: jax.random.normal(k, shape, jnp.float32) * s
    nblk = M_WIDTH // M_QKV_BLOCK
    return {
        "x": nrm(ks[0], (BATCH, SEQ, D_MODEL), 1.0),
        "g_norm": 1.0 + nrm(ks[1], (DEPTH, D_MODEL), 0.02),
        "w_in": nrm(ks[2], (DEPTH, D_MODEL, N_IN), D_MODEL ** -0.5),
        "conv_w": nrm(ks[3], (DEPTH, M_CONV, M_WIDTH), M_CONV ** -0.5),
        "conv_b": nrm(ks[4], (DEPTH, M_WIDTH), 0.02),
        "w_q_m": nrm(ks[5], (DEPTH, nblk, M_QKV_BLOCK, M_QKV_BLOCK), M_QKV_BLOCK ** -0.5),
        "w_k_m": nrm(ks[6], (DEPTH, nblk, M_QKV_BLOCK, M_QKV_BLOCK), M_QKV_BLOCK ** -0.5),
        "w_v_m": nrm(ks[7], (DEPTH, nblk, M_QKV_BLOCK, M_QKV_BLOCK), M_QKV_BLOCK ** -0.5),
        "b_i": nrm(ks[8], (DEPTH, M_HEADS), 0.1),
        "b_f": jnp.linspace(3.0, 6.0, M_HEADS, dtype=jnp.float32)[None, :] + nrm(ks[9], (DEPTH, M_HEADS), 0.1),
        "g_head_m": 1.0 + nrm(ks[10], (DEPTH, M_WIDTH), 0.02),
        "skip_m": 1.0 + nrm(ks[11], (DEPTH, M_WIDTH), 0.02),
        "g_ckv": 1.0 + nrm(ks[12], (DEPTH, A_LATENT), 0.02),
        "w_uk": nrm(ks[13], (DEPTH, A_HEADS, A_LATENT, A_HEAD_DIM), A_LATENT ** -0.5),
        "w_uv": nrm(ks[14], (DEPTH, A_HEADS, A_LATENT, A_HEAD_DIM), A_LATENT ** -0.5),
        "g_kidx": 1.0 + nrm(ks[15], (DEPTH, IDX_DIM), 0.02),
        "b_kidx": nrm(ks[16], (DEPTH, IDX_DIM), 0.02),
        "w_bm": nrm(ks[17], (DEPTH, M_WIDTH, D_MODEL), M_WIDTH ** -0.5),
        "w_ba": nrm(ks[18], (DEPTH, A_WIDTH, D_MODEL), A_WIDTH ** -0.5),
        "w_out": nrm(ks[19], (DEPTH, D_MODEL, D_MODEL), D_MODEL ** -0.5),
        "g_final": 1.0 + nrm(ks[20], (D_MODEL,), 0.02),
    }


def reference(x, g_norm, w_in, conv_w, conv_b, w_q_m, w_k_m, w_v_m, b_i, b_f, g_head_m, skip_m,
              g_ckv, w_uk, w_uv, g_kidx, b_kidx, w_bm, w_ba, w_out, g_final):
    split_at = [int(s) for s in np.cumsum(SPLITS)[:-1]]
    for l in range(DEPTH):
        xn = rmsnorm(x, g_norm[l])
        proj = xn @ w_in[l]
        (xm, om, zm, ip, fp, qa, ckv, za, qi, ki, wi, gm, ga) = jnp.split(proj, split_at, axis=-1)
        y_m = mlstm_branch(xm, om, zm, ip, fp, conv_w[l], conv_b[l], w_q_m[l], w_k_m[l], w_v_m[l],
                           b_i[l], b_f[l], g_head_m[l], skip_m[l])
        y_a = dsa_branch(qa, ckv, za, qi, ki, wi, g_ckv[l], w_uk[l], w_uv[l],
                         g_kidx[l], b_kidx[l])
        merged = jax.nn.sigmoid(gm) * (y_m @ w_bm[l]) + jax.nn.sigmoid(ga) * (y_a @ w_ba[l])
        x = x + merged @ w_out[l]
    return rmsnorm(x, g_final)
```

```python
import numpy as np
import concourse.bass as bass
import concourse.mybir as mybir
from concourse.bass_utils import run_bass_kernel_spmd
from contextlib import ExitStack

F32 = mybir.dt.float32
BF16 = mybir.dt.bfloat16
AF = mybir.ActivationFunctionType
ALU = mybir.AluOpType
AX = mybir.AxisListType

SEM_CAP = 30000
DMA_RING = 8
COMPUTE = ("pe", "act", "dve", "pool")


class View:
    __slots__ = ("ap", "buf", "lo", "hi")

    def __init__(self, ap, buf, lo, hi):
        self.ap, self.buf, self.lo, self.hi = ap, buf, lo, hi


class _Seg:
    def __init__(self, t, lo, hi):
        self.t, self.lo, self.hi = t, lo, hi

    def __getitem__(self, idx):
        return View(self.t.h[idx], self.t, self.lo, self.hi)


class Tile:
    def __init__(self, handle, nseg=1, name=""):
        self.h = handle
        self.nseg = nseg
        self.name = name
        self.lw = [None] * nseg
        self.rd = [dict() for _ in range(nseg)]
        self.psum = False
        self.prd = {}

    def __getitem__(self, idx):
        return View(self.h[idx], self, 0, self.nseg)

    def s(self, lo, hi=None):
        if hi is None:
            hi = lo + 1
        assert 0 <= lo < hi <= self.nseg, (self.name, lo, hi, self.nseg)
        return _Seg(self, lo, hi)

    def v(self, ap, lo=0, hi=None):
        return View(ap, self, lo, self.nseg if hi is None else hi)


class Op:
    __slots__ = ("eng", "fn", "deps", "dma", "signal", "sem", "val", "gidx", "qidx")

    def __init__(self, eng, fn, dma):
        self.eng, self.fn, self.dma = eng, fn, dma
        self.deps = set()
        self.signal = False
        self.sem = None
        self.val = 0
        self.gidx = 0
        self.qidx = 0


class Sched:
    def __init__(self, nc):
        self.nc = nc
        self.ops = []
        self.stack = ExitStack()
        self.pending_barrier = {}
        self.last_op = {}
        self.all_dma = []
        self.ndma = {}
        self._n = 0

    def sbuf(self, name, shape, dtype, nseg=1, stack=None):
        st = stack if stack is not None else self.stack
        self._n += 1
        h = st.enter_context(self.nc.sbuf_tensor(f"{name}_{self._n}", list(shape), dtype))
        fb = int(np.prod(shape[1:])) * (4 if dtype == F32 else 2)
        if fb % 64 != 0:
            st.enter_context(self.nc.sbuf_tensor(f"pad_{self._n}", [128, (64 - fb % 64) // 2], BF16))
        return Tile(h, nseg, name)

    def psum(self, name, shape, dtype, nseg=1, stack=None):
        st = stack if stack is not None else self.stack
        self._n += 1
        fb = int(np.prod(shape[1:])) * (4 if dtype == F32 else 2)
        assert fb % 2048 == 0, ("psum tiles must be whole banks", name, shape)
        h = st.enter_context(self.nc.psum_tensor(f"{name}_{self._n}", list(shape), dtype))
        t = Tile(h, nseg, name)
        t.psum = True
        t.nbank = fb // 2048
        return t

    def dram(self, name, shape, dtype, kind="Internal", nseg=1):
        h = self.nc.dram_tensor(name, list(shape), dtype, kind=kind)
        return Tile(h, nseg, name)

    def op(self, eng, fn, reads=(), writes=(), dma=False):
        o = Op(eng, fn, dma)
        o.gidx = len(self.ops)
        pb = self.pending_barrier.pop(eng, None)
        if pb:
            o.deps |= pb
        for v in reads:
            b = v.buf
            for sg in range(v.lo, v.hi):
                w = b.lw[sg]
                if w is not None:
                    o.deps.add(w)
            if b.psum:
                banks = range(b.nbank) if b.nseg != b.nbank else range(v.lo, v.hi)
                for bk in banks:
                    d = b.prd.setdefault(bk, {})
                    for e2, r in d.items():
                        if e2 != eng:
                            o.deps.add(r)
                    d[eng] = o
        for v in writes:
            b = v.buf
            for sg in range(v.lo, v.hi):
                w = b.lw[sg]
                if w is not None:
                    o.deps.add(w)
                for r in b.rd[sg].values():
                    o.deps.add(r)
        for v in reads:
            b = v.buf
            for sg in range(v.lo, v.hi):
                key = ("dma", o.gidx) if dma else eng
                b.rd[sg][key] = o
        for v in writes:
            b = v.buf
            for sg in range(v.lo, v.hi):
                b.lw[sg] = o
                b.rd[sg] = {}
        o.deps.discard(o)
        self.ops.append(o)
        self.last_op[eng] = o
        if dma:
            self.all_dma.append(o)
        return o

    def barrier(self):
        deps = set(self.last_op.values()) | set(self.all_dma)
        self.all_dma = []
        for e in ("pe", "act", "dve", "pool", "sp"):
            self.pending_barrier[e] = set(deps) | self.pending_barrier.get(e, set())

    def mm(self, out, lhsT, rhs, start=True, stop=True):
        return self.op("pe", lambda e: e.matmul(out.ap, lhsT.ap, rhs.ap, start=start, stop=stop),
                       reads=[lhsT, rhs], writes=[out])

    def transpose(self, out, in_, ident):
        return self.op("pe", lambda e: e.transpose(out.ap, in_.ap, ident.ap),
                       reads=[in_, ident], writes=[out])

    def act(self, out, in_, func, bias=None, scale=None, accum=None, eng="act"):
        reads = [in_]
        kw = {}
        if bias is not None:
            if isinstance(bias, View):
                reads.append(bias)
                kw["bias"] = bias.ap
            else:
                kw["bias"] = bias
        if scale is not None:
            if isinstance(scale, View):
                reads.append(scale)
                kw["scale"] = scale.ap
            else:
                kw["scale"] = scale
        writes = [out]
        if accum is not None:
            writes.append(accum)
            kw["accum_out"] = accum.ap
        return self.op("act", lambda e: e.activation(out.ap, in_.ap, func, **kw), reads=reads, writes=writes)

    def tt(self, eng, out, in0, in1, op):
        return self.op(eng, lambda e: e.tensor_tensor(out.ap, in0.ap, in1.ap, op), reads=[in0, in1], writes=[out])

    def ts(self, eng, out, in0, s1, op0, s2=None, op1=None, accum=None):
        reads = [in0]
        a1 = s1.ap if isinstance(s1, View) else s1
        a2 = s2.ap if isinstance(s2, View) else s2
        if isinstance(s1, View):
            reads.append(s1)
        if isinstance(s2, View):
            reads.append(s2)
        writes = [out]
        kw = {}
        if op1 is not None:
            kw["op1"] = op1
        if accum is not None:
            kw["accum_out"] = accum.ap
            writes.append(accum)
        return self.op(eng, lambda e: e.tensor_scalar(out.ap, in0.ap, a1, a2, op0, **kw), reads=reads, writes=writes)

    def stt(self, eng, out, in0, scalar, in1, op0, op1, accum=None):
        reads = [in0, in1]
        a = scalar.ap if isinstance(scalar, View) else scalar
        if isinstance(scalar, View):
            reads.append(scalar)
        writes = [out]
        kw = {}
        if accum is not None:
            kw["accum_out"] = accum.ap
            writes.append(accum)
        return self.op(eng, lambda e: e.scalar_tensor_tensor(out.ap, in0.ap, a, in1.ap, op0, op1, **kw),
                       reads=reads, writes=writes)

    def rsqrt(self, out, in_, scale, eps):
        self.act(out, in_, AF.Sqrt, bias=eps, scale=scale)
        return self.op("dve", lambda e: e.reciprocal(out.ap, out.ap), reads=[out], writes=[out])

    def copy(self, eng, out, in_):
        if eng == "act":
            return self.op("act", lambda e: e.copy(out.ap, in_.ap), reads=[in_], writes=[out])
        return self.op(eng, lambda e: e.tensor_copy(out.ap, in_.ap), reads=[in_], writes=[out])

    def memset(self, eng, out, val):
        return self.op(eng, lambda e: e.memset(out.ap, val), writes=[out])

    def dma(self, out, in_, eng="sp"):
        return self.op(eng, lambda e: e.dma_start(out=out.ap, in_=in_.ap), reads=[in_], writes=[out], dma=True)

    def emit(self):
        nc = self.nc
        ops = self.ops
        for o in ops:
            for d in o.deps:
                if d.eng == "pe" and o.eng == "pe" and not d.dma and not o.dma:
                    continue
                d.signal = True
        est = ExitStack()
        sems = {}

        def new_sem(tag):
            sems[tag] = est.enter_context(nc.semaphore(tag))
            return sems[tag]

        cnt = {e: 0 for e in COMPUTE + ("sp",)}
        cur = {}
        dcount = {}
        dring = {}
        for o in ops:
            if o.dma:
                q = o.eng
                i = dcount.get(q, 0)
                dcount[q] = i + 1
                slot = i % DMA_RING
                if (q, slot) not in dring:
                    dring[(q, slot)] = [new_sem(f"d_{q}_{slot}"), 0, None]
                ent = dring[(q, slot)]
                prev = ent[2]
                if prev is not None:
                    o.deps.add(prev)
                    prev.signal = True
                ent[1] += 16
                ent[2] = o
                o.sem, o.val = ent[0], ent[1]
                o.signal = True
            elif o.signal:
                e = o.eng
                c = cnt[e]
                if c % SEM_CAP == 0:
                    cur[e] = new_sem(f"s_{e}_{c // SEM_CAP}")
                cnt[e] = c + 1
                o.sem, o.val = cur[e], (c % SEM_CAP) + 1
                o.qidx = c + 1
        by_eng = {e: [] for e in ("pe", "act", "dve", "pool", "sp")}
        for o in ops:
            by_eng[o.eng].append(o)
        nwaits = [0]

        self.streams = {}

        def emit_engine(ename, eobj):
            stream = self.streams.setdefault(ename, [])
            waited_c = {e: 0 for e in COMPUTE + ("sp",)}
            waited_d = {}
            for o in by_eng[ename]:
                need_c = {}
                need_d = {}
                for d in o.deps:
                    if d.dma:
                        k = id(d.sem)
                        if d.val > waited_d.get(k, 0):
                            if k not in need_d or need_d[k][1] < d.val:
                                need_d[k] = (d.sem, d.val)
                    else:
                        if d.eng == "pe" and ename == "pe" and not o.dma:
                            continue
                        if not d.signal:
                            continue
                        if d.qidx > waited_c[d.eng]:
                            if d.eng not in need_c or need_c[d.eng].qidx < d.qidx:
                                need_c[d.eng] = d
                for e, d in need_c.items():
                    eobj.wait_ge(d.sem, d.val)
                    waited_c[e] = d.qidx
                    nwaits[0] += 1
                for k, (sem, val) in need_d.items():
                    eobj.wait_ge(sem, val)
                    waited_d[k] = val
                    nwaits[0] += 1
                ins = o.fn(eobj)
                if o.signal:
                    ins.then_inc(o.sem, 16 if o.dma else 1)
                stream.append(([(id(d.sem), d.val) for d in need_c.values()] + [(k, v) for k, (sm_, v) in need_d.items()],
                               (id(o.sem), 16 if o.dma else 1) if o.signal else None, o.gidx))

        with nc.Block() as block:
            @block.tensor
            def _(e):
                emit_engine("pe", e)

            @block.scalar
            def _(e):
                emit_engine("act", e)

            @block.vector
            def _(e):
                emit_engine("dve", e)

            @block.gpsimd
            def _(e):
                emit_engine("pool", e)

            @block.sync
            def _(e):
                emit_engine("sp", e)
        self.nwaits = nwaits[0]
        est.close()

    def simulate(self):
        sem = {}
        pos = {e: 0 for e in self.streams}
        progress = True
        while progress:
            progress = False
            for e, st in self.streams.items():
                while pos[e] < len(st):
                    waits, sig, gidx = st[pos[e]]
                    if all(sem.get(k, 0) >= v for k, v in waits):
                        if sig:
                            sem[sig[0]] = sem.get(sig[0], 0) + sig[1]
                        pos[e] += 1
                        progress = True
                    else:
                        break
        stuck = {e: (pos[e], len(st)) for e, st in self.streams.items() if pos[e] < len(st)}
        if stuck:
            for e in stuck:
                waits, sig, gidx = self.streams[e][pos[e]]
                print("STUCK", e, stuck[e], "gidx", gidx, [(k, v, sem.get(k, 0)) for k, v in waits])
        return not stuck

    def finish(self):
        self.barrier()
        self.op("sp", lambda e: e.nop())
T = 2048
D = 4096
NIN = 33456
EPS = 1e-6
NEG = -1.0e30
OFF = dict(xm=0, om=4096, zm=8192, ig=12288, fg=12296, qa=12304, ckv=16400, za=16912,
           qi=21008, ki=25104, wi=25232, gm=25264, ga=29360)


class Ctx:
    pass


def declare(S, dbg=()):
    C = Ctx()
    ein = lambda n, s: S.dram(n, s, F32, kind="ExternalInput")
    C.x = S.dram("x", [T, D], F32, kind="ExternalInput", nseg=16)
    C.w_in = ein("w_in", [2, D, NIN])
    C.wmisc = ein("wmisc", [2, D, 176])
    C.w_bm = ein("w_bm", [2, D, D])
    C.w_ba = ein("w_ba", [2, D, D])
    C.w_out = ein("w_out", [2, D, D])
    C.w_ukT = ein("w_ukT", [2, 32, 128, 512])
    C.w_uv = ein("w_uv", [2, 32, 512, 128])
    C.wq_bd = ein("wq_bd", [2, 32, 128, 128])
    C.wk_bd = ein("wk_bd", [2, 32, 128, 128])
    C.wv_bd = ein("wv_bd", [2, 32, 128, 128])
    C.pp = ein("pp", [2, 128, 32, 8])
    C.gnB = ein("gnB", [2, 128, D])
    C.gfB = ein("gfB", [128, D])
    C.biB = ein("biB", [2, 128, 16, 8])
    C.bfB = ein("bfB", [2, 128, 16, 8])
    C.gckvB = ein("gckvB", [2, 128, 512])
    C.gkB = ein("gkB", [2, 128, 128])
    C.bkB = ein("bkB", [2, 128, 128])
    C.out = S.dram("out", [T, D], F32, kind="ExternalOutput", nseg=16)

    def scr(n, shape, dt, nseg):
        kind = "ExternalOutput" if n in dbg else "Internal"
        return S.dram(n, shape, dt, kind=kind, nseg=nseg)
    for n in ("XM", "XC", "SOM", "SZM", "QA", "SZA", "QI", "SGM", "SGA", "YM", "YA", "MG"):
        setattr(C, n, scr(n, [D, T], BF16, 32))
    C.CTM = scr("CTM", [T, 512], BF16, 16)
    C.KTM = scr("KTM", [T, 128], BF16, 16)
    C.GT = scr("GT", [T, 16], F32, 16)
    C.WI = scr("WI", [T, 32], F32, 16)
    C.X1 = scr("X1", [T, D], F32, 16)
    C.X2 = scr("X2", [T, D], F32, 16)
    return C


def consts(S, C):
    C.identf = S.sbuf("identf", [128, 128], F32)
    C.ident = S.sbuf("ident", [128, 128], BF16)
    C.utf = S.sbuf("utf", [128, 128], F32)
    C.negtri = S.sbuf("negtri", [128, 128], F32)
    C.onesf = S.sbuf("onesf", [128, 128], F32)
    C.onesb = S.sbuf("onesb", [128, 128], BF16)
    S.memset("pool", C.identf[:, :], 1.0)
    S.op("pool", lambda e: e.affine_select(C.identf.h[:, :], C.identf.h[:, :], pattern=[[-1, 128]],
                                           compare_op=ALU.is_equal, fill=0.0, base=0, channel_multiplier=1),
         reads=[C.identf[:, :]], writes=[C.identf[:, :]])
    S.copy("dve", C.ident[:, :], C.identf[:, :])
    S.memset("pool", C.utf[:, :], 1.0)
    S.op("pool", lambda e: e.affine_select(C.utf.h[:, :], C.utf.h[:, :], pattern=[[1, 128]],
                                           compare_op=ALU.is_ge, fill=0.0, base=0, channel_multiplier=-1),
         reads=[C.utf[:, :]], writes=[C.utf[:, :]])
    S.memset("pool", C.negtri[:, :], 0.0)
    S.op("pool", lambda e: e.affine_select(C.negtri.h[:, :], C.negtri.h[:, :], pattern=[[-1, 128]],
                                           compare_op=ALU.is_ge, fill=NEG, base=0, channel_multiplier=1),
         reads=[C.negtri[:, :]], writes=[C.negtri[:, :]])
    S.memset("pool", C.onesf[:, :], 1.0)
    S.memset("pool", C.onesb[:, :], 1.0)


def phase_norm(S, C, l, xsrc, xnT):
    with ExitStack() as st:
        xt = S.sbuf("xt", [128, D], F32, stack=st)
        xb = S.sbuf("xb", [128, D], BF16, stack=st)
        junk = S.sbuf("junk", [128, D], BF16, stack=st)
        gB = S.sbuf("gB", [128, D], F32, stack=st)
        ss = S.sbuf("ss", [128, 2], F32, stack=st)
        pT = [S.psum("pT", [128, 1024], BF16, stack=st) for _ in range(2)]
        S.dma(gB[:, :], C.gnB.v(C.gnB.h[l]))
        for tt in range(16):
            S.dma(xt[:, :], xsrc.v(xsrc.h[tt * 128:(tt + 1) * 128, :], tt, tt + 1))
            S.memset("dve", ss[:, :], 0.0)
            S.act(junk[:, :], xt[:, :], AF.Square, accum=ss[:, 0:1])
            S.rsqrt(ss[:, 1:2], ss[:, 0:1], 1.0 / D, EPS)
            S.stt("dve", xb[:, :], xt[:, :], ss[:, 1:2], gB[:, :], ALU.mult, ALU.mult)
            for g in range(8):
                p = pT[g % 2]
                for q in range(4):
                    kc = g * 4 + q
                    S.transpose(p[:, q * 128:(q + 1) * 128], xb[:, kc * 128:(kc + 1) * 128], C.ident[:, :])
                dst = xnT.v(xnT.h[:, g * 4:(g + 1) * 4, tt * 128:(tt + 1) * 128], tt, tt + 1)
                src = p.v(p.h[:, 0:512].rearrange("p (a b) -> p a b", a=4))
                S.copy("act", dst, src)
        S.barrier()


FM_SEGS = [("xm", "XM", None), ("om", "SOM", AF.Sigmoid), ("zm", "SZM", AF.Silu), ("qa", "QA", AF.Copy),
           ("za", "SZA", AF.Silu), ("qi", "QI", AF.Copy), ("gm", "SGM", AF.Sigmoid), ("ga", "SGA", AF.Sigmoid)]


def phase_proj(S, C, l, xnT, segs=None):
    with ExitStack() as st:
        wbig = S.sbuf("wbig", [128, 32, 512], BF16, nseg=2, stack=st)
        ps = [S.psum("ps", [128, 2048], F32, nseg=4, stack=st) for _ in range(2)]
        nslab = [0]

        def run_seg(nm, scr, func, ev, xs=None, acc=None, ppt=None):
            evi = [0]

            def next_ev():
                evi[0] += 1
                return ev[evi[0] % len(ev)]
            scrT = getattr(C, scr)
            for sl in range(16):
                c0 = OFF[nm] + sl * 256
                half = nslab[0] % 2
                nslab[0] += 1
                wv = wbig.s(half)[:, :, half * 256:(half + 1) * 256]
                src = C.w_in.h[l][:, c0:c0 + 256].rearrange("(kc p) f -> p kc f", p=128)
                S.dma(wv, C.w_in.v(src), eng="pool")
                for sub in range(2):
                    p = ps[sub]
                    fchunk = sl * 2 + sub
                    for tt in range(4):
                        for kc in range(32):
                            S.mm(p.s(tt)[:, tt * 512:(tt + 1) * 512],
                                 wbig.s(half)[:, kc, half * 256 + sub * 128: half * 256 + (sub + 1) * 128],
                                 xnT.s(tt * 4, tt * 4 + 4)[:, kc, tt * 512:(tt + 1) * 512],
                                 start=(kc == 0), stop=(kc == 31))
                    rows = scrT.v(scrT.h[fchunk * 128:(fchunk + 1) * 128, :], fchunk, fchunk + 1)
                    if nm != "xm":
                        e = next_ev()
                        for tt in range(4):
                            S.act(e[:, tt * 512:(tt + 1) * 512], p.s(tt)[:, tt * 512:(tt + 1) * 512], func)
                        S.dma(rows, e[:, :])
                    else:
                        x_ = xs[sub]
                        a_ = acc[0]
                        for tt in range(4):
                            S.act(x_[:, 4 + tt * 512:4 + (tt + 1) * 512], p.s(tt)[:, tt * 512:(tt + 1) * 512], AF.Copy)
                        e = next_ev()
                        S.copy("pool", e[:, :], x_[:, 4:4 + T])
                        S.dma(rows, e[:, :])
                        P_ = lambda k, fc=fchunk: ppt[:, fc, k:k + 1]
                        S.ts("dve", a_[:, :], x_[:, 4:4 + T], P_(3), ALU.mult, P_(4), ALU.add)
                        S.stt("dve", a_[:, :], x_[:, 3:3 + T], P_(2), a_[:, :], ALU.mult, ALU.add)
                        S.stt("dve", a_[:, :], x_[:, 2:2 + T], P_(1), a_[:, :], ALU.mult, ALU.add)
                        S.stt("dve", a_[:, :], x_[:, 1:1 + T], P_(0), a_[:, :], ALU.mult, ALU.add)
                        e2 = next_ev()
                        S.act(e2[:, :], a_[:, :], AF.Silu)
                        rows2 = C.XC.v(C.XC.h[fchunk * 128:(fchunk + 1) * 128, :], fchunk, fchunk + 1)
                        S.dma(rows2, e2[:, :])

        if segs is None or "xm" in segs:
            with ExitStack() as st2:
                ev = [S.sbuf("ev", [128, 2048], BF16, stack=st2) for _ in range(2)]
                xs = [S.sbuf("xs", [128, 2048 + 4], F32, stack=st2) for _ in range(2)]
                acc = [S.sbuf("acc", [128, 2048], F32, stack=st2) for _ in range(1)]
                ppt = S.sbuf("ppt", [128, 32, 8], F32, stack=st2)
                S.dma(ppt[:, :, :], C.pp.v(C.pp.h[l]))
                for b in xs:
                    S.memset("pool", b[:, 0:4], 0.0)
                run_seg("xm", "XM", None, ev, xs, acc, ppt)
                S.barrier()
        with ExitStack() as st2:
            ev = [S.sbuf("ev", [128, 2048], BF16, stack=st2) for _ in range(3)]
            for (nm, scr, func) in FM_SEGS[1:]:
                if segs is not None and nm not in segs:
                    continue
                run_seg(nm, scr, func, ev)
            S.barrier()
        if segs is None or "tm" in segs:
            with ExitStack() as st2:
                phase_proj_tm(S, C, l, xnT, wbig, ps, st2)
                S.barrier()


def phase_proj_tm(S, C, l, xnT, wbig, ps, st):
    gck = S.sbuf("gck", [128, 512], F32, stack=st)
    gk = S.sbuf("gk", [128, 128], F32, stack=st)
    bk = S.sbuf("bk", [128, 128], F32, stack=st)
    S.dma(gck[:, :], C.gckvB.v(C.gckvB.h[l]))
    S.dma(gk[:, :], C.gkB.v(C.gkB.h[l]))
    S.dma(bk[:, :], C.bkB.v(C.bkB.h[l]))
    st4 = [S.sbuf("st4", [128, 8], F32, stack=st) for _ in range(2)]
    junk = [S.sbuf("junk2", [128, 512], F32, stack=st) for _ in range(2)]
    cb = [S.sbuf("cb", [128, 512], BF16, stack=st) for _ in range(2)]
    kb = [S.sbuf("kb", [128, 128], BF16, stack=st) for _ in range(2)]
    kf = [S.sbuf("kf", [128, 128], F32, stack=st) for _ in range(2)]
    gt = [S.sbuf("gt", [128, 48], F32, stack=st) for _ in range(2)]
    src = C.w_in.h[l][:, OFF["ckv"]:OFF["ckv"] + 512].rearrange("(kc p) f -> p kc f", p=128)
    S.dma(wbig[:, :, :], C.w_in.v(src), eng="pool")
    for tt in range(16):
        p = ps[tt % 2]
        bank = (tt // 2) % 4
        pv = p.s(bank)[:, bank * 512:(bank + 1) * 512]
        for kc in range(32):
            S.mm(pv, xnT.s(tt)[:, kc, tt * 128:(tt + 1) * 128], wbig[:, kc, :], start=(kc == 0), stop=(kc == 31))
        s4 = st4[tt % 2]
        S.memset("dve", s4[:, :], 0.0)
        S.act(junk[tt % 2][:, :], pv, AF.Square, accum=s4[:, 0:1])
        S.rsqrt(s4[:, 1:2], s4[:, 0:1], 1.0 / 512, EPS)
        S.stt("dve", cb[tt % 2][:, :], pv, s4[:, 1:2], gck[:, :], ALU.mult, ALU.mult)
        S.dma(C.CTM.v(C.CTM.h[tt * 128:(tt + 1) * 128, :], tt, tt + 1), cb[tt % 2][:, :])
    srcm = C.wmisc.h[l].rearrange("(kc p) f -> p kc f", p=128)
    S.dma(wbig.v(wbig.h[:, :, 0:176]), C.wmisc.v(srcm), eng="pool")
    for tt in range(16):
        p = ps[tt % 2]
        bank = (tt // 2) % 4
        pv = p.s(bank)[:, bank * 512:bank * 512 + 176]
        for kc in range(32):
            S.mm(pv, xnT.s(tt)[:, kc, tt * 128:(tt + 1) * 128], wbig.v(wbig.h[:, kc, 0:176]),
                 start=(kc == 0), stop=(kc == 31))
        g_ = gt[tt % 2]
        pg = p.s(bank)[:, bank * 512:bank * 512 + 16]
        pw = p.s(bank)[:, bank * 512 + 16:bank * 512 + 48]
        pk = p.s(bank)[:, bank * 512 + 48:bank * 512 + 176]
        S.copy("act", g_[:, 0:16], pg)
        S.act(g_[:, 16:48], pw, AF.Copy, scale=1.0 / 64.0)
        S.dma(C.GT.v(C.GT.h[tt * 128:(tt + 1) * 128, :], tt, tt + 1), g_[:, 0:16])
        S.dma(C.WI.v(C.WI.h[tt * 128:(tt + 1) * 128, :], tt, tt + 1), g_[:, 16:48])
        s4 = st4[tt % 2]
        S.memset("dve", s4[:, :], 0.0)
        S.act(kf[tt % 2][:, :], pk, AF.Copy, accum=s4[:, 0:1])
        S.act(junk[tt % 2][:, 0:128], pk, AF.Square, accum=s4[:, 1:2])
        S.ts("dve", s4[:, 2:3], s4[:, 0:1], 1.0 / 128, ALU.mult)
        S.tt("dve", s4[:, 3:4], s4[:, 2:3], s4[:, 2:3], ALU.mult)
        S.stt("dve", s4[:, 4:5], s4[:, 1:2], 1.0 / 128, s4[:, 3:4], ALU.mult, ALU.subtract)
        S.rsqrt(s4[:, 5:6], s4[:, 4:5], 1.0, EPS)
        S.ts("dve", kf[tt % 2][:, :], kf[tt % 2][:, :], s4[:, 2:3], ALU.subtract, s4[:, 5:6], ALU.mult)
        S.tt("dve", kf[tt % 2][:, :], kf[tt % 2][:, :], gk[:, :], ALU.mult)
        S.tt("dve", kb[tt % 2][:, :], kf[tt % 2][:, :], bk[:, :], ALU.add)
        S.dma(C.KTM.v(C.KTM.h[tt * 128:(tt + 1) * 128, :], tt, tt + 1), kb[tt % 2][:, :])


import math
LNSC = math.log(512.0 ** -0.5)


def phase_mlstm(S, C, l, heads=range(8), stop=99):
    with ExitStack() as st:
        sb = lambda n, shp, dt=F32, nseg=1: S.sbuf(n, shp, dt, nseg=nseg, stack=st)
        G = sb("G", [128, 16, 16])
        biB = sb("biB", [128, 16, 8])
        bfB = sb("bfB", [128, 16, 8])
        S.dma(G[:, :, :], C.GT.v(C.GT.h[:, :].rearrange("(c p) g -> p c g", p=128)))
        S.dma(biB[:, :, :], C.biB.v(C.biB.h[l]))
        S.dma(bfB[:, :, :], C.bfB.v(C.bfB.h[l]))
        ppt = sb("pptm", [128, 32, 8])
        S.dma(ppt[:, :, :], C.pp.v(C.pp.h[l]))
        li = sb("li", [128, 128]); lf = sb("lf", [128, 128]); bsb = sb("bsb", [128, 128]); BL = sb("BL", [128, 128])
        kA = sb("kA", [128, 128]); wS = sb("wS", [128, 128]); qA = sb("qA", [128, 128]); eBL = sb("eBL", [128, 128])
        wSb = sb("wSb", [128, 128], BF16)
        d1 = sb("d1", [128, 128])
        v3 = lambda t: t.v(t.h[:, :].rearrange("p (c h) -> p c h", h=8))
        psA = S.psum("psA", [128, 512], F32, stack=st)
        psB = S.psum("psB", [128, 512], F32, stack=st)
        pS = S.psum("pS", [128, 512], F32, stack=st)
        pN = S.psum("pN", [128, 512], F32, stack=st)
        pDU = S.psum("pDU", [128, 512], F32, nseg=1, stack=st)
        pU = [S.psum("pU", [128, 512], F32, stack=st) for _ in range(2)]
        pT = S.psum("pTm", [128, 1024], BF16, stack=st)
        S.tt("pool", v3(li), G.v(G.h[:, :, 0:8]), biB[:, :, :], ALU.add)
        S.tt("pool", v3(lf), G.v(G.h[:, :, 8:16]), bfB[:, :, :], ALU.add)
        S.act(lf[:, :], lf[:, :], AF.Exp, scale=-1.0)
        S.act(lf[:, :], lf[:, :], AF.Ln, bias=1.0)
        S.ts("dve", lf[:, :], lf[:, :], -1.0, ALU.mult)
        S.mm(psA[:, 0:128], C.utf[:, :], lf[:, :])
        S.mm(psB[:, 0:128], C.onesf[:, :], lf[:, :])
        S.copy("act", bsb[:, :], psA[:, 0:128])
        S.copy("act", BL[:, :], psB[:, 0:128])
        S.tt("dve", d1[:, :], li[:, :], bsb[:, :], ALU.subtract)
        S.act(kA[:, :], d1[:, :], AF.Exp, bias=LNSC)
        S.tt("dve", d1[:, :], d1[:, :], BL[:, :], ALU.add)
        S.act(wS[:, :], d1[:, :], AF.Exp, bias=LNSC)
        S.copy("dve", wSb[:, :], wS[:, :])
        S.act(qA[:, :], bsb[:, :], AF.Exp)
        S.act(eBL[:, :], BL[:, :], AF.Exp)

        if stop <= 1:
            S.barrier()
            return
        xcT = sb("xcT", [128, 4, T], BF16); xmT = sb("xmT", [128, 4, T], BF16)
        qT = sb("qT", [128, 4, T], BF16, nseg=4); kT = sb("kT", [128, 4, T], BF16, nseg=4)
        som = sb("som", [128, 4, T], BF16); szm = sb("szm", [128, 4, T], BF16)
        xcs = sb("xcs", [128, 4, T], BF16); ymT = sb("ymT", [128, 4, T], BF16, nseg=16)
        wq = sb("wq", [128, 4, 128], BF16); wk = sb("wk", [128, 4, 128], BF16); wv = sb("wv", [128, 4, 128], BF16)
        ktm = [sb("ktm", [128, 512], BF16) for _ in range(2)]
        vtm = [sb("vtm", [128, 512], BF16) for _ in range(2)]
        v3t = [sb("v3t", [128, 520], BF16) for _ in range(2)]
        at = [sb("at", [128, 128], BF16) for _ in range(2)]
        hnt = [sb("hnt", [128, 512], BF16) for _ in range(2)]
        yt = [sb("yt", [128, 4, 128]) for _ in range(2)]
        small = [sb("small", [128, 16]) for _ in range(2)]
        junk = sb("junkm", [128, 512], BF16)
        Cst = sb("Cst", [128, 4, 512], F32, nseg=4)
        Cb = sb("Cb", [128, 4, 520], BF16, nseg=5)
        nst = sb("nst", [128, 4])

        for h in heads:
            rows = lambda Tl: Tl.v(Tl.h[h * 512:(h + 1) * 512, :].rearrange("(j p) t -> p j t", p=128), h * 4, h * 4 + 4)
            S.dma(xcT[:, :, :], rows(C.XC))
            S.dma(xmT[:, :, :], rows(C.XM))
            S.dma(som[:, :, :], rows(C.SOM))
            S.dma(szm[:, :, :], rows(C.SZM))
            for (wt, wsrc) in ((wq, C.wq_bd), (wk, C.wk_bd), (wv, C.wv_bd)):
                S.dma(wt[:, :, :], wsrc.v(wsrc.h[l][h * 4:(h + 1) * 4].rearrange("j p o -> p j o")), eng="pool")
            n = 0
            for (dst, wt) in ((qT, wq), (kT, wk)):
                for j in range(4):
                    for tt in range(4):
                        p = psA if n % 2 == 0 else psB
                        n += 1
                        S.mm(p[:, :], wt[:, j, :], xcT[:, j, tt * 512:(tt + 1) * 512])
                        S.copy("act", dst.s(tt)[:, j, tt * 512:(tt + 1) * 512], p[:, :])
            for j in range(4):
                fc = h * 4 + j
                S.ts("pool", som[:, j, :], som[:, j, :], ppt[:, fc, 5:6], ALU.mult)
                S.ts("pool", xcs[:, j, :], xcT[:, j, :], ppt[:, fc, 6:7], ALU.mult)
            for c in range(16 if stop > 2.05 else 0):
                tc = slice(c * 128, (c + 1) * 128)
                col = c * 8 + h
                cs = slice(col, col + 1)
                b2 = c % 2
                tseg = c // 4
                for j in range(4):
                    S.mm(psA[:, j * 128:(j + 1) * 128], xcT[:, j, tc], wk[:, j, :])
                S.copy("act", ktm[b2][:, :], psA[:, :])
                for j in range(4):
                    S.mm(psB[:, j * 128:(j + 1) * 128], xmT[:, j, tc], wv[:, j, :])
                S.copy("act", vtm[b2][:, :], psB[:, :])
                if stop <= 2.1:
                    continue
                S.act(v3t[b2][:, 0:512], psB[:, :], AF.Copy, scale=wS[:, cs])
                if stop <= 2.2:
                    continue
                S.copy("pool", v3t[b2][:, 512:513], wS[:, cs])
                if stop <= 2.3:
                    continue
                for j in range(4):
                    S.mm(pS[:, 0:128], kT.s(tseg)[:, j, tc], qT.s(tseg)[:, j, tc], start=(j == 0), stop=(j == 3))
                if stop <= 2.4:
                    continue
                S.stt("dve", at[b2][:, :], pS[:, 0:128], kA[:, cs], C.utf[:, :], ALU.mult, ALU.mult)
                if stop <= 3:
                    continue
                S.mm(pN[:, :], at[b2][:, :], vtm[b2][:, :], start=True, stop=(c == 0))
                if c > 0:
                    for j in range(4):
                        S.mm(pN[:, :], qT.s(tseg)[:, j, tc], Cb.s(j)[:, j, 0:512], start=False, stop=(j == 3))
                S.mm(pDU[:, 0:1], at[b2][:, :], C.onesb[:, 0:1], start=True, stop=(c == 0))
                if c > 0:
                    for j in range(4):
                        S.mm(pDU[:, 0:1], qT.s(tseg)[:, j, tc], Cb.s(4)[:, j, 512:513], start=False, stop=(j == 3))
                sm = small[b2]
                k_ = lambda i: sm[:, i:i + 1]
                S.ts("dve", k_(0), pDU[:, 0:1], qA[:, cs], ALU.mult)
                S.ts("dve", k_(1), k_(0), -1.0, ALU.mult)
                S.tt("dve", k_(2), k_(0), k_(1), ALU.max)
                S.ts("dve", k_(2), k_(2), 1.0, ALU.max)
                S.op("dve", lambda e, o=k_(3), i=k_(2): e.reciprocal(o.ap, i.ap), reads=[k_(2)], writes=[k_(3)])
                S.tt("dve", k_(4), k_(3), qA[:, cs], ALU.mult)
                S.memset("pool", sm[:, 5:7], 0.0)
                S.act(junk[:, :], pN[:, :], AF.Copy, accum=k_(5))
                S.act(junk[:, :], pN[:, :], AF.Square, accum=k_(6))
                S.ts("dve", k_(7), k_(5), 1.0 / 512, ALU.mult)
                S.tt("dve", k_(8), k_(7), k_(7), ALU.mult)
                S.stt("dve", k_(9), k_(6), 1.0 / 512, k_(8), ALU.mult, ALU.subtract)
                S.tt("dve", k_(10), k_(4), k_(4), ALU.mult)
                S.tt("dve", k_(11), k_(9), k_(10), ALU.mult)
                S.rsqrt(k_(12), k_(11), 1.0, EPS)
                S.tt("dve", k_(13), k_(12), k_(4), ALU.mult)
                S.ts("dve", hnt[b2][:, :], pN[:, :], k_(7), ALU.subtract, k_(13), ALU.mult)
                if stop <= 4:
                    continue
                for j in range(4):
                    S.transpose(pT[:, j * 128:(j + 1) * 128], hnt[b2][:, j * 128:(j + 1) * 128], C.ident[:, :])
                pTv = pT.v(pT.h[:, 0:512].rearrange("p (a b) -> p a b", a=4))
                S.tt("dve", yt[b2][:, :, :], pTv, som[:, :, tc], ALU.mult)
                S.tt("pool", yt[b2][:, :, :], yt[b2][:, :, :], xcs[:, :, tc], ALU.add)
                S.tt("pool", ymT.s(c)[:, :, tc], yt[b2][:, :, :], szm[:, :, tc], ALU.mult)
                if c < 15 and stop > 5:
                    for j in range(4):
                        pu = pU[j % 2]
                        S.mm(pu[:, :], ktm[b2][:, j * 128:(j + 1) * 128], v3t[b2][:, 0:512])
                        S.mm(pDU[:, 8 + j:9 + j], ktm[b2][:, j * 128:(j + 1) * 128], v3t[b2][:, 512:513])
                        if c == 0:
                            S.copy("act", Cst.s(j)[:, j, :], pu[:, :])
                        else:
                            S.stt("dve", Cst.s(j)[:, j, :], Cst.s(j)[:, j, :], eBL[:, cs], pu[:, :], ALU.mult, ALU.add)
                        S.copy("act", Cb.s(j)[:, j, 0:512], Cst.s(j)[:, j, :])
                    if c == 0:
                        S.copy("dve", nst[:, :], pDU[:, 8:12])
                    else:
                        S.stt("dve", nst[:, :], nst[:, :], eBL[:, cs], pDU[:, 8:12], ALU.mult, ALU.add)
                    S.copy("pool", Cb.s(4)[:, :, 512], nst[:, :])
            S.dma(rows(C.YM), ymT[:, :, :])
        S.barrier()


def phase_dsa(S, C, l, qbs=range(16), groups=range(8)):
    with ExitStack() as st:
        sb = lambda n, shp, dt=F32, nseg=1: S.sbuf(n, shp, dt, nseg=nseg, stack=st)
        c_tm = sb("c_tm", [128, 16, 512], BF16, nseg=16)
        cT = sb("cT", [128, 4, T], BF16, nseg=16)
        kidxT = sb("kidxT", [128, T], BF16, nseg=16)
        wuk = sb("wuk", [128, 32, 512], BF16)
        wuv = sb("wuv", [128, 32, 4, 128], BF16)
        wi_all = sb("wi_all", [128, 16, 32])
        S.dma(c_tm[:, :, :], C.CTM.v(C.CTM.h[:, :].rearrange("(b p) c -> p b c", p=128)))
        S.dma(wi_all[:, :, :], C.WI.v(C.WI.h[:, :].rearrange("(b p) h -> p b h", p=128)))
        S.dma(wuk[:, :, :], C.w_ukT.v(C.w_ukT.h[l].rearrange("h d c -> d h c")), eng="pool")
        for hh_ in range(4):
            S.dma(wuv[:, hh_ * 8:(hh_ + 1) * 8, :, :],
                  C.w_uv.v(C.w_uv.h[l][hh_ * 8:(hh_ + 1) * 8].rearrange("h (cc p) d -> p h cc d", p=128)), eng="pool")
        with ExitStack() as st2:
            kidx_tm = S.sbuf("kidx_tm", [128, 16, 128], BF16, stack=st2)
            pX = [S.psum("pX", [128, 1024], BF16, stack=st2) for _ in range(2)]
            S.dma(kidx_tm[:, :, :], C.KTM.v(C.KTM.h[:, :].rearrange("(b p) i -> p b i", p=128)))
            for b in range(16):
                p = pX[b % 2]
                for cc in range(4):
                    S.transpose(p[:, cc * 128:(cc + 1) * 128], c_tm.s(b)[:, b, cc * 128:(cc + 1) * 128], C.ident[:, :])
                S.copy("act", cT.s(b)[:, :, b * 128:(b + 1) * 128], p.v(p.h[:, 0:512].rearrange("p (a b) -> p a b", a=4)))
            for b4 in range(4):
                p = pX[b4 % 2]
                for k in range(4):
                    b = b4 * 4 + k
                    S.transpose(p[:, k * 128:(k + 1) * 128], kidx_tm[:, b, :], C.ident[:, :])
                S.copy("act", kidxT.s(b4 * 4, b4 * 4 + 4)[:, b4 * 512:(b4 + 1) * 512], p[:, 0:512])
            S.barrier()
        pAB = [S.psum("pAB", [128, 512], F32, stack=st) for _ in range(2)]
        pM = S.psum("pM", [128, 512], F32, stack=st)
        pO = [S.psum("pO", [128, 512], F32, stack=st) for _ in range(4)]
        pSum = S.psum("pSum", [128, 512], F32, stack=st)
        qiT = sb("qiT", [128, 32, 128], BF16)
        qaT = sb("qaT", [128, 32, 128], BF16)
        szaT = sb("szaT", [128, 32, 128], BF16)
        yaT = sb("yaT", [128, 32, 128], BF16, nseg=8)
        score = sb("score", [128, T])
        work = sb("work", [128, T])
        maskf = sb("maskf", [128, T])
        maskT = sb("maskT", [128, 16, 128], BF16)
        rt = [sb("rt", [128, 512]) for _ in range(2)]
        m8 = sb("m8", [128, 8])
        qlat = [sb("qlat", [128, 4, 512], BF16) for _ in range(2)]
        et = [sb("et", [128, 512], BF16) for _ in range(2)]
        ptt = [sb("ptt", [128, 512], BF16) for _ in range(2)]
        rs = sb("rs", [128, 512])
        olat = sb("olat", [128, 4, 512], BF16)
        SC = 128.0 ** -0.5
        for qb in qbs:
            tq = slice(qb * 128, (qb + 1) * 128)
            Sc = (qb + 1) * 128
            col = lambda Tl: Tl.v(Tl.h[:, tq].rearrange("(h i) t -> i h t", i=128))
            S.dma(qiT[:, :, :], col(C.QI))
            S.dma(qaT[:, :, :], col(C.QA))
            S.dma(szaT[:, :, :], col(C.SZA))
            n = 0
            for pc in range((Sc + 511) // 512):
                w = min(512, Sc - pc * 512)
                sv = score[:, pc * 512:pc * 512 + w]
                for h in range(32):
                    p = pAB[n % 2]
                    r = rt[n % 2]
                    n += 1
                    S.mm(p[:, 0:w], qiT[:, h, :], kidxT.s(pc * 4, pc * 4 + (w + 127) // 128)[:, pc * 512:pc * 512 + w])
                    S.act(r[:, 0:w], p[:, 0:w], AF.Relu)
                    if h == 0:
                        S.ts("dve", sv, r[:, 0:w], wi_all[:, qb, h:h + 1], ALU.mult)
                    else:
                        S.stt("dve", sv, r[:, 0:w], wi_all[:, qb, h:h + 1], sv, ALU.mult, ALU.add)
            S.tt("dve", score[:, qb * 128:(qb + 1) * 128], score[:, qb * 128:(qb + 1) * 128], C.negtri[:, :], ALU.add)
            if qb >= 2:
                S.copy("pool", work[:, 0:Sc], score[:, 0:Sc])
                for it in range(32):
                    S.op("dve", lambda e, Sc=Sc: e.max(out=m8.h[:, :], in_=work.h[:, 0:Sc]), reads=[work[:, :]], writes=[m8[:, :]])
                    if it < 31:
                        S.op("dve", lambda e, Sc=Sc: e.match_replace(out=work.h[:, 0:Sc], in_to_replace=m8.h[:, :],
                                                                   in_values=work.h[:, 0:Sc], imm_value=NEG),
                             reads=[work[:, :], m8[:, :]], writes=[work[:, :]])
                S.ts("dve", maskf[:, 0:Sc], score[:, 0:Sc], m8[:, 7:8], ALU.is_ge)
            else:
                S.ts("dve", maskf[:, 0:Sc], score[:, 0:Sc], -1.0e29, ALU.is_gt)
            for b4 in range((qb + 4) // 4):
                nb = min(4, qb + 1 - b4 * 4)
                for k in range(nb):
                    b = b4 * 4 + k
                    S.transpose(pM[:, k * 128:(k + 1) * 128], maskf[:, b * 128:(b + 1) * 128], C.identf[:, :])
                S.copy("act", maskT[:, b4 * 4:b4 * 4 + nb, :],
                       pM.v(pM.h[:, 0:nb * 128].rearrange("p (a b) -> p a b", a=nb)))
            for g in groups:
                ql = qlat[g % 2]
                for cc in range(4):
                    for hh in range(4):
                        h = g * 4 + hh
                        S.mm(pM[:, hh * 128:(hh + 1) * 128], wuk[:, h, cc * 128:(cc + 1) * 128], qaT[:, h, :])
                    S.copy("act", ql[:, cc, :], pM[:, :])
                for b in range(qb + 1):
                    pa = pAB[n % 2]
                    e_ = et[n % 2]
                    pt_ = ptt[n % 2]
                    n += 1
                    for cc in range(4):
                        S.mm(pa[:, :], cT.s(b)[:, cc, b * 128:(b + 1) * 128], ql[:, cc, :], start=(cc == 0), stop=(cc == 3))
                    S.act(e_[:, :], pa[:, :], AF.Exp, scale=SC)
                    mb = maskT.v(maskT.h[:, b:b + 1, :].to_broadcast([128, 4, 128]))
                    S.tt("pool", pt_.v(pt_.h[:, :].rearrange("p (a b) -> p a b", a=4)),
                         e_.v(e_.h[:, :].rearrange("p (a b) -> p a b", a=4)), mb, ALU.mult)
                    for cc in range(4):
                        S.mm(pO[cc][:, :], c_tm.s(b)[:, b, cc * 128:(cc + 1) * 128], pt_[:, :], start=(b == 0), stop=(b == qb))
                    S.mm(pSum[:, :], C.onesb[:, :], pt_[:, :], start=(b == 0), stop=(b == qb))
                S.op("dve", lambda e: e.reciprocal(rs.h[:, :], pSum.h[:, :]), reads=[pSum[:, :]], writes=[rs[:, :]])
                for cc in range(4):
                    S.tt("dve", olat[:, cc, :], pO[cc][:, :], rs[:, :], ALU.mult)
                for hh in range(4):
                    h = g * 4 + hh
                    for cc in range(4):
                        S.mm(pM[:, hh * 128:(hh + 1) * 128], wuv[:, h, cc, :], olat[:, cc, hh * 128:(hh + 1) * 128],
                             start=(cc == 0), stop=(cc == 3))
                S.tt("dve", yaT.s(g)[:, g * 4:(g + 1) * 4, :], pM.v(pM.h[:, :].rearrange("p (a b) -> p a b", a=4)),
                     szaT[:, g * 4:(g + 1) * 4, :], ALU.mult)
            S.dma(C.YA.v(C.YA.h[:, tq].rearrange("(h d) t -> d h t", d=128)), yaT[:, :, :])
        S.barrier()


def phase_out(S, C, l, xsrc, xdst):
    with ExitStack() as st:
        sb = lambda n, shp, dt=F32, nseg=1: S.sbuf(n, shp, dt, nseg=nseg, stack=st)
        ymT = sb("ymTo", [128, 32, 1024], BF16)
        yaT = sb("yaTo", [128, 32, 1024], BF16)
        wm = [sb("wm", [128, 32, 128], BF16) for _ in range(2)]
        wa = [sb("wa", [128, 32, 128], BF16) for _ in range(2)]
        sgm = [sb("sgm", [128, 1024], BF16) for _ in range(2)]
        sga = [sb("sga", [128, 1024], BF16) for _ in range(2)]
        t1 = [sb("t1", [128, 1024]) for _ in range(2)]
        t2 = [sb("t2", [128, 1024]) for _ in range(2)]
        mg = [sb("mg", [128, 1024], BF16) for _ in range(2)]
        pm = [S.psum("pm", [128, 1024], F32, nseg=2, stack=st) for _ in range(2)]
        pa = [S.psum("pa", [128, 1024], F32, nseg=2, stack=st) for _ in range(2)]
        for half in range(2):
            th = slice(half * 1024, (half + 1) * 1024)
            S.dma(ymT[:, :, :], C.YM.v(C.YM.h[:, th].rearrange("(kc p) t -> p kc t", p=128)))
            S.dma(yaT[:, :, :], C.YA.v(C.YA.h[:, th].rearrange("(kc p) t -> p kc t", p=128)))
            for fs in range(32):
                i = fs % 2
                fsl = slice(fs * 128, (fs + 1) * 128)
                S.dma(wm[i][:, :, :], C.w_bm.v(C.w_bm.h[l][:, fsl].rearrange("(kc p) f -> p kc f", p=128)), eng="pool")
                S.dma(wa[i][:, :, :], C.w_ba.v(C.w_ba.h[l][:, fsl].rearrange("(kc p) f -> p kc f", p=128)), eng="pool")
                S.dma(sgm[i][:, :], C.SGM.v(C.SGM.h[fsl, th], fs, fs + 1))
                S.dma(sga[i][:, :], C.SGA.v(C.SGA.h[fsl, th], fs, fs + 1))
                for (pp_, w_, y_) in ((pm[i], wm[i], ymT), (pa[i], wa[i], yaT)):
                    for tt in range(2):
                        for kc in range(32):
                            S.mm(pp_.s(tt)[:, tt * 512:(tt + 1) * 512], w_[:, kc, :], y_[:, kc, tt * 512:(tt + 1) * 512],
                                 start=(kc == 0), stop=(kc == 31))
                S.tt("dve", t1[i][:, :], pm[i][:, :], sgm[i][:, :], ALU.mult)
                S.tt("dve", t2[i][:, :], pa[i][:, :], sga[i][:, :], ALU.mult)
                S.tt("pool", mg[i][:, :], t1[i][:, :], t2[i][:, :], ALU.add)
                S.dma(C.MG.v(C.MG.h[fsl, th], fs, fs + 1), mg[i][:, :])
        S.barrier()
    with ExitStack() as st:
        sb = lambda n, shp, dt=F32, nseg=1: S.sbuf(n, shp, dt, nseg=nseg, stack=st)
        mgT = sb("mgT", [128, 32, 1024], BF16)
        wo = [sb("wo", [128, 32, 512], BF16) for _ in range(2)]
        xo = [sb("xo", [128, 512]) for _ in range(3)]
        po = [S.psum("po", [128, 512], F32, stack=st) for _ in range(4)]
        n = 0
        for half in range(2):
            th = slice(half * 1024, (half + 1) * 1024)
            S.dma(mgT[:, :, :], C.MG.v(C.MG.h[:, th].rearrange("(kc p) t -> p kc t", p=128)))
            for fs in range(8):
                fsl = slice(fs * 512, (fs + 1) * 512)
                w_ = wo[fs % 2]
                S.dma(w_[:, :, :], C.w_out.v(C.w_out.h[l][:, fsl].rearrange("(kc p) f -> p kc f", p=128)), eng="pool")
                for tt in range(8):
                    tok = half * 1024 + tt * 128
                    seg = tok // 128
                    x_ = xo[n % 3]
                    p_ = po[n % 4]
                    n += 1
                    S.dma(x_[:, :], xsrc.v(xsrc.h[tok:tok + 128, fsl], seg, seg + 1))
                    for kc in range(32):
                        S.mm(p_[:, :], mgT[:, kc, tt * 128:(tt + 1) * 128], w_[:, kc, :], start=(kc == 0), stop=(kc == 31))
                    S.tt("dve", x_[:, :], p_[:, :], x_[:, :], ALU.add)
                    S.dma(xdst.v(xdst.h[tok:tok + 128, fsl], seg, seg + 1), x_[:, :])
        S.barrier()


def phase_final(S, C, xsrc):
    with ExitStack() as st:
        xt = [S.sbuf("xtf", [128, D], F32, stack=st) for _ in range(2)]
        junk = S.sbuf("junkf", [128, D], BF16, stack=st)
        gB = S.sbuf("gBf", [128, D], F32, stack=st)
        ss = [S.sbuf("ssf", [128, 16], F32, stack=st) for _ in range(2)]
        S.dma(gB[:, :], C.gfB[:, :])
        for tt in range(16):
            x_ = xt[tt % 2]
            s_ = ss[tt % 2]
            S.dma(x_[:, :], xsrc.v(xsrc.h[tt * 128:(tt + 1) * 128, :], tt, tt + 1))
            S.memset("dve", s_[:, :], 0.0)
            S.act(junk[:, :], x_[:, :], AF.Square, accum=s_[:, 0:1])
            S.rsqrt(s_[:, 1:2], s_[:, 0:1], 1.0 / D, EPS)
            S.stt("dve", x_[:, :], x_[:, :], s_[:, 1:2], gB[:, :], ALU.mult, ALU.mult)
            S.dma(C.out.v(C.out.h[tt * 128:(tt + 1) * 128, :], tt, tt + 1), x_[:, :])
        S.barrier()
def _bd(w):
    out = np.zeros((2, 32, 128, 128), np.float32)
    wr = w.reshape(2, 32, 32, 4, 4)
    for g in range(32):
        out[:, :, 4 * g:4 * g + 4, 4 * g:4 * g + 4] = wr[:, :, g]
    return out


def prep_shared(inp):
    f = lambda a: np.ascontiguousarray(np.asarray(a, dtype=np.float32))
    sh = {}
    w_in = f(inp["w_in"])
    sh["w_in"] = w_in
    sh["wmisc"] = f(np.concatenate([w_in[:, :, 12288:12304], w_in[:, :, 25232:25264], w_in[:, :, 25104:25232]], axis=2))
    sh["w_bm"] = f(inp["w_bm"]); sh["w_ba"] = f(inp["w_ba"]); sh["w_out"] = f(inp["w_out"])
    sh["w_ukT"] = f(np.transpose(np.asarray(inp["w_uk"]), (0, 1, 3, 2)))
    sh["w_uv"] = f(inp["w_uv"])
    sh["wq_bd"] = _bd(np.asarray(inp["w_q_m"], np.float32))
    sh["wk_bd"] = _bd(np.asarray(inp["w_k_m"], np.float32))
    sh["wv_bd"] = _bd(np.asarray(inp["w_v_m"], np.float32))
    pp = np.zeros((2, 128, 32, 8), np.float32)
    cw = np.asarray(inp["conv_w"], np.float32)
    for k in range(4):
        pp[:, :, :, k] = cw[:, k].reshape(2, 32, 128).transpose(0, 2, 1)
    pp[:, :, :, 4] = np.asarray(inp["conv_b"], np.float32).reshape(2, 32, 128).transpose(0, 2, 1)
    pp[:, :, :, 5] = np.asarray(inp["g_head_m"], np.float32).reshape(2, 32, 128).transpose(0, 2, 1)
    pp[:, :, :, 6] = np.asarray(inp["skip_m"], np.float32).reshape(2, 32, 128).transpose(0, 2, 1)
    sh["pp"] = pp
    bc = lambda v, shape: f(np.broadcast_to(np.asarray(v, np.float32), shape))
    sh["gnB"] = bc(np.asarray(inp["g_norm"])[:, None, :], (2, 128, 4096))
    sh["gfB"] = bc(np.asarray(inp["g_final"])[None, :], (128, 4096))
    sh["biB"] = bc(np.asarray(inp["b_i"])[:, None, None, :], (2, 128, 16, 8))
    sh["bfB"] = bc(np.asarray(inp["b_f"])[:, None, None, :], (2, 128, 16, 8))
    sh["gckvB"] = bc(np.asarray(inp["g_ckv"])[:, None, :], (2, 128, 512))
    sh["gkB"] = bc(np.asarray(inp["g_kidx"])[:, None, :], (2, 128, 128))
    sh["bkB"] = bc(np.asarray(inp["b_kidx"])[:, None, :], (2, 128, 128))
    return sh


_CACHE = {}


def build_program():
    nc = bass.Bass("TRN2", target_bir_lowering=False)
    S = Sched(nc)
    C = declare(S)
    consts(S, C)
    xs = [C.x, C.X1, C.X2]
    for l in range(2):
        with ExitStack() as st:
            xnT = S.sbuf("xnT", [128, 32, 2048], BF16, nseg=16, stack=st)
            phase_norm(S, C, l, xs[l], xnT)
            phase_proj(S, C, l, xnT)
        phase_mlstm(S, C, l)
        phase_dsa(S, C, l)
        phase_out(S, C, l, xs[l], xs[l + 1])
    phase_final(S, C, xs[2])
    S.finish()
    S.emit()
    return nc


def kernel(**inputs):
    n = 8
    if "nc" not in _CACHE:
        _CACHE["nc"] = build_program()
    nc = _CACHE["nc"]
    shared = prep_shared(inputs)
    x = np.asarray(inputs["x"], dtype=np.float32)
    in_maps = []
    for c in range(n):
        m = dict(shared)
        m["x"] = np.ascontiguousarray(x[c])
        in_maps.append(m)
    res = run_bass_kernel_spmd(nc, in_maps, core_ids=list(range(n)))
    out = np.stack([np.asarray(res.results[c]["out"], dtype=np.float32) for c in range(n)], axis=0)
    return out
```

```python
import numpy as np
import numpy as np
import concourse.bass as bass
import concourse.mybir as mybir
from concourse.bass_utils import run_bass_kernel_spmd
from contextlib import ExitStack

F32 = mybir.dt.float32
BF16 = mybir.dt.bfloat16
AF = mybir.ActivationFunctionType
ALU = mybir.AluOpType
AX = mybir.AxisListType

SEM_CAP = 30000
DMA_RING = 8
COMPUTE = ("pe", "act", "dve", "pool")


class View:
    __slots__ = ("ap", "buf", "lo", "hi")

    def __init__(self, ap, buf, lo, hi):
        self.ap, self.buf, self.lo, self.hi = ap, buf, lo, hi


class _Seg:
    def __init__(self, t, lo, hi):
        self.t, self.lo, self.hi = t, lo, hi

    def __getitem__(self, idx):
        return View(self.t.h[idx], self.t, self.lo, self.hi)


class Tile:
    def __init__(self, handle, nseg=1, name=""):
        self.h = handle
        self.nseg = nseg
        self.name = name
        self.lw = [None] * nseg
        self.rd = [dict() for _ in range(nseg)]
        self.psum = False
        self.prd = {}

    def __getitem__(self, idx):
        return View(self.h[idx], self, 0, self.nseg)

    def s(self, lo, hi=None):
        if hi is None:
            hi = lo + 1
        assert 0 <= lo < hi <= self.nseg, (self.name, lo, hi, self.nseg)
        return _Seg(self, lo, hi)

    def v(self, ap, lo=0, hi=None):
        return View(ap, self, lo, self.nseg if hi is None else hi)


class Op:
    __slots__ = ("eng", "fn", "deps", "dma", "signal", "sem", "val", "gidx", "qidx")

    def __init__(self, eng, fn, dma):
        self.eng, self.fn, self.dma = eng, fn, dma
        self.deps = set()
        self.signal = False
        self.sem = None
        self.val = 0
        self.gidx = 0
        self.qidx = 0


class Sched:
    def __init__(self, nc):
        self.nc = nc
        self.ops = []
        self.stack = ExitStack()
        self.pending_barrier = {}
        self.last_op = {}
        self.all_dma = []
        self.ndma = {}
        self._n = 0

    def sbuf(self, name, shape, dtype, nseg=1, stack=None):
        st = stack if stack is not None else self.stack
        self._n += 1
        h = st.enter_context(self.nc.sbuf_tensor(f"{name}_{self._n}", list(shape), dtype))
        fb = int(np.prod(shape[1:])) * (4 if dtype == F32 else 2)
        if fb % 64 != 0:
            st.enter_context(self.nc.sbuf_tensor(f"pad_{self._n}", [128, (64 - fb % 64) // 2], BF16))
        return Tile(h, nseg, name)

    def psum(self, name, shape, dtype, nseg=1, stack=None):
        st = stack if stack is not None else self.stack
        self._n += 1
        fb = int(np.prod(shape[1:])) * (4 if dtype == F32 else 2)
        assert fb % 2048 == 0, ("psum tiles must be whole banks", name, shape)
        h = st.enter_context(self.nc.psum_tensor(f"{name}_{self._n}", list(shape), dtype))
        t = Tile(h, nseg, name)
        t.psum = True
        t.nbank = fb // 2048
        return t

    def dram(self, name, shape, dtype, kind="Internal", nseg=1):
        h = self.nc.dram_tensor(name, list(shape), dtype, kind=kind)
        return Tile(h, nseg, name)

    def op(self, eng, fn, reads=(), writes=(), dma=False):
        o = Op(eng, fn, dma)
        o.gidx = len(self.ops)
        pb = self.pending_barrier.pop(eng, None)
        if pb:
            o.deps |= pb
        for v in reads:
            b = v.buf
            for sg in range(v.lo, v.hi):
                w = b.lw[sg]
                if w is not None:
                    o.deps.add(w)
            if b.psum:
                banks = range(b.nbank) if b.nseg != b.nbank else range(v.lo, v.hi)
                for bk in banks:
                    d = b.prd.setdefault(bk, {})
                    for e2, r in d.items():
                        if e2 != eng:
                            o.deps.add(r)
                    d[eng] = o
        for v in writes:
            b = v.buf
            for sg in range(v.lo, v.hi):
                w = b.lw[sg]
                if w is not None:
                    o.deps.add(w)
                for r in b.rd[sg].values():
                    o.deps.add(r)
        for v in reads:
            b = v.buf
            for sg in range(v.lo, v.hi):
                key = ("dma", o.gidx) if dma else eng
                b.rd[sg][key] = o
        for v in writes:
            b = v.buf
            for sg in range(v.lo, v.hi):
                b.lw[sg] = o
                b.rd[sg] = {}
        o.deps.discard(o)
        self.ops.append(o)
        self.last_op[eng] = o
        if dma:
            self.all_dma.append(o)
        return o

    def barrier(self):
        deps = set(self.last_op.values()) | set(self.all_dma)
        self.all_dma = []
        for e in ("pe", "act", "dve", "pool", "sp"):
            self.pending_barrier[e] = set(deps) | self.pending_barrier.get(e, set())

    def mm(self, out, lhsT, rhs, start=True, stop=True):
        return self.op("pe", lambda e: e.matmul(out.ap, lhsT.ap, rhs.ap, start=start, stop=stop),
                       reads=[lhsT, rhs], writes=[out])

    def transpose(self, out, in_, ident):
        return self.op("pe", lambda e: e.transpose(out.ap, in_.ap, ident.ap),
                       reads=[in_, ident], writes=[out])

    def act(self, out, in_, func, bias=None, scale=None, accum=None, eng="act"):
        reads = [in_]
        kw = {}
        if bias is not None:
            if isinstance(bias, View):
                reads.append(bias)
                kw["bias"] = bias.ap
            else:
                kw["bias"] = bias
        if scale is not None:
            if isinstance(scale, View):
                reads.append(scale)
                kw["scale"] = scale.ap
            else:
                kw["scale"] = scale
        writes = [out]
        if accum is not None:
            writes.append(accum)
            kw["accum_out"] = accum.ap
        return self.op("act", lambda e: e.activation(out.ap, in_.ap, func, **kw), reads=reads, writes=writes)

    def tt(self, eng, out, in0, in1, op):
        return self.op(eng, lambda e: e.tensor_tensor(out.ap, in0.ap, in1.ap, op), reads=[in0, in1], writes=[out])

    def ts(self, eng, out, in0, s1, op0, s2=None, op1=None, accum=None):
        reads = [in0]
        a1 = s1.ap if isinstance(s1, View) else s1
        a2 = s2.ap if isinstance(s2, View) else s2
        if isinstance(s1, View):
            reads.append(s1)
        if isinstance(s2, View):
            reads.append(s2)
        writes = [out]
        kw = {}
        if op1 is not None:
            kw["op1"] = op1
        if accum is not None:
            kw["accum_out"] = accum.ap
            writes.append(accum)
        return self.op(eng, lambda e: e.tensor_scalar(out.ap, in0.ap, a1, a2, op0, **kw), reads=reads, writes=writes)

    def stt(self, eng, out, in0, scalar, in1, op0, op1, accum=None):
        reads = [in0, in1]
        a = scalar.ap if isinstance(scalar, View) else scalar
        if isinstance(scalar, View):
            reads.append(scalar)
        writes = [out]
        kw = {}
        if accum is not None:
            kw["accum_out"] = accum.ap
            writes.append(accum)
        return self.op(eng, lambda e: e.scalar_tensor_tensor(out.ap, in0.ap, a, in1.ap, op0, op1, **kw),
                       reads=reads, writes=writes)

    def rsqrt(self, out, in_, scale, eps):
        self.act(out, in_, AF.Sqrt, bias=eps, scale=scale)
        return self.op("dve", lambda e: e.reciprocal(out.ap, out.ap), reads=[out], writes=[out])

    def copy(self, eng, out, in_):
        if eng == "act":
            return self.op("act", lambda e: e.copy(out.ap, in_.ap), reads=[in_], writes=[out])
        return self.op(eng, lambda e: e.tensor_copy(out.ap, in_.ap), reads=[in_], writes=[out])

    def memset(self, eng, out, val):
        return self.op(eng, lambda e: e.memset(out.ap, val), writes=[out])

    def dma(self, out, in_, eng="sp"):
        return self.op(eng, lambda e: e.dma_start(out=out.ap, in_=in_.ap), reads=[in_], writes=[out], dma=True)

    def emit(self):
        nc = self.nc
        ops = self.ops
        for o in ops:
            for d in o.deps:
                if d.eng == "pe" and o.eng == "pe" and not d.dma and not o.dma:
                    continue
                d.signal = True
        est = ExitStack()
        sems = {}

        def new_sem(tag):
            sems[tag] = est.enter_context(nc.semaphore(tag))
            return sems[tag]

        cnt = {e: 0 for e in COMPUTE + ("sp",)}
        cur = {}
        dcount = {}
        dring = {}
        for o in ops:
            if o.dma:
                q = o.eng
                i = dcount.get(q, 0)
                dcount[q] = i + 1
                slot = i % DMA_RING
                if (q, slot) not in dring:
                    dring[(q, slot)] = [new_sem(f"d_{q}_{slot}"), 0, None]
                ent = dring[(q, slot)]
                prev = ent[2]
                if prev is not None:
                    o.deps.add(prev)
                    prev.signal = True
                ent[1] += 16
                ent[2] = o
                o.sem, o.val = ent[0], ent[1]
                o.signal = True
            elif o.signal:
                e = o.eng
                c = cnt[e]
                if c % SEM_CAP == 0:
                    cur[e] = new_sem(f"s_{e}_{c // SEM_CAP}")
                cnt[e] = c + 1
                o.sem, o.val = cur[e], (c % SEM_CAP) + 1
                o.qidx = c + 1
        by_eng = {e: [] for e in ("pe", "act", "dve", "pool", "sp")}
        for o in ops:
            by_eng[o.eng].append(o)
        nwaits = [0]

        self.streams = {}

        def emit_engine(ename, eobj):
            stream = self.streams.setdefault(ename, [])
            waited_c = {e: 0 for e in COMPUTE + ("sp",)}
            waited_d = {}
            for o in by_eng[ename]:
                need_c = {}
                need_d = {}
                for d in o.deps:
                    if d.dma:
                        k = id(d.sem)
                        if d.val > waited_d.get(k, 0):
                            if k not in need_d or need_d[k][1] < d.val:
                                need_d[k] = (d.sem, d.val)
                    else:
                        if d.eng == "pe" and ename == "pe" and not o.dma:
                            continue
                        if not d.signal:
                            continue
                        if d.qidx > waited_c[d.eng]:
                            if d.eng not in need_c or need_c[d.eng].qidx < d.qidx:
                                need_c[d.eng] = d
                for e, d in need_c.items():
                    eobj.wait_ge(d.sem, d.val)
                    waited_c[e] = d.qidx
                    nwaits[0] += 1
                for k, (sem, val) in need_d.items():
                    eobj.wait_ge(sem, val)
                    waited_d[k] = val
                    nwaits[0] += 1
                ins = o.fn(eobj)
                if o.signal:
                    ins.then_inc(o.sem, 16 if o.dma else 1)
                stream.append(([(id(d.sem), d.val) for d in need_c.values()] + [(k, v) for k, (sm_, v) in need_d.items()],
                               (id(o.sem), 16 if o.dma else 1) if o.signal else None, o.gidx))

        with nc.Block() as block:
            @block.tensor
            def _(e):
                emit_engine("pe", e)

            @block.scalar
            def _(e):
                emit_engine("act", e)

            @block.vector
            def _(e):
                emit_engine("dve", e)

            @block.gpsimd
            def _(e):
                emit_engine("pool", e)

            @block.sync
            def _(e):
                emit_engine("sp", e)
        self.nwaits = nwaits[0]
        est.close()

    def simulate(self):
        sem = {}
        pos = {e: 0 for e in self.streams}
        progress = True
        while progress:
            progress = False
            for e, st in self.streams.items():
                while pos[e] < len(st):
                    waits, sig, gidx = st[pos[e]]
                    if all(sem.get(k, 0) >= v for k, v in waits):
                        if sig:
                            sem[sig[0]] = sem.get(sig[0], 0) + sig[1]
                        pos[e] += 1
                        progress = True
                    else:
                        break
        stuck = {e: (pos[e], len(st)) for e, st in self.streams.items() if pos[e] < len(st)}
        if stuck:
            for e in stuck:
                waits, sig, gidx = self.streams[e][pos[e]]
                print("STUCK", e, stuck[e], "gidx", gidx, [(k, v, sem.get(k, 0)) for k, v in waits])
        return not stuck

    def finish(self):
        self.barrier()
        self.op("sp", lambda e: e.nop())
T = 2048
D = 4096
NIN = 33456
EPS = 1e-6
NEG = -1.0e30
OFF = dict(xm=0, om=4096, zm=8192, ig=12288, fg=12296, qa=12304, ckv=16400, za=16912,
           qi=21008, ki=25104, wi=25232, gm=25264, ga=29360)


class Ctx:
    pass


def declare(S, dbg=()):
    C = Ctx()
    ein = lambda n, s: S.dram(n, s, F32, kind="ExternalInput")
    C.x = S.dram("x", [T, D], F32, kind="ExternalInput", nseg=16)
    C.w_in = ein("w_in", [2, D, NIN])
    C.wmisc = ein("wmisc", [2, D, 176])
    C.w_bm = ein("w_bm", [2, 32, 128, 32, 128])
    C.w_ba = ein("w_ba", [2, 32, 128, 32, 128])
    C.w_out = ein("w_out", [2, 8, 128, 32, 512])
    C.w_ukT = ein("w_ukT", [2, 32, 128, 512])
    C.w_uv = ein("w_uv", [2, 32, 512, 128])
    C.wq_bd = ein("wq_bd", [2, 32, 128, 128])
    C.wk_bd = ein("wk_bd", [2, 32, 128, 128])
    C.wv_bd = ein("wv_bd", [2, 32, 128, 128])
    C.pp = ein("pp", [2, 128, 32, 8])
    C.gnB = ein("gnB", [2, 128, D])
    C.gfB = ein("gfB", [128, D])
    C.biB = ein("biB", [2, 128, 16, 8])
    C.bfB = ein("bfB", [2, 128, 16, 8])
    C.gckvB = ein("gckvB", [2, 128, 512])
    C.gkB = ein("gkB", [2, 128, 128])
    C.bkB = ein("bkB", [2, 128, 128])
    C.out = S.dram("out", [T, D], F32, kind="ExternalOutput", nseg=16)

    def scr(n, shape, dt, nseg):
        kind = "ExternalOutput" if n in dbg else "Internal"
        return S.dram(n, shape, dt, kind=kind, nseg=nseg)
    for n in ("XM", "XC", "SOM", "SZM", "QA", "SZA", "QI", "SGM", "SGA", "YM", "YA", "MG"):
        setattr(C, n, scr(n, [D, T], BF16, 32))
    C.CTM = scr("CTM", [T, 512], BF16, 16)
    C.KTM = scr("KTM", [T, 128], BF16, 16)
    C.GT = scr("GT", [T, 16], F32, 16)
    C.WI = scr("WI", [T, 32], F32, 16)
    C.X1 = scr("X1", [T, D], F32, 16)
    C.X2 = scr("X2", [T, D], F32, 16)
    return C


def consts(S, C):
    C.identf = S.sbuf("identf", [128, 128], F32)
    C.ident = S.sbuf("ident", [128, 128], BF16)
    C.utf = S.sbuf("utf", [128, 128], F32)
    C.negtri = S.sbuf("negtri", [128, 128], F32)
    C.onesf = S.sbuf("onesf", [128, 128], F32)
    C.onesb = S.sbuf("onesb", [128, 128], BF16)
    S.memset("pool", C.identf[:, :], 1.0)
    S.op("pool", lambda e: e.affine_select(C.identf.h[:, :], C.identf.h[:, :], pattern=[[-1, 128]],
                                           compare_op=ALU.is_equal, fill=0.0, base=0, channel_multiplier=1),
         reads=[C.identf[:, :]], writes=[C.identf[:, :]])
    S.copy("dve", C.ident[:, :], C.identf[:, :])
    S.memset("pool", C.utf[:, :], 1.0)
    S.op("pool", lambda e: e.affine_select(C.utf.h[:, :], C.utf.h[:, :], pattern=[[1, 128]],
                                           compare_op=ALU.is_ge, fill=0.0, base=0, channel_multiplier=-1),
         reads=[C.utf[:, :]], writes=[C.utf[:, :]])
    S.memset("pool", C.negtri[:, :], 0.0)
    S.op("pool", lambda e: e.affine_select(C.negtri.h[:, :], C.negtri.h[:, :], pattern=[[-1, 128]],
                                           compare_op=ALU.is_ge, fill=NEG, base=0, channel_multiplier=1),
         reads=[C.negtri[:, :]], writes=[C.negtri[:, :]])
    S.memset("pool", C.onesf[:, :], 1.0)
    S.memset("pool", C.onesb[:, :], 1.0)


def phase_norm(S, C, l, xsrc, xnT):
    with ExitStack() as st:
        xt = [S.sbuf("xt", [128, D], F32, stack=st) for _ in range(2)]
        xb = [S.sbuf("xb", [128, D], BF16, stack=st) for _ in range(2)]
        gB = S.sbuf("gB", [128, D], F32, stack=st)
        ss = [S.sbuf("ss", [128, 16], F32, stack=st) for _ in range(2)]
        pT = [S.psum("pT", [128, 1024], BF16, stack=st) for _ in range(2)]
        S.dma(gB[:, :], C.gnB.v(C.gnB.h[l]))
        for tt in range(16):
            x_ = xt[tt % 2]
            b_ = xb[tt % 2]
            s_ = ss[tt % 2]
            S.dma(x_[:, :], xsrc.v(xsrc.h[tt * 128:(tt + 1) * 128, :], tt, tt + 1))
            S.memset("dve", s_[:, :], 0.0)
            S.act(b_[:, :], x_[:, :], AF.Square, accum=s_[:, 0:1])
            S.rsqrt(s_[:, 1:2], s_[:, 0:1], 1.0 / D, EPS)
            S.stt("dve", b_[:, :], x_[:, :], s_[:, 1:2], gB[:, :], ALU.mult, ALU.mult)
            for g in range(8):
                p = pT[g % 2]
                for q in range(4):
                    kc = g * 4 + q
                    S.transpose(p[:, q * 128:(q + 1) * 128], b_[:, kc * 128:(kc + 1) * 128], C.ident[:, :])
                dst = xnT.v(xnT.h[:, g * 4:(g + 1) * 4, tt * 128:(tt + 1) * 128], tt, tt + 1)
                src = p.v(p.h[:, 0:512].rearrange("p (a b) -> p a b", a=4))
                S.copy("act", dst, src)
        S.barrier()


FM_SEGS = [("xm", "XM", None), ("om", "SOM", AF.Sigmoid), ("zm", "SZM", AF.Silu), ("qa", "QA", AF.Copy),
           ("za", "SZA", AF.Silu), ("qi", "QI", AF.Copy), ("gm", "SGM", AF.Sigmoid), ("ga", "SGA", AF.Sigmoid)]


def phase_proj(S, C, l, xnT, segs=None):
    with ExitStack() as st:
        wbig = S.sbuf("wbig", [128, 32, 512], BF16, nseg=2, stack=st)
        ps = [S.psum("ps", [128, 2048], F32, nseg=4, stack=st) for _ in range(2)]
        nslab = [0]

        def run_seg(nm, scr, func, ev, xs=None, acc=None, ppt=None):
            evi = [0]

            def next_ev():
                evi[0] += 1
                return ev[evi[0] % len(ev)]
            scrT = getattr(C, scr)
            for sl in range(16):
                c0 = OFF[nm] + sl * 256
                half = nslab[0] % 2
                nslab[0] += 1
                wv = wbig.s(half)[:, :, half * 256:(half + 1) * 256]
                src = C.w_in.h[l][:, c0:c0 + 256].rearrange("(kc p) f -> p kc f", p=128)
                S.dma(wv, C.w_in.v(src), eng="pool")
                for sub in range(2):
                    p = ps[sub]
                    fchunk = sl * 2 + sub
                    for tt in range(4):
                        for kc in range(32):
                            S.mm(p.s(tt)[:, tt * 512:(tt + 1) * 512],
                                 wbig.s(half)[:, kc, half * 256 + sub * 128: half * 256 + (sub + 1) * 128],
                                 xnT.s(tt * 4, tt * 4 + 4)[:, kc, tt * 512:(tt + 1) * 512],
                                 start=(kc == 0), stop=(kc == 31))
                    rows = scrT.v(scrT.h[fchunk * 128:(fchunk + 1) * 128, :], fchunk, fchunk + 1)
                    if nm != "xm":
                        e = next_ev()
                        for tt in range(4):
                            S.act(e[:, tt * 512:(tt + 1) * 512], p.s(tt)[:, tt * 512:(tt + 1) * 512], func)
                        S.dma(rows, e[:, :])
                    else:
                        x_ = xs[sub]
                        a_ = acc[0]
                        for tt in range(4):
                            S.act(x_[:, 4 + tt * 512:4 + (tt + 1) * 512], p.s(tt)[:, tt * 512:(tt + 1) * 512], AF.Copy)
                        e = next_ev()
                        S.copy("pool", e[:, :], x_[:, 4:4 + T])
                        S.dma(rows, e[:, :])
                        P_ = lambda k, fc=fchunk: ppt[:, fc, k:k + 1]
                        S.ts("dve", a_[:, :], x_[:, 4:4 + T], P_(3), ALU.mult, P_(4), ALU.add)
                        S.stt("dve", a_[:, :], x_[:, 3:3 + T], P_(2), a_[:, :], ALU.mult, ALU.add)
                        S.stt("dve", a_[:, :], x_[:, 2:2 + T], P_(1), a_[:, :], ALU.mult, ALU.add)
                        S.stt("dve", a_[:, :], x_[:, 1:1 + T], P_(0), a_[:, :], ALU.mult, ALU.add)
                        e2 = next_ev()
                        S.act(e2[:, :], a_[:, :], AF.Silu)
                        rows2 = C.XC.v(C.XC.h[fchunk * 128:(fchunk + 1) * 128, :], fchunk, fchunk + 1)
                        S.dma(rows2, e2[:, :])

        if segs is None or "xm" in segs:
            with ExitStack() as st2:
                ev = [S.sbuf("ev", [128, 2048], BF16, stack=st2) for _ in range(2)]
                xs = [S.sbuf("xs", [128, 2048 + 4], F32, stack=st2) for _ in range(2)]
                acc = [S.sbuf("acc", [128, 2048], F32, stack=st2) for _ in range(1)]
                ppt = S.sbuf("ppt", [128, 32, 8], F32, stack=st2)
                S.dma(ppt[:, :, :], C.pp.v(C.pp.h[l]))
                for b in xs:
                    S.memset("pool", b[:, 0:4], 0.0)
                run_seg("xm", "XM", None, ev, xs, acc, ppt)
                S.barrier()
        with ExitStack() as st2:
            ev = [S.sbuf("ev", [128, 2048], BF16, stack=st2) for _ in range(3)]
            for (nm, scr, func) in FM_SEGS[1:]:
                if segs is not None and nm not in segs:
                    continue
                run_seg(nm, scr, func, ev)
            S.barrier()
        if segs is None or "tm" in segs:
            with ExitStack() as st2:
                phase_proj_tm(S, C, l, xnT, wbig, ps, st2)
                S.barrier()


def phase_proj_tm(S, C, l, xnT, wbig, ps, st):
    gck = S.sbuf("gck", [128, 512], F32, stack=st)
    gk = S.sbuf("gk", [128, 128], F32, stack=st)
    bk = S.sbuf("bk", [128, 128], F32, stack=st)
    S.dma(gck[:, :], C.gckvB.v(C.gckvB.h[l]))
    S.dma(gk[:, :], C.gkB.v(C.gkB.h[l]))
    S.dma(bk[:, :], C.bkB.v(C.bkB.h[l]))
    st4 = [S.sbuf("st4", [128, 8], F32, stack=st) for _ in range(2)]
    junk = [S.sbuf("junk2", [128, 512], F32, stack=st) for _ in range(2)]
    cb = [S.sbuf("cb", [128, 512], BF16, stack=st) for _ in range(2)]
    kb = [S.sbuf("kb", [128, 128], BF16, stack=st) for _ in range(2)]
    kf = [S.sbuf("kf", [128, 128], F32, stack=st) for _ in range(2)]
    gt = [S.sbuf("gt", [128, 48], F32, stack=st) for _ in range(2)]
    src = C.w_in.h[l][:, OFF["ckv"]:OFF["ckv"] + 512].rearrange("(kc p) f -> p kc f", p=128)
    S.dma(wbig[:, :, :], C.w_in.v(src), eng="pool")
    for tt in range(16):
        p = ps[tt % 2]
        bank = (tt // 2) % 4
        pv = p.s(bank)[:, bank * 512:(bank + 1) * 512]
        for kc in range(32):
            S.mm(pv, xnT.s(tt)[:, kc, tt * 128:(tt + 1) * 128], wbig[:, kc, :], start=(kc == 0), stop=(kc == 31))
        s4 = st4[tt % 2]
        S.memset("dve", s4[:, :], 0.0)
        S.act(junk[tt % 2][:, :], pv, AF.Square, accum=s4[:, 0:1])
        S.rsqrt(s4[:, 1:2], s4[:, 0:1], 1.0 / 512, EPS)
        S.stt("dve", cb[tt % 2][:, :], pv, s4[:, 1:2], gck[:, :], ALU.mult, ALU.mult)
        S.dma(C.CTM.v(C.CTM.h[tt * 128:(tt + 1) * 128, :], tt, tt + 1), cb[tt % 2][:, :])
    srcm = C.wmisc.h[l].rearrange("(kc p) f -> p kc f", p=128)
    S.dma(wbig.v(wbig.h[:, :, 0:176]), C.wmisc.v(srcm), eng="pool")
    for tt in range(16):
        p = ps[tt % 2]
        bank = (tt // 2) % 4
        pv = p.s(bank)[:, bank * 512:bank * 512 + 176]
        for kc in range(32):
            S.mm(pv, xnT.s(tt)[:, kc, tt * 128:(tt + 1) * 128], wbig.v(wbig.h[:, kc, 0:176]),
                 start=(kc == 0), stop=(kc == 31))
        g_ = gt[tt % 2]
        pg = p.s(bank)[:, bank * 512:bank * 512 + 16]
        pw = p.s(bank)[:, bank * 512 + 16:bank * 512 + 48]
        pk = p.s(bank)[:, bank * 512 + 48:bank * 512 + 176]
        S.copy("act", g_[:, 0:16], pg)
        S.act(g_[:, 16:48], pw, AF.Copy, scale=1.0 / 64.0)
        S.dma(C.GT.v(C.GT.h[tt * 128:(tt + 1) * 128, :], tt, tt + 1), g_[:, 0:16])
        S.dma(C.WI.v(C.WI.h[tt * 128:(tt + 1) * 128, :], tt, tt + 1), g_[:, 16:48])
        s4 = st4[tt % 2]
        S.memset("dve", s4[:, :], 0.0)
        S.act(kf[tt % 2][:, :], pk, AF.Copy, accum=s4[:, 0:1])
        S.act(junk[tt % 2][:, 0:128], pk, AF.Square, accum=s4[:, 1:2])
        S.ts("dve", s4[:, 2:3], s4[:, 0:1], 1.0 / 128, ALU.mult)
        S.tt("dve", s4[:, 3:4], s4[:, 2:3], s4[:, 2:3], ALU.mult)
        S.stt("dve", s4[:, 4:5], s4[:, 1:2], 1.0 / 128, s4[:, 3:4], ALU.mult, ALU.subtract)
        S.rsqrt(s4[:, 5:6], s4[:, 4:5], 1.0, EPS)
        S.ts("dve", kf[tt % 2][:, :], kf[tt % 2][:, :], s4[:, 2:3], ALU.subtract, s4[:, 5:6], ALU.mult)
        S.tt("dve", kf[tt % 2][:, :], kf[tt % 2][:, :], gk[:, :], ALU.mult)
        S.tt("dve", kb[tt % 2][:, :], kf[tt % 2][:, :], bk[:, :], ALU.add)
        S.dma(C.KTM.v(C.KTM.h[tt * 128:(tt + 1) * 128, :], tt, tt + 1), kb[tt % 2][:, :])


import math
LNSC = math.log(512.0 ** -0.5)


def phase_mlstm(S, C, l, heads=range(8), stop=99):
    with ExitStack() as st:
        sb = lambda n, shp, dt=F32, nseg=1: S.sbuf(n, shp, dt, nseg=nseg, stack=st)
        G = sb("G", [128, 16, 16])
        biB = sb("biB", [128, 16, 8])
        bfB = sb("bfB", [128, 16, 8])
        S.dma(G[:, :, :], C.GT.v(C.GT.h[:, :].rearrange("(c p) g -> p c g", p=128)))
        S.dma(biB[:, :, :], C.biB.v(C.biB.h[l]))
        S.dma(bfB[:, :, :], C.bfB.v(C.bfB.h[l]))
        ppt = sb("pptm", [128, 32, 8])
        S.dma(ppt[:, :, :], C.pp.v(C.pp.h[l]))
        li = sb("li", [128, 128]); lf = sb("lf", [128, 128]); bsb = sb("bsb", [128, 128]); BL = sb("BL", [128, 128])
        kA = sb("kA", [128, 128]); wS = sb("wS", [128, 128]); qA = sb("qA", [128, 128]); eBL = sb("eBL", [128, 128])
        wSb = sb("wSb", [128, 128], BF16)
        d1 = sb("d1", [128, 128])
        v3 = lambda t: t.v(t.h[:, :].rearrange("p (c h) -> p c h", h=8))
        psA = S.psum("psA", [128, 512], F32, stack=st)
        psB = S.psum("psB", [128, 512], F32, stack=st)
        pS = S.psum("pS", [128, 512], F32, stack=st)
        pN = S.psum("pN", [128, 512], F32, stack=st)
        pDU = S.psum("pDU", [128, 512], F32, nseg=1, stack=st)
        pU = [S.psum("pU", [128, 512], F32, stack=st) for _ in range(2)]
        pT = S.psum("pTm", [128, 1024], BF16, stack=st)
        S.tt("dve", v3(li), G.v(G.h[:, :, 0:8]), biB[:, :, :], ALU.add)
        S.tt("dve", v3(lf), G.v(G.h[:, :, 8:16]), bfB[:, :, :], ALU.add)
        S.act(lf[:, :], lf[:, :], AF.Exp, scale=-1.0)
        S.act(lf[:, :], lf[:, :], AF.Ln, bias=1.0)
        S.ts("dve", lf[:, :], lf[:, :], -1.0, ALU.mult)
        S.mm(psA[:, 0:128], C.utf[:, :], lf[:, :])
        S.mm(psB[:, 0:128], C.onesf[:, :], lf[:, :])
        S.copy("act", bsb[:, :], psA[:, 0:128])
        S.copy("act", BL[:, :], psB[:, 0:128])
        S.tt("dve", d1[:, :], li[:, :], bsb[:, :], ALU.subtract)
        S.act(kA[:, :], d1[:, :], AF.Exp, bias=LNSC)
        S.tt("dve", d1[:, :], d1[:, :], BL[:, :], ALU.add)
        S.act(wS[:, :], d1[:, :], AF.Exp, bias=LNSC)
        S.copy("dve", wSb[:, :], wS[:, :])
        S.act(qA[:, :], bsb[:, :], AF.Exp)
        S.act(eBL[:, :], BL[:, :], AF.Exp)

        if stop <= 1:
            S.barrier()
            return
        xcT = sb("xcT", [128, 4, T], BF16); xmT = sb("xmT", [128, 4, T], BF16)
        qT = sb("qT", [128, 4, T], BF16, nseg=4); kT = sb("kT", [128, 4, T], BF16, nseg=4)
        som = sb("som", [128, 4, T], BF16); szm = sb("szm", [128, 4, T], BF16)
        xcs = sb("xcs", [128, 4, T], BF16); ymT = sb("ymT", [128, 4, T], BF16, nseg=16)
        wq = sb("wq", [128, 4, 128], BF16); wk = sb("wk", [128, 4, 128], BF16); wv = sb("wv", [128, 4, 128], BF16)
        ktm = [sb("ktm", [128, 512], BF16) for _ in range(2)]
        vtm = [sb("vtm", [128, 512], BF16) for _ in range(2)]
        v3t = [sb("v3t", [128, 520], BF16) for _ in range(2)]
        at = [sb("at", [128, 128], BF16) for _ in range(2)]
        hnt = [sb("hnt", [128, 512], BF16) for _ in range(2)]
        yt = [sb("yt", [128, 4, 128]) for _ in range(2)]
        small = [sb("small", [128, 16]) for _ in range(2)]
        junk = sb("junkm", [128, 512], BF16)
        Cst = sb("Cst", [128, 4, 512], F32, nseg=4)
        Cb = sb("Cb", [128, 4, 520], BF16, nseg=5)
        nst = sb("nst", [128, 4])

        for h in heads:
            rows = lambda Tl: Tl.v(Tl.h[h * 512:(h + 1) * 512, :].rearrange("(j p) t -> p j t", p=128), h * 4, h * 4 + 4)
            S.dma(xcT[:, :, :], rows(C.XC))
            S.dma(xmT[:, :, :], rows(C.XM))
            S.dma(som[:, :, :], rows(C.SOM))
            S.dma(szm[:, :, :], rows(C.SZM))
            for (wt, wsrc) in ((wq, C.wq_bd), (wk, C.wk_bd), (wv, C.wv_bd)):
                S.dma(wt[:, :, :], wsrc.v(wsrc.h[l][h * 4:(h + 1) * 4].rearrange("j p o -> p j o")), eng="pool")
            n = 0
            for (dst, wt) in ((qT, wq), (kT, wk)):
                for j in range(4):
                    for tt in range(4):
                        p = psA if n % 2 == 0 else psB
                        n += 1
                        S.mm(p[:, :], wt[:, j, :], xcT[:, j, tt * 512:(tt + 1) * 512])
                        S.copy("act", dst.s(tt)[:, j, tt * 512:(tt + 1) * 512], p[:, :])
            for j in range(4):
                fc = h * 4 + j
                S.act(som[:, j, :], som[:, j, :], AF.Copy, scale=ppt[:, fc, 5:6])
                S.act(xcs[:, j, :], xcT[:, j, :], AF.Copy, scale=ppt[:, fc, 6:7])
            for c in range(16 if stop > 2.05 else 0):
                tc = slice(c * 128, (c + 1) * 128)
                col = c * 8 + h
                cs = slice(col, col + 1)
                b2 = c % 2
                tseg = c // 4
                for j in range(4):
                    S.mm(psA[:, j * 128:(j + 1) * 128], xcT[:, j, tc], wk[:, j, :])
                S.copy("act", ktm[b2][:, :], psA[:, :])
                for j in range(4):
                    S.mm(psB[:, j * 128:(j + 1) * 128], xmT[:, j, tc], wv[:, j, :])
                S.copy("act", vtm[b2][:, :], psB[:, :])
                if stop <= 2.1:
                    continue
                S.act(v3t[b2][:, 0:512], psB[:, :], AF.Copy, scale=wS[:, cs])
                if stop <= 2.2:
                    continue
                S.copy("dve", v3t[b2][:, 512:513], wS[:, cs])
                if stop <= 2.3:
                    continue
                for j in range(4):
                    S.mm(pS[:, 0:128], kT.s(tseg)[:, j, tc], qT.s(tseg)[:, j, tc], start=(j == 0), stop=(j == 3))
                if stop <= 2.4:
                    continue
                S.stt("dve", at[b2][:, :], pS[:, 0:128], kA[:, cs], C.utf[:, :], ALU.mult, ALU.mult)
                if stop <= 3:
                    continue
                S.mm(pN[:, :], at[b2][:, :], vtm[b2][:, :], start=True, stop=(c == 0))
                if c > 0:
                    for j in range(4):
                        S.mm(pN[:, :], qT.s(tseg)[:, j, tc], Cb.s(j)[:, j, 0:512], start=False, stop=(j == 3))
                S.mm(pDU[:, 0:1], at[b2][:, :], C.onesb[:, 0:1], start=True, stop=(c == 0))
                if c > 0:
                    for j in range(4):
                        S.mm(pDU[:, 0:1], qT.s(tseg)[:, j, tc], Cb.s(4)[:, j, 512:513], start=False, stop=(j == 3))
                sm = small[b2]
                k_ = lambda i: sm[:, i:i + 1]
                S.ts("dve", k_(0), pDU[:, 0:1], qA[:, cs], ALU.mult)
                S.ts("dve", k_(1), k_(0), -1.0, ALU.mult)
                S.tt("dve", k_(2), k_(0), k_(1), ALU.max)
                S.ts("dve", k_(2), k_(2), 1.0, ALU.max)
                S.op("dve", lambda e, o=k_(3), i=k_(2): e.reciprocal(o.ap, i.ap), reads=[k_(2)], writes=[k_(3)])
                S.tt("dve", k_(4), k_(3), qA[:, cs], ALU.mult)
                S.memset("dve", sm[:, 5:7], 0.0)
                S.act(junk[:, :], pN[:, :], AF.Copy, accum=k_(5))
                S.act(junk[:, :], pN[:, :], AF.Square, accum=k_(6))
                S.ts("dve", k_(7), k_(5), 1.0 / 512, ALU.mult)
                S.tt("dve", k_(8), k_(7), k_(7), ALU.mult)
                S.stt("dve", k_(9), k_(6), 1.0 / 512, k_(8), ALU.mult, ALU.subtract)
                S.tt("dve", k_(10), k_(4), k_(4), ALU.mult)
                S.tt("dve", k_(11), k_(9), k_(10), ALU.mult)
                S.rsqrt(k_(12), k_(11), 1.0, EPS)
                S.tt("dve", k_(13), k_(12), k_(4), ALU.mult)
                S.ts("dve", hnt[b2][:, :], pN[:, :], k_(7), ALU.subtract, k_(13), ALU.mult)
                if stop <= 4:
                    continue
                for j in range(4):
                    S.transpose(pT[:, j * 128:(j + 1) * 128], hnt[b2][:, j * 128:(j + 1) * 128], C.ident[:, :])
                pTv = pT.v(pT.h[:, 0:512].rearrange("p (a b) -> p a b", a=4))
                S.tt("dve", yt[b2][:, :, :], pTv, som[:, :, tc], ALU.mult)
                S.tt("dve", yt[b2][:, :, :], yt[b2][:, :, :], xcs[:, :, tc], ALU.add)
                S.tt("dve", ymT.s(c)[:, :, tc], yt[b2][:, :, :], szm[:, :, tc], ALU.mult)
                if c < 15 and stop > 5:
                    for j in range(4):
                        pu = pU[j % 2]
                        S.mm(pu[:, :], ktm[b2][:, j * 128:(j + 1) * 128], v3t[b2][:, 0:512])
                        S.mm(pDU[:, 8 + j:9 + j], ktm[b2][:, j * 128:(j + 1) * 128], v3t[b2][:, 512:513])
                        if c == 0:
                            S.copy("act", Cst.s(j)[:, j, :], pu[:, :])
                        else:
                            S.stt("dve", Cst.s(j)[:, j, :], Cst.s(j)[:, j, :], eBL[:, cs], pu[:, :], ALU.mult, ALU.add)
                        S.copy("act", Cb.s(j)[:, j, 0:512], Cst.s(j)[:, j, :])
                    if c == 0:
                        S.copy("dve", nst[:, :], pDU[:, 8:12])
                    else:
                        S.stt("dve", nst[:, :], nst[:, :], eBL[:, cs], pDU[:, 8:12], ALU.mult, ALU.add)
                    S.copy("dve", Cb.s(4)[:, :, 512], nst[:, :])
            S.dma(rows(C.YM), ymT[:, :, :])
        S.barrier()


def phase_dsa(S, C, l, qbs=range(16), groups=range(8)):
    with ExitStack() as st:
        sb = lambda n, shp, dt=F32, nseg=1: S.sbuf(n, shp, dt, nseg=nseg, stack=st)
        c_tm = sb("c_tm", [128, 16, 512], BF16, nseg=16)
        cT = sb("cT", [128, 4, T], BF16, nseg=16)
        kidxT = sb("kidxT", [128, T], BF16, nseg=16)
        wuk = sb("wuk", [128, 32, 512], BF16)
        wuv = sb("wuv", [128, 32, 4, 128], BF16)
        wi_all = sb("wi_all", [128, 16, 32])
        S.dma(c_tm[:, :, :], C.CTM.v(C.CTM.h[:, :].rearrange("(b p) c -> p b c", p=128)))
        S.dma(wi_all[:, :, :], C.WI.v(C.WI.h[:, :].rearrange("(b p) h -> p b h", p=128)))
        S.dma(wuk[:, :, :], C.w_ukT.v(C.w_ukT.h[l].rearrange("h d c -> d h c")), eng="pool")
        for hh_ in range(4):
            S.dma(wuv[:, hh_ * 8:(hh_ + 1) * 8, :, :],
                  C.w_uv.v(C.w_uv.h[l][hh_ * 8:(hh_ + 1) * 8].rearrange("h (cc p) d -> p h cc d", p=128)), eng="pool")
        with ExitStack() as st2:
            kidx_tm = S.sbuf("kidx_tm", [128, 16, 128], BF16, stack=st2)
            pX = [S.psum("pX", [128, 1024], BF16, stack=st2) for _ in range(2)]
            S.dma(kidx_tm[:, :, :], C.KTM.v(C.KTM.h[:, :].rearrange("(b p) i -> p b i", p=128)))
            for b in range(16):
                p = pX[b % 2]
                for cc in range(4):
                    S.transpose(p[:, cc * 128:(cc + 1) * 128], c_tm.s(b)[:, b, cc * 128:(cc + 1) * 128], C.ident[:, :])
                S.copy("act", cT.s(b)[:, :, b * 128:(b + 1) * 128], p.v(p.h[:, 0:512].rearrange("p (a b) -> p a b", a=4)))
            for b4 in range(4):
                p = pX[b4 % 2]
                for k in range(4):
                    b = b4 * 4 + k
                    S.transpose(p[:, k * 128:(k + 1) * 128], kidx_tm[:, b, :], C.ident[:, :])
                S.copy("act", kidxT.s(b4 * 4, b4 * 4 + 4)[:, b4 * 512:(b4 + 1) * 512], p[:, 0:512])
            S.barrier()
        pG = [S.psum("pG", [128, 512], F32, stack=st) for _ in range(3)]
        gi = [0]

        def nextp():
            gi[0] += 1
            return pG[gi[0] % 3]
        pO = [S.psum("pO", [128, 512], F32, stack=st) for _ in range(4)]
        pSum = S.psum("pSum", [128, 512], F32, stack=st)
        qiT = sb("qiT", [128, 32, 128], BF16)
        qaT = sb("qaT", [128, 32, 128], BF16)
        szaT = sb("szaT", [128, 32, 128], BF16)
        yaT = sb("yaT", [128, 32, 128], BF16, nseg=8)
        score = sb("score", [128, T])
        work = sb("work", [128, T])
        maskf = sb("maskf", [128, T])
        maskT = sb("maskT", [128, 16, 128], BF16)
        rt = [sb("rt", [128, 512]) for _ in range(3)]
        m8 = sb("m8", [128, 8])
        qlat = [sb("qlat", [128, 4, 512], BF16) for _ in range(2)]
        et = [sb("et", [128, 512], BF16) for _ in range(3)]
        ptt = [sb("ptt", [128, 512], BF16) for _ in range(3)]
        rs = sb("rs", [128, 512])
        olat = sb("olat", [128, 4, 512], BF16)
        SC = 128.0 ** -0.5
        for qb in qbs:
            tq = slice(qb * 128, (qb + 1) * 128)
            Sc = (qb + 1) * 128
            col = lambda Tl: Tl.v(Tl.h[:, tq].rearrange("(h i) t -> i h t", i=128))
            S.dma(qiT[:, :, :], col(C.QI))
            S.dma(qaT[:, :, :], col(C.QA))
            S.dma(szaT[:, :, :], col(C.SZA))
            n = 0
            for pc in range((Sc + 511) // 512):
                w = min(512, Sc - pc * 512)
                sv = score[:, pc * 512:pc * 512 + w]
                for h in range(32):
                    p = nextp()
                    r = rt[n % 3]
                    n += 1
                    S.mm(p[:, 0:w], qiT[:, h, :], kidxT.s(pc * 4, pc * 4 + (w + 127) // 128)[:, pc * 512:pc * 512 + w])
                    S.act(r[:, 0:w], p[:, 0:w], AF.Relu)
                    if h == 0:
                        S.ts("dve", sv, r[:, 0:w], wi_all[:, qb, h:h + 1], ALU.mult)
                    else:
                        S.stt("dve", sv, r[:, 0:w], wi_all[:, qb, h:h + 1], sv, ALU.mult, ALU.add)
            S.tt("dve", score[:, qb * 128:(qb + 1) * 128], score[:, qb * 128:(qb + 1) * 128], C.negtri[:, :], ALU.add)
            if qb >= 2:
                S.copy("act", work[:, 0:Sc], score[:, 0:Sc])
                for it in range(32):
                    S.op("dve", lambda e, Sc=Sc: e.max(out=m8.h[:, :], in_=work.h[:, 0:Sc]), reads=[work[:, :]], writes=[m8[:, :]])
                    if it < 31:
                        S.op("dve", lambda e, Sc=Sc: e.match_replace(out=work.h[:, 0:Sc], in_to_replace=m8.h[:, :],
                                                                   in_values=work.h[:, 0:Sc], imm_value=NEG),
                             reads=[work[:, :], m8[:, :]], writes=[work[:, :]])
                S.ts("dve", maskf[:, 0:Sc], score[:, 0:Sc], m8[:, 7:8], ALU.is_ge)
            else:
                S.ts("dve", maskf[:, 0:Sc], score[:, 0:Sc], -1.0e29, ALU.is_gt)
            for b4 in range((qb + 4) // 4):
                nb = min(4, qb + 1 - b4 * 4)
                pM = nextp()
                for k in range(nb):
                    b = b4 * 4 + k
                    S.transpose(pM[:, k * 128:(k + 1) * 128], maskf[:, b * 128:(b + 1) * 128], C.identf[:, :])
                S.copy("act", maskT[:, b4 * 4:b4 * 4 + nb, :],
                       pM.v(pM.h[:, 0:nb * 128].rearrange("p (a b) -> p a b", a=nb)))
            for g in groups:
                ql = qlat[g % 2]
                for cc in range(4):
                    pM = nextp()
                    for hh in range(4):
                        h = g * 4 + hh
                        S.mm(pM[:, hh * 128:(hh + 1) * 128], wuk[:, h, cc * 128:(cc + 1) * 128], qaT[:, h, :])
                    S.copy("act", ql[:, cc, :], pM[:, :])
                for b in range(qb + 1):
                    pa = nextp()
                    e_ = et[n % 3]
                    pt_ = ptt[n % 3]
                    n += 1
                    for cc in range(4):
                        S.mm(pa[:, :], cT.s(b)[:, cc, b * 128:(b + 1) * 128], ql[:, cc, :], start=(cc == 0), stop=(cc == 3))
                    S.act(e_[:, :], pa[:, :], AF.Exp, scale=SC)
                    mb = maskT.v(maskT.h[:, b:b + 1, :].to_broadcast([128, 4, 128]))
                    S.tt("dve", pt_.v(pt_.h[:, :].rearrange("p (a b) -> p a b", a=4)),
                         e_.v(e_.h[:, :].rearrange("p (a b) -> p a b", a=4)), mb, ALU.mult)
                    for cc in range(4):
                        S.mm(pO[cc][:, :], c_tm.s(b)[:, b, cc * 128:(cc + 1) * 128], pt_[:, :], start=(b == 0), stop=(b == qb))
                    S.mm(pSum[:, :], C.onesb[:, :], pt_[:, :], start=(b == 0), stop=(b == qb))
                S.op("dve", lambda e: e.reciprocal(rs.h[:, :], pSum.h[:, :]), reads=[pSum[:, :]], writes=[rs[:, :]])
                for cc in range(4):
                    S.tt("dve", olat[:, cc, :], pO[cc][:, :], rs[:, :], ALU.mult)
                pM = nextp()
                for hh in range(4):
                    h = g * 4 + hh
                    for cc in range(4):
                        S.mm(pM[:, hh * 128:(hh + 1) * 128], wuv[:, h, cc, :], olat[:, cc, hh * 128:(hh + 1) * 128],
                             start=(cc == 0), stop=(cc == 3))
                S.tt("dve", yaT.s(g)[:, g * 4:(g + 1) * 4, :], pM.v(pM.h[:, :].rearrange("p (a b) -> p a b", a=4)),
                     szaT[:, g * 4:(g + 1) * 4, :], ALU.mult)
            S.dma(C.YA.v(C.YA.h[:, tq].rearrange("(h d) t -> d h t", d=128)), yaT[:, :, :])
        S.barrier()


def phase_out(S, C, l, xsrc, xdst):
    with ExitStack() as st:
        sb = lambda n, shp, dt=F32, nseg=1: S.sbuf(n, shp, dt, nseg=nseg, stack=st)
        ymT = sb("ymTo", [128, 32, 1024], BF16)
        yaT = sb("yaTo", [128, 32, 1024], BF16)
        wm = [sb("wm", [128, 32, 128], BF16) for _ in range(2)]
        wa = [sb("wa", [128, 32, 128], BF16) for _ in range(2)]
        sgm = [sb("sgm", [128, 1024], BF16) for _ in range(2)]
        sga = [sb("sga", [128, 1024], BF16) for _ in range(2)]
        t1 = [sb("t1", [128, 1024]) for _ in range(2)]
        t2 = [sb("t2", [128, 1024]) for _ in range(2)]
        mg = [sb("mg", [128, 1024], BF16) for _ in range(2)]
        pm = [S.psum("pm", [128, 1024], F32, nseg=2, stack=st) for _ in range(2)]
        pa = [S.psum("pa", [128, 1024], F32, nseg=2, stack=st) for _ in range(2)]
        for half in range(2):
            th = slice(half * 1024, (half + 1) * 1024)
            S.dma(ymT[:, :, :], C.YM.v(C.YM.h[:, th].rearrange("(kc p) t -> p kc t", p=128)))
            S.dma(yaT[:, :, :], C.YA.v(C.YA.h[:, th].rearrange("(kc p) t -> p kc t", p=128)))
            for fs in range(32):
                i = fs % 2
                fsl = slice(fs * 128, (fs + 1) * 128)
                S.dma(wm[i][:, :, :], C.w_bm.v(C.w_bm.h[l, fs]), eng="pool")
                S.dma(wa[i][:, :, :], C.w_ba.v(C.w_ba.h[l, fs]), eng="pool")
                S.dma(sgm[i][:, :], C.SGM.v(C.SGM.h[fsl, th], fs, fs + 1))
                S.dma(sga[i][:, :], C.SGA.v(C.SGA.h[fsl, th], fs, fs + 1))
                for (pp_, w_, y_) in ((pm[i], wm[i], ymT), (pa[i], wa[i], yaT)):
                    for tt in range(2):
                        for kc in range(32):
                            S.mm(pp_.s(tt)[:, tt * 512:(tt + 1) * 512], w_[:, kc, :], y_[:, kc, tt * 512:(tt + 1) * 512],
                                 start=(kc == 0), stop=(kc == 31))
                S.tt("dve", t1[i][:, :], pm[i][:, :], sgm[i][:, :], ALU.mult)
                S.tt("dve", t2[i][:, :], pa[i][:, :], sga[i][:, :], ALU.mult)
                S.tt("dve", mg[i][:, :], t1[i][:, :], t2[i][:, :], ALU.add)
                S.dma(C.MG.v(C.MG.h[fsl, th], fs, fs + 1), mg[i][:, :])
        S.barrier()
    with ExitStack() as st:
        sb = lambda n, shp, dt=F32, nseg=1: S.sbuf(n, shp, dt, nseg=nseg, stack=st)
        mgT = sb("mgT", [128, 32, 1024], BF16)
        wo = [sb("wo", [128, 32, 512], BF16) for _ in range(2)]
        xo = [sb("xo", [128, 512]) for _ in range(3)]
        po = [S.psum("po", [128, 512], F32, stack=st) for _ in range(4)]
        n = 0
        for half in range(2):
            th = slice(half * 1024, (half + 1) * 1024)
            S.dma(mgT[:, :, :], C.MG.v(C.MG.h[:, th].rearrange("(kc p) t -> p kc t", p=128)))
            for fs in range(8):
                fsl = slice(fs * 512, (fs + 1) * 512)
                w_ = wo[fs % 2]
                S.dma(w_[:, :, :], C.w_out.v(C.w_out.h[l, fs]), eng="pool")
                for tt in range(8):
                    tok = half * 1024 + tt * 128
                    seg = tok // 128
                    x_ = xo[n % 3]
                    p_ = po[n % 4]
                    n += 1
                    S.dma(x_[:, :], xsrc.v(xsrc.h[tok:tok + 128, fsl], seg, seg + 1))
                    for kc in range(32):
                        S.mm(p_[:, :], mgT[:, kc, tt * 128:(tt + 1) * 128], w_[:, kc, :], start=(kc == 0), stop=(kc == 31))
                    S.tt("dve", x_[:, :], p_[:, :], x_[:, :], ALU.add)
                    S.dma(xdst.v(xdst.h[tok:tok + 128, fsl], seg, seg + 1), x_[:, :])
        S.barrier()


def phase_final(S, C, xsrc):
    with ExitStack() as st:
        xt = [S.sbuf("xtf", [128, D], F32, stack=st) for _ in range(2)]
        junk = S.sbuf("junkf", [128, D], BF16, stack=st)
        gB = S.sbuf("gBf", [128, D], F32, stack=st)
        ss = [S.sbuf("ssf", [128, 16], F32, stack=st) for _ in range(2)]
        S.dma(gB[:, :], C.gfB[:, :])
        for tt in range(16):
            x_ = xt[tt % 2]
            s_ = ss[tt % 2]
            S.dma(x_[:, :], xsrc.v(xsrc.h[tt * 128:(tt + 1) * 128, :], tt, tt + 1))
            S.memset("dve", s_[:, :], 0.0)
            S.act(junk[:, :], x_[:, :], AF.Square, accum=s_[:, 0:1])
            S.rsqrt(s_[:, 1:2], s_[:, 0:1], 1.0 / D, EPS)
            S.stt("dve", x_[:, :], x_[:, :], s_[:, 1:2], gB[:, :], ALU.mult, ALU.mult)
            S.dma(C.out.v(C.out.h[tt * 128:(tt + 1) * 128, :], tt, tt + 1), x_[:, :])
        S.barrier()
def _bd(w):
    out = np.zeros((2, 32, 128, 128), np.float32)
    wr = w.reshape(2, 32, 32, 4, 4)
    for g in range(32):
        out[:, :, 4 * g:4 * g + 4, 4 * g:4 * g + 4] = wr[:, :, g]
    return out


def prep_shared(inp):
    f = lambda a: np.ascontiguousarray(np.asarray(a, dtype=np.float32))
    sh = {}
    w_in = f(inp["w_in"])
    sh["w_in"] = w_in
    sh["wmisc"] = f(np.concatenate([w_in[:, :, 12288:12304], w_in[:, :, 25232:25264], w_in[:, :, 25104:25232]], axis=2))
    slab = lambda w, fw: f(np.asarray(w, np.float32).reshape(2, 32, 128, D // fw, fw).transpose(0, 3, 2, 1, 4))
    sh["w_bm"] = slab(inp["w_bm"], 128); sh["w_ba"] = slab(inp["w_ba"], 128); sh["w_out"] = slab(inp["w_out"], 512)
    sh["w_ukT"] = f(np.transpose(np.asarray(inp["w_uk"]), (0, 1, 3, 2)))
    sh["w_uv"] = f(inp["w_uv"])
    sh["wq_bd"] = _bd(np.asarray(inp["w_q_m"], np.float32))
    sh["wk_bd"] = _bd(np.asarray(inp["w_k_m"], np.float32))
    sh["wv_bd"] = _bd(np.asarray(inp["w_v_m"], np.float32))
    pp = np.zeros((2, 128, 32, 8), np.float32)
    cw = np.asarray(inp["conv_w"], np.float32)
    for k in range(4):
        pp[:, :, :, k] = cw[:, k].reshape(2, 32, 128).transpose(0, 2, 1)
    pp[:, :, :, 4] = np.asarray(inp["conv_b"], np.float32).reshape(2, 32, 128).transpose(0, 2, 1)
    pp[:, :, :, 5] = np.asarray(inp["g_head_m"], np.float32).reshape(2, 32, 128).transpose(0, 2, 1)
    pp[:, :, :, 6] = np.asarray(inp["skip_m"], np.float32).reshape(2, 32, 128).transpose(0, 2, 1)
    sh["pp"] = pp
    bc = lambda v, shape: f(np.broadcast_to(np.asarray(v, np.float32), shape))
    sh["gnB"] = bc(np.asarray(inp["g_norm"])[:, None, :], (2, 128, 4096))
    sh["gfB"] = bc(np.asarray(inp["g_final"])[None, :], (128, 4096))
    sh["biB"] = bc(np.asarray(inp["b_i"])[:, None, None, :], (2, 128, 16, 8))
    sh["bfB"] = bc(np.asarray(inp["b_f"])[:, None, None, :], (2, 128, 16, 8))
    sh["gckvB"] = bc(np.asarray(inp["g_ckv"])[:, None, :], (2, 128, 512))
    sh["gkB"] = bc(np.asarray(inp["g_kidx"])[:, None, :], (2, 128, 128))
    sh["bkB"] = bc(np.asarray(inp["b_kidx"])[:, None, :], (2, 128, 128))
    return sh


_CACHE = {}


def build_program():
    nc = bass.Bass("TRN2", target_bir_lowering=False)
    S = Sched(nc)
    C = declare(S)
    consts(S, C)
    xs = [C.x, C.X1, C.X2]
    for l in range(2):
        with ExitStack() as st:
            xnT = S.sbuf("xnT", [128, 32, 2048], BF16, nseg=16, stack=st)
            phase_norm(S, C, l, xs[l], xnT)
            phase_proj(S, C, l, xnT)
        phase_mlstm(S, C, l)
        phase_dsa(S, C, l)
        phase_out(S, C, l, xs[l], xs[l + 1])
    phase_final(S, C, xs[2])
    S.finish()
    S.emit()
    return nc


def kernel(**inputs):
    n = 8
    if "nc" not in _CACHE:
        _CACHE["nc"] = build_program()
    nc = _CACHE["nc"]
    shared = prep_shared(inputs)
    x = np.asarray(inputs["x"], dtype=np.float32)
    in_maps = []
    for c in range(n):
        m = dict(shared)
        m["x"] = np.ascontiguousarray(x[c])
        in_maps.append(m)
    res = run_bass_kernel_spmd(nc, in_maps, core_ids=list(range(n)))
    out = np.stack([np.asarray(res.results[c]["out"], dtype=np.float32) for c in range(n)], axis=0)
    return out
```

```python
import numpy as np
import numpy as np
import concourse.bass as bass
import concourse.mybir as mybir
from concourse.bass_utils import run_bass_kernel_spmd
from contextlib import ExitStack

F32 = mybir.dt.float32
BF16 = mybir.dt.bfloat16
AF = mybir.ActivationFunctionType
ALU = mybir.AluOpType
AX = mybir.AxisListType

SEM_CAP = 30000
DMA_RING = 8
COMPUTE = ("pe", "act", "dve", "pool")


class View:
    __slots__ = ("ap", "buf", "lo", "hi")

    def __init__(self, ap, buf, lo, hi):
        self.ap, self.buf, self.lo, self.hi = ap, buf, lo, hi


class _Seg:
    def __init__(self, t, lo, hi):
        self.t, self.lo, self.hi = t, lo, hi

    def __getitem__(self, idx):
        return View(self.t.h[idx], self.t, self.lo, self.hi)


class Tile:
    def __init__(self, handle, nseg=1, name=""):
        self.h = handle
        self.nseg = nseg
        self.name = name
        self.lw = [None] * nseg
        self.rd = [dict() for _ in range(nseg)]
        self.psum = False
        self.prd = {}

    def __getitem__(self, idx):
        return View(self.h[idx], self, 0, self.nseg)

    def s(self, lo, hi=None):
        if hi is None:
            hi = lo + 1
        assert 0 <= lo < hi <= self.nseg, (self.name, lo, hi, self.nseg)
        return _Seg(self, lo, hi)

    def v(self, ap, lo=0, hi=None):
        return View(ap, self, lo, self.nseg if hi is None else hi)


class Op:
    __slots__ = ("eng", "fn", "deps", "dma", "signal", "sem", "val", "gidx", "qidx")

    def __init__(self, eng, fn, dma):
        self.eng, self.fn, self.dma = eng, fn, dma
        self.deps = set()
        self.signal = False
        self.sem = None
        self.val = 0
        self.gidx = 0
        self.qidx = 0


class Sched:
    def __init__(self, nc):
        self.nc = nc
        self.ops = []
        self.stack = ExitStack()
        self.pending_barrier = {}
        self.last_op = {}
        self.all_dma = []
        self.ndma = {}
        self._n = 0

    def sbuf(self, name, shape, dtype, nseg=1, stack=None):
        st = stack if stack is not None else self.stack
        self._n += 1
        h = st.enter_context(self.nc.sbuf_tensor(f"{name}_{self._n}", list(shape), dtype))
        fb = int(np.prod(shape[1:])) * (4 if dtype == F32 else 2)
        if fb % 64 != 0:
            st.enter_context(self.nc.sbuf_tensor(f"pad_{self._n}", [128, (64 - fb % 64) // 2], BF16))
        return Tile(h, nseg, name)

    def psum(self, name, shape, dtype, nseg=1, stack=None):
        st = stack if stack is not None else self.stack
        self._n += 1
        fb = int(np.prod(shape[1:])) * (4 if dtype == F32 else 2)
        assert fb % 2048 == 0, ("psum tiles must be whole banks", name, shape)
        h = st.enter_context(self.nc.psum_tensor(f"{name}_{self._n}", list(shape), dtype))
        t = Tile(h, nseg, name)
        t.psum = True
        t.nbank = fb // 2048
        return t

    def dram(self, name, shape, dtype, kind="Internal", nseg=1):
        h = self.nc.dram_tensor(name, list(shape), dtype, kind=kind)
        return Tile(h, nseg, name)

    def op(self, eng, fn, reads=(), writes=(), dma=False):
        o = Op(eng, fn, dma)
        o.gidx = len(self.ops)
        pb = self.pending_barrier.pop(eng, None)
        if pb:
            o.deps |= pb
        for v in reads:
            b = v.buf
            for sg in range(v.lo, v.hi):
                w = b.lw[sg]
                if w is not None:
                    o.deps.add(w)
            if b.psum:
                banks = range(b.nbank) if b.nseg != b.nbank else range(v.lo, v.hi)
                for bk in banks:
                    d = b.prd.setdefault(bk, {})
                    for e2, r in d.items():
                        if e2 != eng:
                            o.deps.add(r)
                    d[eng] = o
        for v in writes:
            b = v.buf
            for sg in range(v.lo, v.hi):
                w = b.lw[sg]
                if w is not None:
                    o.deps.add(w)
                for r in b.rd[sg].values():
                    o.deps.add(r)
        for v in reads:
            b = v.buf
            for sg in range(v.lo, v.hi):
                key = ("dma", o.gidx) if dma else eng
                b.rd[sg][key] = o
        for v in writes:
            b = v.buf
            for sg in range(v.lo, v.hi):
                b.lw[sg] = o
                b.rd[sg] = {}
        o.deps.discard(o)
        self.ops.append(o)
        self.last_op[eng] = o
        if dma:
            self.all_dma.append(o)
        return o

    def barrier(self):
        deps = set(self.last_op.values()) | set(self.all_dma)
        self.all_dma = []
        for e in ("pe", "act", "dve", "pool", "sp"):
            self.pending_barrier[e] = set(deps) | self.pending_barrier.get(e, set())

    def mm(self, out, lhsT, rhs, start=True, stop=True):
        return self.op("pe", lambda e: e.matmul(out.ap, lhsT.ap, rhs.ap, start=start, stop=stop),
                       reads=[lhsT, rhs], writes=[out])

    def transpose(self, out, in_, ident):
        return self.op("pe", lambda e: e.transpose(out.ap, in_.ap, ident.ap),
                       reads=[in_, ident], writes=[out])

    def act(self, out, in_, func, bias=None, scale=None, accum=None, eng="act"):
        reads = [in_]
        kw = {}
        if bias is not None:
            if isinstance(bias, View):
                reads.append(bias)
                kw["bias"] = bias.ap
            else:
                kw["bias"] = bias
        if scale is not None:
            if isinstance(scale, View):
                reads.append(scale)
                kw["scale"] = scale.ap
            else:
                kw["scale"] = scale
        writes = [out]
        if accum is not None:
            writes.append(accum)
            kw["accum_out"] = accum.ap
        return self.op("act", lambda e: e.activation(out.ap, in_.ap, func, **kw), reads=reads, writes=writes)

    def tt(self, eng, out, in0, in1, op):
        return self.op(eng, lambda e: e.tensor_tensor(out.ap, in0.ap, in1.ap, op), reads=[in0, in1], writes=[out])

    def ts(self, eng, out, in0, s1, op0, s2=None, op1=None, accum=None):
        reads = [in0]
        a1 = s1.ap if isinstance(s1, View) else s1
        a2 = s2.ap if isinstance(s2, View) else s2
        if isinstance(s1, View):
            reads.append(s1)
        if isinstance(s2, View):
            reads.append(s2)
        writes = [out]
        kw = {}
        if op1 is not None:
            kw["op1"] = op1
        if accum is not None:
            kw["accum_out"] = accum.ap
            writes.append(accum)
        return self.op(eng, lambda e: e.tensor_scalar(out.ap, in0.ap, a1, a2, op0, **kw), reads=reads, writes=writes)

    def stt(self, eng, out, in0, scalar, in1, op0, op1, accum=None):
        reads = [in0, in1]
        a = scalar.ap if isinstance(scalar, View) else scalar
        if isinstance(scalar, View):
            reads.append(scalar)
        writes = [out]
        kw = {}
        if accum is not None:
            kw["accum_out"] = accum.ap
            writes.append(accum)
        return self.op(eng, lambda e: e.scalar_tensor_tensor(out.ap, in0.ap, a, in1.ap, op0, op1, **kw),
                       reads=reads, writes=writes)

    def rsqrt(self, out, in_, scale, eps):
        self.act(out, in_, AF.Sqrt, bias=eps, scale=scale)
        return self.op("dve", lambda e: e.reciprocal(out.ap, out.ap), reads=[out], writes=[out])

    def copy(self, eng, out, in_):
        if eng == "act":
            return self.op("act", lambda e: e.copy(out.ap, in_.ap), reads=[in_], writes=[out])
        return self.op(eng, lambda e: e.tensor_copy(out.ap, in_.ap), reads=[in_], writes=[out])

    def memset(self, eng, out, val):
        return self.op(eng, lambda e: e.memset(out.ap, val), writes=[out])

    def dma(self, out, in_, eng="sp"):
        return self.op(eng, lambda e: e.dma_start(out=out.ap, in_=in_.ap), reads=[in_], writes=[out], dma=True)

    def emit(self):
        nc = self.nc
        ops = self.ops
        for o in ops:
            for d in o.deps:
                if d.eng == "pe" and o.eng == "pe" and not d.dma and not o.dma:
                    continue
                d.signal = True
        est = ExitStack()
        sems = {}

        def new_sem(tag):
            sems[tag] = est.enter_context(nc.semaphore(tag))
            return sems[tag]

        cnt = {e: 0 for e in COMPUTE + ("sp",)}
        cur = {}
        dcount = {}
        dring = {}
        for o in ops:
            if o.dma:
                q = o.eng
                i = dcount.get(q, 0)
                dcount[q] = i + 1
                slot = i % DMA_RING
                if (q, slot) not in dring:
                    dring[(q, slot)] = [new_sem(f"d_{q}_{slot}"), 0, None]
                ent = dring[(q, slot)]
                prev = ent[2]
                if prev is not None:
                    o.deps.add(prev)
                    prev.signal = True
                ent[1] += 16
                ent[2] = o
                o.sem, o.val = ent[0], ent[1]
                o.signal = True
            elif o.signal:
                e = o.eng
                c = cnt[e]
                if c % SEM_CAP == 0:
                    cur[e] = new_sem(f"s_{e}_{c // SEM_CAP}")
                cnt[e] = c + 1
                o.sem, o.val = cur[e], (c % SEM_CAP) + 1
                o.qidx = c + 1
        by_eng = {e: [] for e in ("pe", "act", "dve", "pool", "sp")}
        for o in ops:
            by_eng[o.eng].append(o)
        nwaits = [0]

        self.streams = {}

        def emit_engine(ename, eobj):
            stream = self.streams.setdefault(ename, [])
            waited_c = {e: 0 for e in COMPUTE + ("sp",)}
            waited_d = {}
            for o in by_eng[ename]:
                need_c = {}
                need_d = {}
                for d in o.deps:
                    if d.dma:
                        k = id(d.sem)
                        if d.val > waited_d.get(k, 0):
                            if k not in need_d or need_d[k][1] < d.val:
                                need_d[k] = (d.sem, d.val)
                    else:
                        if d.eng == "pe" and ename == "pe" and not o.dma:
                            continue
                        if not d.signal:
                            continue
                        if d.qidx > waited_c[d.eng]:
                            if d.eng not in need_c or need_c[d.eng].qidx < d.qidx:
                                need_c[d.eng] = d
                for e, d in need_c.items():
                    eobj.wait_ge(d.sem, d.val)
                    waited_c[e] = d.qidx
                    nwaits[0] += 1
                for k, (sem, val) in need_d.items():
                    eobj.wait_ge(sem, val)
                    waited_d[k] = val
                    nwaits[0] += 1
                ins = o.fn(eobj)
                if o.signal:
                    ins.then_inc(o.sem, 16 if o.dma else 1)
                stream.append(([(id(d.sem), d.val) for d in need_c.values()] + [(k, v) for k, (sm_, v) in need_d.items()],
                               (id(o.sem), 16 if o.dma else 1) if o.signal else None, o.gidx))

        with nc.Block() as block:
            @block.tensor
            def _(e):
                emit_engine("pe", e)

            @block.scalar
            def _(e):
                emit_engine("act", e)

            @block.vector
            def _(e):
                emit_engine("dve", e)

            @block.gpsimd
            def _(e):
                emit_engine("pool", e)

            @block.sync
            def _(e):
                emit_engine("sp", e)
        self.nwaits = nwaits[0]
        est.close()

    def simulate(self):
        sem = {}
        pos = {e: 0 for e in self.streams}
        progress = True
        while progress:
            progress = False
            for e, st in self.streams.items():
                while pos[e] < len(st):
                    waits, sig, gidx = st[pos[e]]
                    if all(sem.get(k, 0) >= v for k, v in waits):
                        if sig:
                            sem[sig[0]] = sem.get(sig[0], 0) + sig[1]
                        pos[e] += 1
                        progress = True
                    else:
                        break
        stuck = {e: (pos[e], len(st)) for e, st in self.streams.items() if pos[e] < len(st)}
        if stuck:
            for e in stuck:
                waits, sig, gidx = self.streams[e][pos[e]]
                print("STUCK", e, stuck[e], "gidx", gidx, [(k, v, sem.get(k, 0)) for k, v in waits])
        return not stuck

    def finish(self):
        self.barrier()
        self.op("sp", lambda e: e.nop())
T = 2048
D = 4096
NIN = 33456
EPS = 1e-6
NEG = -1.0e30
OFF = dict(xm=0, om=4096, zm=8192, ig=12288, fg=12296, qa=12304, ckv=16400, za=16912,
           qi=21008, ki=25104, wi=25232, gm=25264, ga=29360)


class Ctx:
    pass


def declare(S, dbg=()):
    C = Ctx()
    ein = lambda n, s: S.dram(n, s, F32, kind="ExternalInput")
    C.x = S.dram("x", [T, D], F32, kind="ExternalInput", nseg=16)
    C.w_in = ein("w_in", [2, D, NIN])
    C.wmisc = ein("wmisc", [2, D, 176])
    C.w_bm = ein("w_bm", [2, 32, 128, 32, 128])
    C.w_ba = ein("w_ba", [2, 32, 128, 32, 128])
    C.w_out = ein("w_out", [2, 8, 128, 32, 512])
    C.w_ukT = ein("w_ukT", [2, 32, 128, 512])
    C.w_uv = ein("w_uv", [2, 32, 512, 128])
    C.wq_bd = ein("wq_bd", [2, 32, 128, 128])
    C.wk_bd = ein("wk_bd", [2, 32, 128, 128])
    C.wv_bd = ein("wv_bd", [2, 32, 128, 128])
    C.pp = ein("pp", [2, 128, 32, 8])
    C.gnB = ein("gnB", [2, 128, D])
    C.gfB = ein("gfB", [128, D])
    C.biB = ein("biB", [2, 128, 16, 8])
    C.bfB = ein("bfB", [2, 128, 16, 8])
    C.gckvB = ein("gckvB", [2, 128, 512])
    C.gkB = ein("gkB", [2, 128, 128])
    C.bkB = ein("bkB", [2, 128, 128])
    C.out = S.dram("out", [T, D], F32, kind="ExternalOutput", nseg=16)

    def scr(n, shape, dt, nseg):
        kind = "ExternalOutput" if n in dbg else "Internal"
        return S.dram(n, shape, dt, kind=kind, nseg=nseg)
    for n in ("XM", "XC", "SOM", "SZM", "QA", "SZA", "QI", "SGM", "SGA", "YM", "YA", "MG"):
        setattr(C, n, scr(n, [D, T], BF16, 32))
    C.CTM = scr("CTM", [T, 512], BF16, 16)
    C.KTM = scr("KTM", [T, 128], BF16, 16)
    C.GT = scr("GT", [T, 16], F32, 16)
    C.WI = scr("WI", [T, 32], F32, 16)
    C.X1 = scr("X1", [T, D], F32, 16)
    C.X2 = scr("X2", [T, D], F32, 16)
    return C


def consts(S, C):
    C.identf = S.sbuf("identf", [128, 128], F32)
    C.ident = S.sbuf("ident", [128, 128], BF16)
    C.utf = S.sbuf("utf", [128, 128], F32)
    C.negtri = S.sbuf("negtri", [128, 128], F32)
    C.onesf = S.sbuf("onesf", [128, 128], F32)
    C.onesb = S.sbuf("onesb", [128, 128], BF16)
    S.memset("pool", C.identf[:, :], 1.0)
    S.op("pool", lambda e: e.affine_select(C.identf.h[:, :], C.identf.h[:, :], pattern=[[-1, 128]],
                                           compare_op=ALU.is_equal, fill=0.0, base=0, channel_multiplier=1),
         reads=[C.identf[:, :]], writes=[C.identf[:, :]])
    S.copy("dve", C.ident[:, :], C.identf[:, :])
    S.memset("pool", C.utf[:, :], 1.0)
    S.op("pool", lambda e: e.affine_select(C.utf.h[:, :], C.utf.h[:, :], pattern=[[1, 128]],
                                           compare_op=ALU.is_ge, fill=0.0, base=0, channel_multiplier=-1),
         reads=[C.utf[:, :]], writes=[C.utf[:, :]])
    S.memset("pool", C.negtri[:, :], 0.0)
    S.op("pool", lambda e: e.affine_select(C.negtri.h[:, :], C.negtri.h[:, :], pattern=[[-1, 128]],
                                           compare_op=ALU.is_ge, fill=NEG, base=0, channel_multiplier=1),
         reads=[C.negtri[:, :]], writes=[C.negtri[:, :]])
    S.memset("pool", C.onesf[:, :], 1.0)
    S.memset("pool", C.onesb[:, :], 1.0)


def phase_norm(S, C, l, xsrc, xnT):
    with ExitStack() as st:
        xt = [S.sbuf("xt", [128, D], F32, stack=st) for _ in range(2)]
        xb = [S.sbuf("xb", [128, D], BF16, stack=st) for _ in range(2)]
        gB = S.sbuf("gB", [128, D], F32, stack=st)
        ss = [S.sbuf("ss", [128, 16], F32, stack=st) for _ in range(2)]
        pT = [S.psum("pT", [128, 1024], BF16, stack=st) for _ in range(2)]
        S.dma(gB[:, :], C.gnB.v(C.gnB.h[l]))
        for tt in range(16):
            x_ = xt[tt % 2]
            b_ = xb[tt % 2]
            s_ = ss[tt % 2]
            S.dma(x_[:, :], xsrc.v(xsrc.h[tt * 128:(tt + 1) * 128, :], tt, tt + 1))
            S.memset("dve", s_[:, :], 0.0)
            S.act(b_[:, :], x_[:, :], AF.Square, accum=s_[:, 0:1])
            S.rsqrt(s_[:, 1:2], s_[:, 0:1], 1.0 / D, EPS)
            S.stt("dve", b_[:, :], x_[:, :], s_[:, 1:2], gB[:, :], ALU.mult, ALU.mult)
            for g in range(8):
                p = pT[g % 2]
                for q in range(4):
                    kc = g * 4 + q
                    S.transpose(p[:, q * 128:(q + 1) * 128], b_[:, kc * 128:(kc + 1) * 128], C.ident[:, :])
                dst = xnT.v(xnT.h[:, g * 4:(g + 1) * 4, tt * 128:(tt + 1) * 128], tt, tt + 1)
                src = p.v(p.h[:, 0:512].rearrange("p (a b) -> p a b", a=4))
                S.copy("act", dst, src)
        S.barrier()


FM_SEGS = [("xm", "XM", None), ("om", "SOM", AF.Sigmoid), ("zm", "SZM", AF.Silu), ("qa", "QA", AF.Copy),
           ("za", "SZA", AF.Silu), ("qi", "QI", AF.Copy), ("gm", "SGM", AF.Sigmoid), ("ga", "SGA", AF.Sigmoid)]


def phase_proj(S, C, l, xnT, segs=None):
    with ExitStack() as st:
        wbig = S.sbuf("wbig", [128, 32, 512], BF16, nseg=2, stack=st)
        ps = [S.psum("ps", [128, 2048], F32, nseg=4, stack=st) for _ in range(2)]
        nslab = [0]

        def run_seg(nm, scr, func, ev, xs=None, acc=None, ppt=None):
            evi = [0]

            def next_ev():
                evi[0] += 1
                return ev[evi[0] % len(ev)]
            scrT = getattr(C, scr)
            for sl in range(16):
                c0 = OFF[nm] + sl * 256
                half = nslab[0] % 2
                nslab[0] += 1
                wv = wbig.s(half)[:, :, half * 256:(half + 1) * 256]
                src = C.w_in.h[l][:, c0:c0 + 256].rearrange("(kc p) f -> p kc f", p=128)
                S.dma(wv, C.w_in.v(src), eng="pool")
                for sub in range(2):
                    p = ps[sub]
                    fchunk = sl * 2 + sub
                    for tt in range(4):
                        for kc in range(32):
                            S.mm(p.s(tt)[:, tt * 512:(tt + 1) * 512],
                                 wbig.s(half)[:, kc, half * 256 + sub * 128: half * 256 + (sub + 1) * 128],
                                 xnT.s(tt * 4, tt * 4 + 4)[:, kc, tt * 512:(tt + 1) * 512],
                                 start=(kc == 0), stop=(kc == 31))
                    rows = scrT.v(scrT.h[fchunk * 128:(fchunk + 1) * 128, :], fchunk, fchunk + 1)
                    if nm != "xm":
                        e = next_ev()
                        for tt in range(4):
                            S.act(e[:, tt * 512:(tt + 1) * 512], p.s(tt)[:, tt * 512:(tt + 1) * 512], func)
                        S.dma(rows, e[:, :])
                    else:
                        x_ = xs[sub]
                        a_ = acc[0]
                        for tt in range(4):
                            S.act(x_[:, 4 + tt * 512:4 + (tt + 1) * 512], p.s(tt)[:, tt * 512:(tt + 1) * 512], AF.Copy)
                        e = next_ev()
                        S.copy("pool", e[:, :], x_[:, 4:4 + T])
                        S.dma(rows, e[:, :])
                        P_ = lambda k, fc=fchunk: ppt[:, fc, k:k + 1]
                        S.ts("dve", a_[:, :], x_[:, 4:4 + T], P_(3), ALU.mult, P_(4), ALU.add)
                        S.stt("dve", a_[:, :], x_[:, 3:3 + T], P_(2), a_[:, :], ALU.mult, ALU.add)
                        S.stt("dve", a_[:, :], x_[:, 2:2 + T], P_(1), a_[:, :], ALU.mult, ALU.add)
                        S.stt("dve", a_[:, :], x_[:, 1:1 + T], P_(0), a_[:, :], ALU.mult, ALU.add)
                        e2 = next_ev()
                        S.act(e2[:, :], a_[:, :], AF.Silu)
                        rows2 = C.XC.v(C.XC.h[fchunk * 128:(fchunk + 1) * 128, :], fchunk, fchunk + 1)
                        S.dma(rows2, e2[:, :])

        if segs is None or "xm" in segs:
            with ExitStack() as st2:
                ev = [S.sbuf("ev", [128, 2048], BF16, stack=st2) for _ in range(2)]
                xs = [S.sbuf("xs", [128, 2048 + 4], F32, stack=st2) for _ in range(2)]
                acc = [S.sbuf("acc", [128, 2048], F32, stack=st2) for _ in range(1)]
                ppt = S.sbuf("ppt", [128, 32, 8], F32, stack=st2)
                S.dma(ppt[:, :, :], C.pp.v(C.pp.h[l]))
                for b in xs:
                    S.memset("pool", b[:, 0:4], 0.0)
                run_seg("xm", "XM", None, ev, xs, acc, ppt)
                S.barrier()
        with ExitStack() as st2:
            ev = [S.sbuf("ev", [128, 2048], BF16, stack=st2) for _ in range(3)]
            for (nm, scr, func) in FM_SEGS[1:]:
                if segs is not None and nm not in segs:
                    continue
                run_seg(nm, scr, func, ev)
            S.barrier()
        if segs is None or "tm" in segs:
            with ExitStack() as st2:
                phase_proj_tm(S, C, l, xnT, wbig, ps, st2)
                S.barrier()


def phase_proj_tm(S, C, l, xnT, wbig, ps, st):
    gck = S.sbuf("gck", [128, 512], F32, stack=st)
    gk = S.sbuf("gk", [128, 128], F32, stack=st)
    bk = S.sbuf("bk", [128, 128], F32, stack=st)
    S.dma(gck[:, :], C.gckvB.v(C.gckvB.h[l]))
    S.dma(gk[:, :], C.gkB.v(C.gkB.h[l]))
    S.dma(bk[:, :], C.bkB.v(C.bkB.h[l]))
    st4 = [S.sbuf("st4", [128, 8], F32, stack=st) for _ in range(2)]
    junk = [S.sbuf("junk2", [128, 512], F32, stack=st) for _ in range(2)]
    cb = [S.sbuf("cb", [128, 512], BF16, stack=st) for _ in range(2)]
    kb = [S.sbuf("kb", [128, 128], BF16, stack=st) for _ in range(2)]
    kf = [S.sbuf("kf", [128, 128], F32, stack=st) for _ in range(2)]
    gt = [S.sbuf("gt", [128, 48], F32, stack=st) for _ in range(2)]
    src = C.w_in.h[l][:, OFF["ckv"]:OFF["ckv"] + 512].rearrange("(kc p) f -> p kc f", p=128)
    S.dma(wbig[:, :, :], C.w_in.v(src), eng="pool")
    for tt in range(16):
        p = ps[tt % 2]
        bank = (tt // 2) % 4
        pv = p.s(bank)[:, bank * 512:(bank + 1) * 512]
        for kc in range(32):
            S.mm(pv, xnT.s(tt)[:, kc, tt * 128:(tt + 1) * 128], wbig[:, kc, :], start=(kc == 0), stop=(kc == 31))
        s4 = st4[tt % 2]
        S.memset("dve", s4[:, :], 0.0)
        S.act(junk[tt % 2][:, :], pv, AF.Square, accum=s4[:, 0:1])
        S.rsqrt(s4[:, 1:2], s4[:, 0:1], 1.0 / 512, EPS)
        S.stt("dve", cb[tt % 2][:, :], pv, s4[:, 1:2], gck[:, :], ALU.mult, ALU.mult)
        S.dma(C.CTM.v(C.CTM.h[tt * 128:(tt + 1) * 128, :], tt, tt + 1), cb[tt % 2][:, :])
    srcm = C.wmisc.h[l].rearrange("(kc p) f -> p kc f", p=128)
    S.dma(wbig.v(wbig.h[:, :, 0:176]), C.wmisc.v(srcm), eng="pool")
    for tt in range(16):
        p = ps[tt % 2]
        bank = (tt // 2) % 4
        pv = p.s(bank)[:, bank * 512:bank * 512 + 176]
        for kc in range(32):
            S.mm(pv, xnT.s(tt)[:, kc, tt * 128:(tt + 1) * 128], wbig.v(wbig.h[:, kc, 0:176]),
                 start=(kc == 0), stop=(kc == 31))
        g_ = gt[tt % 2]
        pg = p.s(bank)[:, bank * 512:bank * 512 + 16]
        pw = p.s(bank)[:, bank * 512 + 16:bank * 512 + 48]
        pk = p.s(bank)[:, bank * 512 + 48:bank * 512 + 176]
        S.copy("act", g_[:, 0:16], pg)
        S.act(g_[:, 16:48], pw, AF.Copy, scale=1.0 / 64.0)
        S.dma(C.GT.v(C.GT.h[tt * 128:(tt + 1) * 128, :], tt, tt + 1), g_[:, 0:16])
        S.dma(C.WI.v(C.WI.h[tt * 128:(tt + 1) * 128, :], tt, tt + 1), g_[:, 16:48])
        s4 = st4[tt % 2]
        S.memset("dve", s4[:, :], 0.0)
        S.act(kf[tt % 2][:, :], pk, AF.Copy, accum=s4[:, 0:1])
        S.act(junk[tt % 2][:, 0:128], pk, AF.Square, accum=s4[:, 1:2])
        S.ts("dve", s4[:, 2:3], s4[:, 0:1], 1.0 / 128, ALU.mult)
        S.tt("dve", s4[:, 3:4], s4[:, 2:3], s4[:, 2:3], ALU.mult)
        S.stt("dve", s4[:, 4:5], s4[:, 1:2], 1.0 / 128, s4[:, 3:4], ALU.mult, ALU.subtract)
        S.rsqrt(s4[:, 5:6], s4[:, 4:5], 1.0, EPS)
        S.ts("dve", kf[tt % 2][:, :], kf[tt % 2][:, :], s4[:, 2:3], ALU.subtract, s4[:, 5:6], ALU.mult)
        S.tt("dve", kf[tt % 2][:, :], kf[tt % 2][:, :], gk[:, :], ALU.mult)
        S.tt("dve", kb[tt % 2][:, :], kf[tt % 2][:, :], bk[:, :], ALU.add)
        S.dma(C.KTM.v(C.KTM.h[tt * 128:(tt + 1) * 128, :], tt, tt + 1), kb[tt % 2][:, :])


import math
LNSC = math.log(512.0 ** -0.5)


def phase_mlstm(S, C, l, heads=range(8), stop=99):
    with ExitStack() as st:
        sb = lambda n, shp, dt=F32, nseg=1: S.sbuf(n, shp, dt, nseg=nseg, stack=st)
        G = sb("G", [128, 16, 16])
        biB = sb("biB", [128, 16, 8])
        bfB = sb("bfB", [128, 16, 8])
        S.dma(G[:, :, :], C.GT.v(C.GT.h[:, :].rearrange("(c p) g -> p c g", p=128)))
        S.dma(biB[:, :, :], C.biB.v(C.biB.h[l]))
        S.dma(bfB[:, :, :], C.bfB.v(C.bfB.h[l]))
        ppt = sb("pptm", [128, 32, 8])
        S.dma(ppt[:, :, :], C.pp.v(C.pp.h[l]))
        li = sb("li", [128, 128]); lf = sb("lf", [128, 128]); bsb = sb("bsb", [128, 128]); BL = sb("BL", [128, 128])
        kA = sb("kA", [128, 128]); wS = sb("wS", [128, 128]); qA = sb("qA", [128, 128]); eBL = sb("eBL", [128, 128])
        wSb = sb("wSb", [128, 128], BF16)
        d1 = sb("d1", [128, 128])
        v3 = lambda t: t.v(t.h[:, :].rearrange("p (c h) -> p c h", h=8))
        psA = S.psum("psA", [128, 512], F32, stack=st)
        psB = S.psum("psB", [128, 512], F32, stack=st)
        pS = S.psum("pS", [128, 512], F32, stack=st)
        pN = S.psum("pN", [128, 512], F32, stack=st)
        pDU = S.psum("pDU", [128, 512], F32, nseg=1, stack=st)
        pU = [S.psum("pU", [128, 512], F32, stack=st) for _ in range(2)]
        pT = S.psum("pTm", [128, 1024], BF16, stack=st)
        S.tt("dve", v3(li), G.v(G.h[:, :, 0:8]), biB[:, :, :], ALU.add)
        S.tt("dve", v3(lf), G.v(G.h[:, :, 8:16]), bfB[:, :, :], ALU.add)
        S.act(lf[:, :], lf[:, :], AF.Exp, scale=-1.0)
        S.act(lf[:, :], lf[:, :], AF.Ln, bias=1.0)
        S.ts("dve", lf[:, :], lf[:, :], -1.0, ALU.mult)
        S.mm(psA[:, 0:128], C.utf[:, :], lf[:, :])
        S.mm(psB[:, 0:128], C.onesf[:, :], lf[:, :])
        S.copy("act", bsb[:, :], psA[:, 0:128])
        S.copy("act", BL[:, :], psB[:, 0:128])
        S.tt("dve", d1[:, :], li[:, :], bsb[:, :], ALU.subtract)
        S.act(kA[:, :], d1[:, :], AF.Exp, bias=LNSC)
        S.tt("dve", d1[:, :], d1[:, :], BL[:, :], ALU.add)
        S.act(wS[:, :], d1[:, :], AF.Exp, bias=LNSC)
        S.copy("dve", wSb[:, :], wS[:, :])
        S.act(qA[:, :], bsb[:, :], AF.Exp)
        S.act(eBL[:, :], BL[:, :], AF.Exp)

        if stop <= 1:
            S.barrier()
            return
        xcT = sb("xcT", [128, 4, T], BF16); xmT = sb("xmT", [128, 4, T], BF16)
        qT = sb("qT", [128, 4, T], BF16, nseg=4); kT = sb("kT", [128, 4, T], BF16, nseg=4)
        som = sb("som", [128, 4, T], BF16); szm = sb("szm", [128, 4, T], BF16)
        xcs = sb("xcs", [128, 4, T], BF16); ymT = sb("ymT", [128, 4, T], BF16, nseg=16)
        wq = sb("wq", [128, 4, 128], BF16); wk = sb("wk", [128, 4, 128], BF16); wv = sb("wv", [128, 4, 128], BF16)
        ktm = [sb("ktm", [128, 512], BF16) for _ in range(2)]
        vtm = [sb("vtm", [128, 512], BF16) for _ in range(2)]
        v3t = [sb("v3t", [128, 520], BF16) for _ in range(2)]
        at = [sb("at", [128, 128], BF16) for _ in range(2)]
        hnt = [sb("hnt", [128, 512], BF16) for _ in range(2)]
        yt = [sb("yt", [128, 4, 128]) for _ in range(2)]
        small = [sb("small", [128, 16]) for _ in range(2)]
        junk = sb("junkm", [128, 512], BF16)
        Cst = sb("Cst", [128, 4, 512], F32, nseg=4)
        Cb = sb("Cb", [128, 4, 520], BF16, nseg=5)
        nst = sb("nst", [128, 4])

        for h in heads:
            rows = lambda Tl: Tl.v(Tl.h[h * 512:(h + 1) * 512, :].rearrange("(j p) t -> p j t", p=128), h * 4, h * 4 + 4)
            S.dma(xcT[:, :, :], rows(C.XC))
            S.dma(xmT[:, :, :], rows(C.XM))
            S.dma(som[:, :, :], rows(C.SOM))
            S.dma(szm[:, :, :], rows(C.SZM))
            for (wt, wsrc) in ((wq, C.wq_bd), (wk, C.wk_bd), (wv, C.wv_bd)):
                S.dma(wt[:, :, :], wsrc.v(wsrc.h[l][h * 4:(h + 1) * 4].rearrange("j p o -> p j o")), eng="pool")
            n = 0
            for (dst, wt) in ((qT, wq), (kT, wk)):
                for j in range(4):
                    for tt in range(4):
                        p = psA if n % 2 == 0 else psB
                        n += 1
                        S.mm(p[:, :], wt[:, j, :], xcT[:, j, tt * 512:(tt + 1) * 512])
                        S.copy("act", dst.s(tt)[:, j, tt * 512:(tt + 1) * 512], p[:, :])
            for j in range(4):
                fc = h * 4 + j
                S.act(som[:, j, :], som[:, j, :], AF.Copy, scale=ppt[:, fc, 5:6])
                S.act(xcs[:, j, :], xcT[:, j, :], AF.Copy, scale=ppt[:, fc, 6:7])
            for c in range(16 if stop > 2.05 else 0):
                tc = slice(c * 128, (c + 1) * 128)
                col = c * 8 + h
                cs = slice(col, col + 1)
                b2 = c % 2
                tseg = c // 4
                for j in range(4):
                    S.mm(psA[:, j * 128:(j + 1) * 128], xcT[:, j, tc], wk[:, j, :])
                S.copy("act", ktm[b2][:, :], psA[:, :])
                for j in range(4):
                    S.mm(psB[:, j * 128:(j + 1) * 128], xmT[:, j, tc], wv[:, j, :])
                S.copy("act", vtm[b2][:, :], psB[:, :])
                if stop <= 2.1:
                    continue
                S.act(v3t[b2][:, 0:512], psB[:, :], AF.Copy, scale=wS[:, cs])
                if stop <= 2.2:
                    continue
                S.copy("dve", v3t[b2][:, 512:513], wS[:, cs])
                if stop <= 2.3:
                    continue
                for j in range(4):
                    S.mm(pS[:, 0:128], kT.s(tseg)[:, j, tc], qT.s(tseg)[:, j, tc], start=(j == 0), stop=(j == 3))
                if stop <= 2.4:
                    continue
                S.stt("dve", at[b2][:, :], pS[:, 0:128], kA[:, cs], C.utf[:, :], ALU.mult, ALU.mult)
                if stop <= 3:
                    continue
                S.mm(pN[:, :], at[b2][:, :], vtm[b2][:, :], start=True, stop=(c == 0))
                if c > 0:
                    for j in range(4):
                        S.mm(pN[:, :], qT.s(tseg)[:, j, tc], Cb.s(j)[:, j, 0:512], start=False, stop=(j == 3))
                S.mm(pDU[:, 0:1], at[b2][:, :], C.onesb[:, 0:1], start=True, stop=(c == 0))
                if c > 0:
                    for j in range(4):
                        S.mm(pDU[:, 0:1], qT.s(tseg)[:, j, tc], Cb.s(4)[:, j, 512:513], start=False, stop=(j == 3))
                sm = small[b2]
                k_ = lambda i: sm[:, i:i + 1]
                S.ts("dve", k_(0), pDU[:, 0:1], qA[:, cs], ALU.mult)
                S.ts("dve", k_(1), k_(0), -1.0, ALU.mult)
                S.tt("dve", k_(2), k_(0), k_(1), ALU.max)
                S.ts("dve", k_(2), k_(2), 1.0, ALU.max)
                S.op("dve", lambda e, o=k_(3), i=k_(2): e.reciprocal(o.ap, i.ap), reads=[k_(2)], writes=[k_(3)])
                S.tt("dve", k_(4), k_(3), qA[:, cs], ALU.mult)
                S.memset("dve", sm[:, 5:7], 0.0)
                S.act(junk[:, :], pN[:, :], AF.Copy, accum=k_(5))
                S.act(junk[:, :], pN[:, :], AF.Square, accum=k_(6))
                S.ts("dve", k_(7), k_(5), 1.0 / 512, ALU.mult)
                S.tt("dve", k_(8), k_(7), k_(7), ALU.mult)
                S.stt("dve", k_(9), k_(6), 1.0 / 512, k_(8), ALU.mult, ALU.subtract)
                S.tt("dve", k_(10), k_(4), k_(4), ALU.mult)
                S.tt("dve", k_(11), k_(9), k_(10), ALU.mult)
                S.rsqrt(k_(12), k_(11), 1.0, EPS)
                S.tt("dve", k_(13), k_(12), k_(4), ALU.mult)
                S.ts("dve", hnt[b2][:, :], pN[:, :], k_(7), ALU.subtract, k_(13), ALU.mult)
                if stop <= 4:
                    continue
                for j in range(4):
                    S.transpose(pT[:, j * 128:(j + 1) * 128], hnt[b2][:, j * 128:(j + 1) * 128], C.ident[:, :])
                pTv = pT.v(pT.h[:, 0:512].rearrange("p (a b) -> p a b", a=4))
                S.tt("dve", yt[b2][:, :, :], pTv, som[:, :, tc], ALU.mult)
                S.tt("dve", yt[b2][:, :, :], yt[b2][:, :, :], xcs[:, :, tc], ALU.add)
                S.tt("dve", ymT.s(c)[:, :, tc], yt[b2][:, :, :], szm[:, :, tc], ALU.mult)
                if c < 15 and stop > 5:
                    for j in range(4):
                        pu = pU[j % 2]
                        S.mm(pu[:, :], ktm[b2][:, j * 128:(j + 1) * 128], v3t[b2][:, 0:512])
                        S.mm(pDU[:, 8 + j:9 + j], ktm[b2][:, j * 128:(j + 1) * 128], v3t[b2][:, 512:513])
                        if c == 0:
                            S.copy("act", Cst.s(j)[:, j, :], pu[:, :])
                        else:
                            S.stt("dve", Cst.s(j)[:, j, :], Cst.s(j)[:, j, :], eBL[:, cs], pu[:, :], ALU.mult, ALU.add)
                        S.copy("act", Cb.s(j)[:, j, 0:512], Cst.s(j)[:, j, :])
                    if c == 0:
                        S.copy("dve", nst[:, :], pDU[:, 8:12])
                    else:
                        S.stt("dve", nst[:, :], nst[:, :], eBL[:, cs], pDU[:, 8:12], ALU.mult, ALU.add)
                    S.copy("dve", Cb.s(4)[:, :, 512], nst[:, :])
            S.dma(rows(C.YM), ymT[:, :, :])
        S.barrier()


def phase_dsa(S, C, l, qbs=range(16), groups=range(8)):
    qbs = list(qbs)
    with ExitStack() as st:
        sb = lambda n, shp, dt=F32, nseg=1: S.sbuf(n, shp, dt, nseg=nseg, stack=st)
        c_tm = sb("c_tm", [128, 16, 512], BF16, nseg=16)
        cT = sb("cT", [128, 4, T], BF16, nseg=16)
        kidxT = sb("kidxT", [128, T], BF16, nseg=16)
        wuk = sb("wuk", [128, 32, 512], BF16)
        wuv = sb("wuv", [128, 32, 4, 128], BF16)
        wi_all = sb("wi_all", [128, 16, 32])
        S.dma(c_tm[:, :, :], C.CTM.v(C.CTM.h[:, :].rearrange("(b p) c -> p b c", p=128)))
        S.dma(wi_all[:, :, :], C.WI.v(C.WI.h[:, :].rearrange("(b p) h -> p b h", p=128)))
        S.dma(wuk[:, :, :], C.w_ukT.v(C.w_ukT.h[l].rearrange("h d c -> d h c")), eng="pool")
        for hh_ in range(4):
            S.dma(wuv[:, hh_ * 8:(hh_ + 1) * 8, :, :],
                  C.w_uv.v(C.w_uv.h[l][hh_ * 8:(hh_ + 1) * 8].rearrange("h (cc p) d -> p h cc d", p=128)), eng="pool")
        with ExitStack() as st2:
            kidx_tm = S.sbuf("kidx_tm", [128, 16, 128], BF16, stack=st2)
            pX = [S.psum("pX", [128, 1024], BF16, stack=st2) for _ in range(2)]
            S.dma(kidx_tm[:, :, :], C.KTM.v(C.KTM.h[:, :].rearrange("(b p) i -> p b i", p=128)))
            for b in range(16):
                p = pX[b % 2]
                for cc in range(4):
                    S.transpose(p[:, cc * 128:(cc + 1) * 128], c_tm.s(b)[:, b, cc * 128:(cc + 1) * 128], C.ident[:, :])
                S.copy("act", cT.s(b)[:, :, b * 128:(b + 1) * 128], p.v(p.h[:, 0:512].rearrange("p (a b) -> p a b", a=4)))
            for b4 in range(4):
                p = pX[b4 % 2]
                for k in range(4):
                    b = b4 * 4 + k
                    S.transpose(p[:, k * 128:(k + 1) * 128], kidx_tm[:, b, :], C.ident[:, :])
                S.copy("act", kidxT.s(b4 * 4, b4 * 4 + 4)[:, b4 * 512:(b4 + 1) * 512], p[:, 0:512])
            S.barrier()
        pG = [S.psum("pG", [128, 512], F32, stack=st) for _ in range(3)]
        gi = [0]

        def nextp():
            gi[0] += 1
            return pG[gi[0] % 3]
        pO = [S.psum("pO", [128, 512], F32, stack=st) for _ in range(4)]
        pSum = S.psum("pSum", [128, 512], F32, stack=st)
        qiT = [sb("qiT", [128, 32, 128], BF16) for _ in range(2)]
        qaT = [sb("qaT", [128, 32, 128], BF16) for _ in range(2)]
        szaT = [sb("szaT", [128, 32, 128], BF16, nseg=8) for _ in range(2)]
        score = sb("score", [128, T])
        work = sb("work", [128, T])
        maskT = [sb("maskT", [128, 16, 128], BF16) for _ in range(2)]
        rt = [sb("rt", [128, 512]) for _ in range(2)]
        m8 = sb("m8", [128, 8])
        qlat = [sb("qlat", [128, 4, 512], BF16) for _ in range(2)]
        et = [sb("et", [128, 512], BF16) for _ in range(3)]
        ptt = [sb("ptt", [128, 512], BF16) for _ in range(3)]
        rs = sb("rs", [128, 512])
        olat = sb("olat", [128, 4, 512], BF16)
        SC = 128.0 ** -0.5
        cnt = [0, 0]
        col = lambda Tl, qb: Tl.v(Tl.h[:, qb * 128:(qb + 1) * 128].rearrange("(h i) t -> i h t", i=128))

        def load_A(qb, k):
            S.dma(qiT[k % 2][:, :, :], col(C.QI, qb))

        def load_B(qb, k):
            S.dma(qaT[k % 2][:, :, :], col(C.QA, qb))
            S.dma(szaT[k % 2][:, :, :], col(C.SZA, qb))

        def gen_A(qb, k):
            Sc = (qb + 1) * 128
            qi_ = qiT[k % 2]
            mT = maskT[k % 2]
            for pc in range((Sc + 511) // 512):
                w = min(512, Sc - pc * 512)
                sv = score[:, pc * 512:pc * 512 + w]
                for h in range(32):
                    p = nextp()
                    r = rt[cnt[0] % 2]
                    cnt[0] += 1
                    S.mm(p[:, 0:w], qi_[:, h, :], kidxT.s(pc * 4, pc * 4 + (w + 127) // 128)[:, pc * 512:pc * 512 + w])
                    S.act(r[:, 0:w], p[:, 0:w], AF.Relu)
                    if h == 0:
                        S.ts("dve", sv, r[:, 0:w], wi_all[:, qb, h:h + 1], ALU.mult)
                    else:
                        S.stt("dve", sv, r[:, 0:w], wi_all[:, qb, h:h + 1], sv, ALU.mult, ALU.add)
                    yield
            S.tt("dve", score[:, qb * 128:(qb + 1) * 128], score[:, qb * 128:(qb + 1) * 128], C.negtri[:, :], ALU.add)
            if qb >= 2:
                S.copy("act", work[:, 0:Sc], score[:, 0:Sc])
                yield
                for it in range(32):
                    S.op("dve", lambda e, Sc=Sc: e.max(out=m8.h[:, :], in_=work.h[:, 0:Sc]), reads=[work[:, :]], writes=[m8[:, :]])
                    if it < 31:
                        S.op("dve", lambda e, Sc=Sc: e.match_replace(out=work.h[:, 0:Sc], in_to_replace=m8.h[:, :],
                                                                   in_values=work.h[:, 0:Sc], imm_value=NEG),
                             reads=[work[:, :], m8[:, :]], writes=[work[:, :]])
                    yield
                S.ts("dve", work[:, 0:Sc], score[:, 0:Sc], m8[:, 7:8], ALU.is_ge)
            else:
                S.ts("dve", work[:, 0:Sc], score[:, 0:Sc], -1.0e29, ALU.is_gt)
            yield
            for b4 in range((qb + 4) // 4):
                nb = min(4, qb + 1 - b4 * 4)
                pM = nextp()
                for k in range(nb):
                    b = b4 * 4 + k
                    S.transpose(pM[:, k * 128:(k + 1) * 128], work[:, b * 128:(b + 1) * 128], C.identf[:, :])
                S.copy("act", mT[:, b4 * 4:b4 * 4 + nb, :],
                       pM.v(pM.h[:, 0:nb * 128].rearrange("p (a b) -> p a b", a=nb)))
                yield

        def gen_B(qb, k):
            qa_ = qaT[k % 2]
            sz_ = szaT[k % 2]
            mT = maskT[k % 2]
            for g in groups:
                ql = qlat[g % 2]
                for cc in range(4):
                    pM = nextp()
                    for hh in range(4):
                        h = g * 4 + hh
                        S.mm(pM[:, hh * 128:(hh + 1) * 128], wuk[:, h, cc * 128:(cc + 1) * 128], qa_[:, h, :])
                    S.copy("act", ql[:, cc, :], pM[:, :])
                yield
                for b in range(qb + 1):
                    pa = nextp()
                    e_ = et[cnt[1] % 3]
                    pt_ = ptt[cnt[1] % 3]
                    cnt[1] += 1
                    for cc in range(4):
                        S.mm(pa[:, :], cT.s(b)[:, cc, b * 128:(b + 1) * 128], ql[:, cc, :], start=(cc == 0), stop=(cc == 3))
                    S.act(e_[:, :], pa[:, :], AF.Exp, scale=SC)
                    mb = mT.v(mT.h[:, b:b + 1, :].to_broadcast([128, 4, 128]))
                    S.tt("dve", pt_.v(pt_.h[:, :].rearrange("p (a b) -> p a b", a=4)),
                         e_.v(e_.h[:, :].rearrange("p (a b) -> p a b", a=4)), mb, ALU.mult)
                    for cc in range(4):
                        S.mm(pO[cc][:, :], c_tm.s(b)[:, b, cc * 128:(cc + 1) * 128], pt_[:, :], start=(b == 0), stop=(b == qb))
                    S.mm(pSum[:, :], C.onesb[:, :], pt_[:, :], start=(b == 0), stop=(b == qb))
                    yield
                S.op("dve", lambda e: e.reciprocal(rs.h[:, :], pSum.h[:, :]), reads=[pSum[:, :]], writes=[rs[:, :]])
                for cc in range(4):
                    S.tt("dve", olat[:, cc, :], pO[cc][:, :], rs[:, :], ALU.mult)
                yield
                pM = nextp()
                for hh in range(4):
                    h = g * 4 + hh
                    for cc in range(4):
                        S.mm(pM[:, hh * 128:(hh + 1) * 128], wuv[:, h, cc, :], olat[:, cc, hh * 128:(hh + 1) * 128],
                             start=(cc == 0), stop=(cc == 3))
                S.tt("dve", sz_.s(g)[:, g * 4:(g + 1) * 4, :], pM.v(pM.h[:, :].rearrange("p (a b) -> p a b", a=4)),
                     sz_.s(g)[:, g * 4:(g + 1) * 4, :], ALU.mult)
                yield
            S.dma(C.YA.v(C.YA.h[:, qb * 128:(qb + 1) * 128].rearrange("(h d) t -> d h t", d=128)), sz_[:, :, :])

        def n_units_A(qb):
            Sc = (qb + 1) * 128
            return 32 * ((Sc + 511) // 512) + (34 if qb >= 2 else 1) + (qb + 4) // 4

        def n_units_B(qb):
            return len(list(groups)) * (qb + 4)

        load_A(qbs[0], 0)
        if len(qbs) > 1:
            load_A(qbs[1], 1)
        load_B(qbs[0], 0)
        for _ in gen_A(qbs[0], 0):
            pass
        for i, qb in enumerate(qbs):
            nxt = qbs[i + 1] if i + 1 < len(qbs) else None
            if nxt is not None:
                load_B(nxt, i + 1)
            if i + 2 < len(qbs):
                load_A(qbs[i + 2], i + 2)
            gA = gen_A(nxt, i + 1) if nxt is not None else None
            ratio = (n_units_A(nxt) / float(n_units_B(qb))) if nxt is not None else 0.0
            acc = 0.0
            first = True
            for _ in gen_B(qb, i):
                if gA is not None:
                    acc += ratio
                    while acc >= 1.0 and gA is not None:
                        acc -= 1.0
                        try:
                            next(gA)
                        except StopIteration:
                            gA = None
            if gA is not None:
                for _ in gA:
                    pass
        S.barrier()


def phase_out(S, C, l, xsrc, xdst):
    with ExitStack() as st:
        sb = lambda n, shp, dt=F32, nseg=1: S.sbuf(n, shp, dt, nseg=nseg, stack=st)
        ymT = sb("ymTo", [128, 32, 1024], BF16)
        yaT = sb("yaTo", [128, 32, 1024], BF16)
        wm = [sb("wm", [128, 32, 128], BF16) for _ in range(2)]
        wa = [sb("wa", [128, 32, 128], BF16) for _ in range(2)]
        sgm = [sb("sgm", [128, 1024], BF16) for _ in range(2)]
        sga = [sb("sga", [128, 1024], BF16) for _ in range(2)]
        t1 = [sb("t1", [128, 1024]) for _ in range(2)]
        t2 = [sb("t2", [128, 1024]) for _ in range(2)]
        mg = [sb("mg", [128, 1024], BF16) for _ in range(2)]
        pm = [S.psum("pm", [128, 1024], F32, nseg=2, stack=st) for _ in range(2)]
        pa = [S.psum("pa", [128, 1024], F32, nseg=2, stack=st) for _ in range(2)]
        for half in range(2):
            th = slice(half * 1024, (half + 1) * 1024)
            S.dma(ymT[:, :, :], C.YM.v(C.YM.h[:, th].rearrange("(kc p) t -> p kc t", p=128)))
            S.dma(yaT[:, :, :], C.YA.v(C.YA.h[:, th].rearrange("(kc p) t -> p kc t", p=128)))
            for fs in range(32):
                i = fs % 2
                fsl = slice(fs * 128, (fs + 1) * 128)
                S.dma(wm[i][:, :, :], C.w_bm.v(C.w_bm.h[l, fs]), eng="pool")
                S.dma(wa[i][:, :, :], C.w_ba.v(C.w_ba.h[l, fs]), eng="pool")
                S.dma(sgm[i][:, :], C.SGM.v(C.SGM.h[fsl, th], fs, fs + 1))
                S.dma(sga[i][:, :], C.SGA.v(C.SGA.h[fsl, th], fs, fs + 1))
                for (pp_, w_, y_) in ((pm[i], wm[i], ymT), (pa[i], wa[i], yaT)):
                    for tt in range(2):
                        for kc in range(32):
                            S.mm(pp_.s(tt)[:, tt * 512:(tt + 1) * 512], w_[:, kc, :], y_[:, kc, tt * 512:(tt + 1) * 512],
                                 start=(kc == 0), stop=(kc == 31))
                S.tt("dve", t1[i][:, :], pm[i][:, :], sgm[i][:, :], ALU.mult)
                S.tt("dve", t2[i][:, :], pa[i][:, :], sga[i][:, :], ALU.mult)
                S.tt("dve", mg[i][:, :], t1[i][:, :], t2[i][:, :], ALU.add)
                S.dma(C.MG.v(C.MG.h[fsl, th], fs, fs + 1), mg[i][:, :])
        S.barrier()
    with ExitStack() as st:
        sb = lambda n, shp, dt=F32, nseg=1: S.sbuf(n, shp, dt, nseg=nseg, stack=st)
        mgT = sb("mgT", [128, 32, 1024], BF16)
        wo = [sb("wo", [128, 32, 512], BF16) for _ in range(2)]
        xo = [sb("xo", [128, 512]) for _ in range(3)]
        po = [S.psum("po", [128, 512], F32, stack=st) for _ in range(4)]
        n = 0
        for half in range(2):
            th = slice(half * 1024, (half + 1) * 1024)
            S.dma(mgT[:, :, :], C.MG.v(C.MG.h[:, th].rearrange("(kc p) t -> p kc t", p=128)))
            for fs in range(8):
                fsl = slice(fs * 512, (fs + 1) * 512)
                w_ = wo[fs % 2]
                S.dma(w_[:, :, :], C.w_out.v(C.w_out.h[l, fs]), eng="pool")
                for tt in range(8):
                    tok = half * 1024 + tt * 128
                    seg = tok // 128
                    x_ = xo[n % 3]
                    p_ = po[n % 4]
                    n += 1
                    S.dma(x_[:, :], xsrc.v(xsrc.h[tok:tok + 128, fsl], seg, seg + 1))
                    for kc in range(32):
                        S.mm(p_[:, :], mgT[:, kc, tt * 128:(tt + 1) * 128], w_[:, kc, :], start=(kc == 0), stop=(kc == 31))
                    S.tt("dve", x_[:, :], p_[:, :], x_[:, :], ALU.add)
                    S.dma(xdst.v(xdst.h[tok:tok + 128, fsl], seg, seg + 1), x_[:, :])
        S.barrier()


def phase_final(S, C, xsrc):
    with ExitStack() as st:
        xt = [S.sbuf("xtf", [128, D], F32, stack=st) for _ in range(2)]
        junk = S.sbuf("junkf", [128, D], BF16, stack=st)
        gB = S.sbuf("gBf", [128, D], F32, stack=st)
        ss = [S.sbuf("ssf", [128, 16], F32, stack=st) for _ in range(2)]
        S.dma(gB[:, :], C.gfB[:, :])
        for tt in range(16):
            x_ = xt[tt % 2]
            s_ = ss[tt % 2]
            S.dma(x_[:, :], xsrc.v(xsrc.h[tt * 128:(tt + 1) * 128, :], tt, tt + 1))
            S.memset("dve", s_[:, :], 0.0)
            S.act(junk[:, :], x_[:, :], AF.Square, accum=s_[:, 0:1])
            S.rsqrt(s_[:, 1:2], s_[:, 0:1], 1.0 / D, EPS)
            S.stt("dve", x_[:, :], x_[:, :], s_[:, 1:2], gB[:, :], ALU.mult, ALU.mult)
            S.dma(C.out.v(C.out.h[tt * 128:(tt + 1) * 128, :], tt, tt + 1), x_[:, :])
        S.barrier()
def _bd(w):
    out = np.zeros((2, 32, 128, 128), np.float32)
    wr = w.reshape(2, 32, 32, 4, 4)
    for g in range(32):
        out[:, :, 4 * g:4 * g + 4, 4 * g:4 * g + 4] = wr[:, :, g]
    return out


def prep_shared(inp):
    f = lambda a: np.ascontiguousarray(np.asarray(a, dtype=np.float32))
    sh = {}
    w_in = f(inp["w_in"])
    sh["w_in"] = w_in
    sh["wmisc"] = f(np.concatenate([w_in[:, :, 12288:12304], w_in[:, :, 25232:25264], w_in[:, :, 25104:25232]], axis=2))
    slab = lambda w, fw: f(np.asarray(w, np.float32).reshape(2, 32, 128, D // fw, fw).transpose(0, 3, 2, 1, 4))
    sh["w_bm"] = slab(inp["w_bm"], 128); sh["w_ba"] = slab(inp["w_ba"], 128); sh["w_out"] = slab(inp["w_out"], 512)
    sh["w_ukT"] = f(np.transpose(np.asarray(inp["w_uk"]), (0, 1, 3, 2)))
    sh["w_uv"] = f(inp["w_uv"])
    sh["wq_bd"] = _bd(np.asarray(inp["w_q_m"], np.float32))
    sh["wk_bd"] = _bd(np.asarray(inp["w_k_m"], np.float32))
    sh["wv_bd"] = _bd(np.asarray(inp["w_v_m"], np.float32))
    pp = np.zeros((2, 128, 32, 8), np.float32)
    cw = np.asarray(inp["conv_w"], np.float32)
    for k in range(4):
        pp[:, :, :, k] = cw[:, k].reshape(2, 32, 128).transpose(0, 2, 1)
    pp[:, :, :, 4] = np.asarray(inp["conv_b"], np.float32).reshape(2, 32, 128).transpose(0, 2, 1)
    pp[:, :, :, 5] = np.asarray(inp["g_head_m"], np.float32).reshape(2, 32, 128).transpose(0, 2, 1)
    pp[:, :, :, 6] = np.asarray(inp["skip_m"], np.float32).reshape(2, 32, 128).transpose(0, 2, 1)
    sh["pp"] = pp
    bc = lambda v, shape: f(np.broadcast_to(np.asarray(v, np.float32), shape))
    sh["gnB"] = bc(np.asarray(inp["g_norm"])[:, None, :], (2, 128, 4096))
    sh["gfB"] = bc(np.asarray(inp["g_final"])[None, :], (128, 4096))
    sh["biB"] = bc(np.asarray(inp["b_i"])[:, None, None, :], (2, 128, 16, 8))
    sh["bfB"] = bc(np.asarray(inp["b_f"])[:, None, None, :], (2, 128, 16, 8))
    sh["gckvB"] = bc(np.asarray(inp["g_ckv"])[:, None, :], (2, 128, 512))
    sh["gkB"] = bc(np.asarray(inp["g_kidx"])[:, None, :], (2, 128, 128))
    sh["bkB"] = bc(np.asarray(inp["b_kidx"])[:, None, :], (2, 128, 128))
    return sh


_CACHE = {}


def build_program():
    nc = bass.Bass("TRN2", target_bir_lowering=False)
    S = Sched(nc)
    C = declare(S)
    consts(S, C)
    xs = [C.x, C.X1, C.X2]
    for l in range(2):
        with ExitStack() as st:
            xnT = S.sbuf("xnT", [128, 32, 2048], BF16, nseg=16, stack=st)
            phase_norm(S, C, l, xs[l], xnT)
            phase_proj(S, C, l, xnT)
        phase_mlstm(S, C, l)
        phase_dsa(S, C, l)
        phase_out(S, C, l, xs[l], xs[l + 1])
    phase_final(S, C, xs[2])
    S.finish()
    S.emit()
    return nc


def kernel(**inputs):
    n = 8
    if "nc" not in _CACHE:
        _CACHE["nc"] = build_program()
    nc = _CACHE["nc"]
    shared = prep_shared(inputs)
    x = np.asarray(inputs["x"], dtype=np.float32)
    in_maps = []
    for c in range(n):
        m = dict(shared)
        m["x"] = np.ascontiguousarray(x[c])
        in_maps.append(m)
    res = run_bass_kernel_spmd(nc, in_maps, core_ids=list(range(n)))
    out = np.stack([np.asarray(res.results[c]["out"], dtype=np.float32) for c in range(n)], axis=0)
    return out
```

```python
import numpy as np
import numpy as np
import concourse.bass as bass
import concourse.mybir as mybir
from concourse.bass_utils import run_bass_kernel_spmd
from contextlib import ExitStack

F32 = mybir.dt.float32
BF16 = mybir.dt.bfloat16
AF = mybir.ActivationFunctionType
ALU = mybir.AluOpType
AX = mybir.AxisListType

SEM_CAP = 30000
DMA_RING = 8
COMPUTE = ("pe", "act", "dve", "pool")


class View:
    __slots__ = ("ap", "buf", "lo", "hi")

    def __init__(self, ap, buf, lo, hi):
        self.ap, self.buf, self.lo, self.hi = ap, buf, lo, hi


class _Seg:
    def __init__(self, t, lo, hi):
        self.t, self.lo, self.hi = t, lo, hi

    def __getitem__(self, idx):
        return View(self.t.h[idx], self.t, self.lo, self.hi)


class Tile:
    def __init__(self, handle, nseg=1, name=""):
        self.h = handle
        self.nseg = nseg
        self.name = name
        self.lw = [None] * nseg
        self.rd = [dict() for _ in range(nseg)]
        self.psum = False
        self.prd = {}

    def __getitem__(self, idx):
        return View(self.h[idx], self, 0, self.nseg)

    def s(self, lo, hi=None):
        if hi is None:
            hi = lo + 1
        assert 0 <= lo < hi <= self.nseg, (self.name, lo, hi, self.nseg)
        return _Seg(self, lo, hi)

    def v(self, ap, lo=0, hi=None):
        return View(ap, self, lo, self.nseg if hi is None else hi)


class Op:
    __slots__ = ("eng", "fn", "deps", "dma", "signal", "sem", "val", "gidx", "qidx")

    def __init__(self, eng, fn, dma):
        self.eng, self.fn, self.dma = eng, fn, dma
        self.deps = set()
        self.signal = False
        self.sem = None
        self.val = 0
        self.gidx = 0
        self.qidx = 0


class Sched:
    def __init__(self, nc):
        self.nc = nc
        self.ops = []
        self.stack = ExitStack()
        self.pending_barrier = {}
        self.last_op = {}
        self.all_dma = []
        self.ndma = {}
        self._n = 0

    def sbuf(self, name, shape, dtype, nseg=1, stack=None):
        st = stack if stack is not None else self.stack
        self._n += 1
        h = st.enter_context(self.nc.sbuf_tensor(f"{name}_{self._n}", list(shape), dtype))
        fb = int(np.prod(shape[1:])) * (4 if dtype == F32 else 2)
        if fb % 64 != 0:
            st.enter_context(self.nc.sbuf_tensor(f"pad_{self._n}", [128, (64 - fb % 64) // 2], BF16))
        return Tile(h, nseg, name)

    def psum(self, name, shape, dtype, nseg=1, stack=None):
        st = stack if stack is not None else self.stack
        self._n += 1
        fb = int(np.prod(shape[1:])) * (4 if dtype == F32 else 2)
        assert fb % 2048 == 0, ("psum tiles must be whole banks", name, shape)
        h = st.enter_context(self.nc.psum_tensor(f"{name}_{self._n}", list(shape), dtype))
        t = Tile(h, nseg, name)
        t.psum = True
        t.nbank = fb // 2048
        return t

    def dram(self, name, shape, dtype, kind="Internal", nseg=1):
        h = self.nc.dram_tensor(name, list(shape), dtype, kind=kind)
        return Tile(h, nseg, name)

    def op(self, eng, fn, reads=(), writes=(), dma=False):
        o = Op(eng, fn, dma)
        o.gidx = len(self.ops)
        pb = self.pending_barrier.pop(eng, None)
        if pb:
            o.deps |= pb
        for v in reads:
            b = v.buf
            for sg in range(v.lo, v.hi):
                w = b.lw[sg]
                if w is not None:
                    o.deps.add(w)
            if b.psum:
                banks = range(b.nbank) if b.nseg != b.nbank else range(v.lo, v.hi)
                for bk in banks:
                    d = b.prd.setdefault(bk, {})
                    for e2, r in d.items():
                        if e2 != eng:
                            o.deps.add(r)
                    d[eng] = o
        for v in writes:
            b = v.buf
            for sg in range(v.lo, v.hi):
                w = b.lw[sg]
                if w is not None:
                    o.deps.add(w)
                for r in b.rd[sg].values():
                    o.deps.add(r)
        for v in reads:
            b = v.buf
            for sg in range(v.lo, v.hi):
                key = ("dma", o.gidx) if dma else eng
                b.rd[sg][key] = o
        for v in writes:
            b = v.buf
            for sg in range(v.lo, v.hi):
                b.lw[sg] = o
                b.rd[sg] = {}
        o.deps.discard(o)
        self.ops.append(o)
        self.last_op[eng] = o
        if dma:
            self.all_dma.append(o)
        return o

    def barrier(self):
        deps = set(self.last_op.values()) | set(self.all_dma)
        self.all_dma = []
        for e in ("pe", "act", "dve", "pool", "sp"):
            self.pending_barrier[e] = set(deps) | self.pending_barrier.get(e, set())

    def mm(self, out, lhsT, rhs, start=True, stop=True):
        return self.op("pe", lambda e: e.matmul(out.ap, lhsT.ap, rhs.ap, start=start, stop=stop),
                       reads=[lhsT, rhs], writes=[out])

    def transpose(self, out, in_, ident):
        return self.op("pe", lambda e: e.transpose(out.ap, in_.ap, ident.ap),
                       reads=[in_, ident], writes=[out])

    def act(self, out, in_, func, bias=None, scale=None, accum=None, eng="act"):
        reads = [in_]
        kw = {}
        if bias is not None:
            if isinstance(bias, View):
                reads.append(bias)
                kw["bias"] = bias.ap
            else:
                kw["bias"] = bias
        if scale is not None:
            if isinstance(scale, View):
                reads.append(scale)
                kw["scale"] = scale.ap
            else:
                kw["scale"] = scale
        writes = [out]
        if accum is not None:
            writes.append(accum)
            kw["accum_out"] = accum.ap
        return self.op("act", lambda e: e.activation(out.ap, in_.ap, func, **kw), reads=reads, writes=writes)

    def tt(self, eng, out, in0, in1, op):
        return self.op(eng, lambda e: e.tensor_tensor(out.ap, in0.ap, in1.ap, op), reads=[in0, in1], writes=[out])

    def ts(self, eng, out, in0, s1, op0, s2=None, op1=None, accum=None):
        reads = [in0]
        a1 = s1.ap if isinstance(s1, View) else s1
        a2 = s2.ap if isinstance(s2, View) else s2
        if isinstance(s1, View):
            reads.append(s1)
        if isinstance(s2, View):
            reads.append(s2)
        writes = [out]
        kw = {}
        if op1 is not None:
            kw["op1"] = op1
        if accum is not None:
            kw["accum_out"] = accum.ap
            writes.append(accum)
        return self.op(eng, lambda e: e.tensor_scalar(out.ap, in0.ap, a1, a2, op0, **kw), reads=reads, writes=writes)

    def stt(self, eng, out, in0, scalar, in1, op0, op1, accum=None):
        reads = [in0, in1]
        a = scalar.ap if isinstance(scalar, View) else scalar
        if isinstance(scalar, View):
            reads.append(scalar)
        writes = [out]
        kw = {}
        if accum is not None:
            kw["accum_out"] = accum.ap
            writes.append(accum)
        return self.op(eng, lambda e: e.scalar_tensor_tensor(out.ap, in0.ap, a, in1.ap, op0, op1, **kw),
                       reads=reads, writes=writes)

    def rsqrt(self, out, in_, scale, eps):
        self.act(out, in_, AF.Sqrt, bias=eps, scale=scale)
        return self.op("dve", lambda e: e.reciprocal(out.ap, out.ap), reads=[out], writes=[out])

    def copy(self, eng, out, in_):
        if eng == "act":
            return self.op("act", lambda e: e.copy(out.ap, in_.ap), reads=[in_], writes=[out])
        return self.op(eng, lambda e: e.tensor_copy(out.ap, in_.ap), reads=[in_], writes=[out])

    def memset(self, eng, out, val):
        return self.op(eng, lambda e: e.memset(out.ap, val), writes=[out])

    def dma(self, out, in_, eng="sp"):
        return self.op(eng, lambda e: e.dma_start(out=out.ap, in_=in_.ap), reads=[in_], writes=[out], dma=True)

    def emit(self):
        nc = self.nc
        ops = self.ops
        for o in ops:
            for d in o.deps:
                if d.eng == "pe" and o.eng == "pe" and not d.dma and not o.dma:
                    continue
                d.signal = True
        est = ExitStack()
        sems = {}

        def new_sem(tag):
            sems[tag] = est.enter_context(nc.semaphore(tag))
            return sems[tag]

        cnt = {e: 0 for e in COMPUTE + ("sp",)}
        cur = {}
        dcount = {}
        dring = {}
        for o in ops:
            if o.dma:
                q = o.eng
                i = dcount.get(q, 0)
                dcount[q] = i + 1
                slot = i % DMA_RING
                if (q, slot) not in dring:
                    dring[(q, slot)] = [new_sem(f"d_{q}_{slot}"), 0, None]
                ent = dring[(q, slot)]
                prev = ent[2]
                if prev is not None:
                    o.deps.add(prev)
                    prev.signal = True
                ent[1] += 16
                ent[2] = o
                o.sem, o.val = ent[0], ent[1]
                o.signal = True
            elif o.signal:
                e = o.eng
                c = cnt[e]
                if c % SEM_CAP == 0:
                    cur[e] = new_sem(f"s_{e}_{c // SEM_CAP}")
                cnt[e] = c + 1
                o.sem, o.val = cur[e], (c % SEM_CAP) + 1
                o.qidx = c + 1
        by_eng = {e: [] for e in ("pe", "act", "dve", "pool", "sp")}
        for o in ops:
            by_eng[o.eng].append(o)
        nwaits = [0]

        self.streams = {}

        def emit_engine(ename, eobj):
            stream = self.streams.setdefault(ename, [])
            waited_c = {e: 0 for e in COMPUTE + ("sp",)}
            waited_d = {}
            for o in by_eng[ename]:
                need_c = {}
                need_d = {}
                for d in o.deps:
                    if d.dma:
                        k = id(d.sem)
                        if d.val > waited_d.get(k, 0):
                            if k not in need_d or need_d[k][1] < d.val:
                                need_d[k] = (d.sem, d.val)
                    else:
                        if d.eng == "pe" and ename == "pe" and not o.dma:
                            continue
                        if not d.signal:
                            continue
                        if d.qidx > waited_c[d.eng]:
                            if d.eng not in need_c or need_c[d.eng].qidx < d.qidx:
                                need_c[d.eng] = d
                for e, d in need_c.items():
                    eobj.wait_ge(d.sem, d.val)
                    waited_c[e] = d.qidx
                    nwaits[0] += 1
                for k, (sem, val) in need_d.items():
                    eobj.wait_ge(sem, val)
                    waited_d[k] = val
                    nwaits[0] += 1
                ins = o.fn(eobj)
                if o.signal:
                    ins.then_inc(o.sem, 16 if o.dma else 1)
                stream.append(([(id(d.sem), d.val) for d in need_c.values()] + [(k, v) for k, (sm_, v) in need_d.items()],
                               (id(o.sem), 16 if o.dma else 1) if o.signal else None, o.gidx))

        with nc.Block() as block:
            @block.tensor
            def _(e):
                emit_engine("pe", e)

            @block.scalar
            def _(e):
                emit_engine("act", e)

            @block.vector
            def _(e):
                emit_engine("dve", e)

            @block.gpsimd
            def _(e):
                emit_engine("pool", e)

            @block.sync
            def _(e):
                emit_engine("sp", e)
        self.nwaits = nwaits[0]
        est.close()

    def simulate(self):
        sem = {}
        pos = {e: 0 for e in self.streams}
        progress = True
        while progress:
            progress = False
            for e, st in self.streams.items():
                while pos[e] < len(st):
                    waits, sig, gidx = st[pos[e]]
                    if all(sem.get(k, 0) >= v for k, v in waits):
                        if sig:
                            sem[sig[0]] = sem.get(sig[0], 0) + sig[1]
                        pos[e] += 1
                        progress = True
                    else:
                        break
        stuck = {e: (pos[e], len(st)) for e, st in self.streams.items() if pos[e] < len(st)}
        if stuck:
            for e in stuck:
                waits, sig, gidx = self.streams[e][pos[e]]
                print("STUCK", e, stuck[e], "gidx", gidx, [(k, v, sem.get(k, 0)) for k, v in waits])
        return not stuck

    def finish(self):
        self.barrier()
        self.op("sp", lambda e: e.nop())
T = 2048
D = 4096
NIN = 33456
EPS = 1e-6
NEG = -1.0e30
OFF = dict(xm=0, om=4096, zm=8192, ig=12288, fg=12296, qa=12304, ckv=16400, za=16912,
           qi=21008, ki=25104, wi=25232, gm=25264, ga=29360)


class Ctx:
    pass


def declare(S, dbg=()):
    C = Ctx()
    ein = lambda n, s: S.dram(n, s, F32, kind="ExternalInput")
    C.x = S.dram("x", [T, D], F32, kind="ExternalInput", nseg=16)
    C.w_in = ein("w_in", [2, D, NIN])
    C.wmisc = ein("wmisc", [2, D, 176])
    C.w_bm = ein("w_bm", [2, 32, 128, 32, 128])
    C.w_ba = ein("w_ba", [2, 32, 128, 32, 128])
    C.w_out = ein("w_out", [2, 8, 128, 32, 512])
    C.w_ukT = ein("w_ukT", [2, 32, 128, 512])
    C.w_uv = ein("w_uv", [2, 32, 512, 128])
    C.wq_bd = ein("wq_bd", [2, 32, 128, 128])
    C.wk_bd = ein("wk_bd", [2, 32, 128, 128])
    C.wv_bd = ein("wv_bd", [2, 32, 128, 128])
    C.pp = ein("pp", [2, 128, 32, 8])
    C.gnB = ein("gnB", [2, 128, D])
    C.gfB = ein("gfB", [128, D])
    C.biB = ein("biB", [2, 128, 16, 8])
    C.bfB = ein("bfB", [2, 128, 16, 8])
    C.gckvB = ein("gckvB", [2, 128, 512])
    C.gkB = ein("gkB", [2, 128, 128])
    C.bkB = ein("bkB", [2, 128, 128])
    C.out = S.dram("out", [T, D], F32, kind="ExternalOutput", nseg=16)

    def scr(n, shape, dt, nseg):
        kind = "ExternalOutput" if n in dbg else "Internal"
        return S.dram(n, shape, dt, kind=kind, nseg=nseg)
    for n in ("XM", "XC", "SOM", "SZM", "QA", "SZA", "QI", "SGM", "SGA", "YM", "YA", "MG"):
        setattr(C, n, scr(n, [D, T], BF16, 32))
    C.CTM = scr("CTM", [T, 512], BF16, 16)
    C.KTM = scr("KTM", [T, 128], BF16, 16)
    C.GT = scr("GT", [T, 16], F32, 16)
    C.WI = scr("WI", [T, 32], F32, 16)
    C.X1 = scr("X1", [T, D], F32, 16)
    C.X2 = scr("X2", [T, D], F32, 16)
    return C


def consts(S, C):
    C.identf = S.sbuf("identf", [128, 128], F32)
    C.ident = S.sbuf("ident", [128, 128], BF16)
    C.utf = S.sbuf("utf", [128, 128], F32)
    C.negtri = S.sbuf("negtri", [128, 128], F32)
    C.onesf = S.sbuf("onesf", [128, 128], F32)
    C.onesb = S.sbuf("onesb", [128, 128], BF16)
    S.memset("pool", C.identf[:, :], 1.0)
    S.op("pool", lambda e: e.affine_select(C.identf.h[:, :], C.identf.h[:, :], pattern=[[-1, 128]],
                                           compare_op=ALU.is_equal, fill=0.0, base=0, channel_multiplier=1),
         reads=[C.identf[:, :]], writes=[C.identf[:, :]])
    S.copy("dve", C.ident[:, :], C.identf[:, :])
    S.memset("pool", C.utf[:, :], 1.0)
    S.op("pool", lambda e: e.affine_select(C.utf.h[:, :], C.utf.h[:, :], pattern=[[1, 128]],
                                           compare_op=ALU.is_ge, fill=0.0, base=0, channel_multiplier=-1),
         reads=[C.utf[:, :]], writes=[C.utf[:, :]])
    S.memset("pool", C.negtri[:, :], 0.0)
    S.op("pool", lambda e: e.affine_select(C.negtri.h[:, :], C.negtri.h[:, :], pattern=[[-1, 128]],
                                           compare_op=ALU.is_ge, fill=NEG, base=0, channel_multiplier=1),
         reads=[C.negtri[:, :]], writes=[C.negtri[:, :]])
    S.memset("pool", C.onesf[:, :], 1.0)
    S.memset("pool", C.onesb[:, :], 1.0)


def phase_norm(S, C, l, xsrc, xnT):
    with ExitStack() as st:
        xt = [S.sbuf("xt", [128, D], F32, stack=st) for _ in range(2)]
        xb = [S.sbuf("xb", [128, D], BF16, stack=st) for _ in range(2)]
        gB = S.sbuf("gB", [128, D], F32, stack=st)
        ss = [S.sbuf("ss", [128, 16], F32, stack=st) for _ in range(2)]
        pT = [S.psum("pT", [128, 512], F32, stack=st) for _ in range(2)]
        S.dma(gB[:, :], C.gnB.v(C.gnB.h[l]))
        for tt in range(16):
            x_ = xt[tt % 2]
            b_ = xb[tt % 2]
            s_ = ss[tt % 2]
            S.dma(x_[:, :], xsrc.v(xsrc.h[tt * 128:(tt + 1) * 128, :], tt, tt + 1))
            S.memset("dve", s_[:, :], 0.0)
            S.act(b_[:, :], x_[:, :], AF.Square, accum=s_[:, 0:1])
            S.rsqrt(s_[:, 1:2], s_[:, 0:1], 1.0 / D, EPS)
            S.stt("dve", b_[:, :], x_[:, :], s_[:, 1:2], gB[:, :], ALU.mult, ALU.mult)
            for g in range(8):
                p = pT[g % 2]
                for q in range(4):
                    kc = g * 4 + q
                    S.mm(p[:, q * 128:(q + 1) * 128], b_[:, kc * 128:(kc + 1) * 128], C.ident[:, :])
                dst = xnT.v(xnT.h[:, g * 4:(g + 1) * 4, tt * 128:(tt + 1) * 128], tt, tt + 1)
                src = p.v(p.h[:, 0:512].rearrange("p (a b) -> p a b", a=4))
                S.copy("act", dst, src)
        S.barrier()


FM_SEGS = [("xm", "XM", None), ("om", "SOM", AF.Sigmoid), ("zm", "SZM", AF.Silu), ("qa", "QA", AF.Copy),
           ("za", "SZA", AF.Silu), ("qi", "QI", AF.Copy), ("gm", "SGM", AF.Sigmoid), ("ga", "SGA", AF.Sigmoid)]


def phase_proj(S, C, l, xnT, segs=None):
    with ExitStack() as st:
        wbig = S.sbuf("wbig", [128, 32, 512], BF16, nseg=2, stack=st)
        ps = [S.psum("ps", [128, 2048], F32, nseg=4, stack=st) for _ in range(2)]
        nslab = [0]

        def run_seg(nm, scr, func, ev, xs=None, acc=None, ppt=None):
            evi = [0]

            def next_ev():
                evi[0] += 1
                return ev[evi[0] % len(ev)]
            scrT = getattr(C, scr)
            for sl in range(16):
                c0 = OFF[nm] + sl * 256
                half = nslab[0] % 2
                nslab[0] += 1
                wv = wbig.s(half)[:, :, half * 256:(half + 1) * 256]
                src = C.w_in.h[l][:, c0:c0 + 256].rearrange("(kc p) f -> p kc f", p=128)
                S.dma(wv, C.w_in.v(src), eng="pool")
                for sub in range(2):
                    p = ps[sub]
                    fchunk = sl * 2 + sub
                    for tt in range(4):
                        for kc in range(32):
                            S.mm(p.s(tt)[:, tt * 512:(tt + 1) * 512],
                                 wbig.s(half)[:, kc, half * 256 + sub * 128: half * 256 + (sub + 1) * 128],
                                 xnT.s(tt * 4, tt * 4 + 4)[:, kc, tt * 512:(tt + 1) * 512],
                                 start=(kc == 0), stop=(kc == 31))
                    rows = scrT.v(scrT.h[fchunk * 128:(fchunk + 1) * 128, :], fchunk, fchunk + 1)
                    if nm != "xm":
                        e = next_ev()
                        for tt in range(4):
                            S.act(e[:, tt * 512:(tt + 1) * 512], p.s(tt)[:, tt * 512:(tt + 1) * 512], func)
                        S.dma(rows, e[:, :])
                    else:
                        x_ = xs[sub]
                        a_ = acc[0]
                        for tt in range(4):
                            S.act(x_[:, 4 + tt * 512:4 + (tt + 1) * 512], p.s(tt)[:, tt * 512:(tt + 1) * 512], AF.Copy)
                        e = next_ev()
                        S.copy("pool", e[:, :], x_[:, 4:4 + T])
                        S.dma(rows, e[:, :])
                        P_ = lambda k, fc=fchunk: ppt[:, fc, k:k + 1]
                        S.ts("dve", a_[:, :], x_[:, 4:4 + T], P_(3), ALU.mult, P_(4), ALU.add)
                        S.stt("dve", a_[:, :], x_[:, 3:3 + T], P_(2), a_[:, :], ALU.mult, ALU.add)
                        S.stt("dve", a_[:, :], x_[:, 2:2 + T], P_(1), a_[:, :], ALU.mult, ALU.add)
                        S.stt("dve", a_[:, :], x_[:, 1:1 + T], P_(0), a_[:, :], ALU.mult, ALU.add)
                        e2 = next_ev()
                        S.act(e2[:, :], a_[:, :], AF.Silu)
                        rows2 = C.XC.v(C.XC.h[fchunk * 128:(fchunk + 1) * 128, :], fchunk, fchunk + 1)
                        S.dma(rows2, e2[:, :])

        if segs is None or "xm" in segs:
            with ExitStack() as st2:
                ev = [S.sbuf("ev", [128, 2048], BF16, stack=st2) for _ in range(2)]
                xs = [S.sbuf("xs", [128, 2048 + 4], F32, stack=st2) for _ in range(2)]
                acc = [S.sbuf("acc", [128, 2048], F32, stack=st2) for _ in range(1)]
                ppt = S.sbuf("ppt", [128, 32, 8], F32, stack=st2)
                S.dma(ppt[:, :, :], C.pp.v(C.pp.h[l]))
                for b in xs:
                    S.memset("pool", b[:, 0:4], 0.0)
                run_seg("xm", "XM", None, ev, xs, acc, ppt)
                S.barrier()
        with ExitStack() as st2:
            ev = [S.sbuf("ev", [128, 2048], BF16, stack=st2) for _ in range(3)]
            for (nm, scr, func) in FM_SEGS[1:]:
                if segs is not None and nm not in segs:
                    continue
                run_seg(nm, scr, func, ev)
            S.barrier()
        if segs is None or "tm" in segs:
            with ExitStack() as st2:
                phase_proj_tm(S, C, l, xnT, wbig, ps, st2)
                S.barrier()


def phase_proj_tm(S, C, l, xnT, wbig, ps, st):
    gck = S.sbuf("gck", [128, 512], F32, stack=st)
    gk = S.sbuf("gk", [128, 128], F32, stack=st)
    bk = S.sbuf("bk", [128, 128], F32, stack=st)
    S.dma(gck[:, :], C.gckvB.v(C.gckvB.h[l]))
    S.dma(gk[:, :], C.gkB.v(C.gkB.h[l]))
    S.dma(bk[:, :], C.bkB.v(C.bkB.h[l]))
    st4 = [S.sbuf("st4", [128, 8], F32, stack=st) for _ in range(2)]
    junk = [S.sbuf("junk2", [128, 512], F32, stack=st) for _ in range(2)]
    cb = [S.sbuf("cb", [128, 512], BF16, stack=st) for _ in range(2)]
    kb = [S.sbuf("kb", [128, 128], BF16, stack=st) for _ in range(2)]
    kf = [S.sbuf("kf", [128, 128], F32, stack=st) for _ in range(2)]
    gt = [S.sbuf("gt", [128, 48], F32, stack=st) for _ in range(2)]
    src = C.w_in.h[l][:, OFF["ckv"]:OFF["ckv"] + 512].rearrange("(kc p) f -> p kc f", p=128)
    S.dma(wbig[:, :, :], C.w_in.v(src), eng="pool")
    for tt in range(16):
        p = ps[tt % 2]
        bank = (tt // 2) % 4
        pv = p.s(bank)[:, bank * 512:(bank + 1) * 512]
        for kc in range(32):
            S.mm(pv, xnT.s(tt)[:, kc, tt * 128:(tt + 1) * 128], wbig[:, kc, :], start=(kc == 0), stop=(kc == 31))
        s4 = st4[tt % 2]
        S.memset("dve", s4[:, :], 0.0)
        S.act(junk[tt % 2][:, :], pv, AF.Square, accum=s4[:, 0:1])
        S.rsqrt(s4[:, 1:2], s4[:, 0:1], 1.0 / 512, EPS)
        S.stt("dve", cb[tt % 2][:, :], pv, s4[:, 1:2], gck[:, :], ALU.mult, ALU.mult)
        S.dma(C.CTM.v(C.CTM.h[tt * 128:(tt + 1) * 128, :], tt, tt + 1), cb[tt % 2][:, :])
    srcm = C.wmisc.h[l].rearrange("(kc p) f -> p kc f", p=128)
    S.dma(wbig.v(wbig.h[:, :, 0:176]), C.wmisc.v(srcm), eng="pool")
    for tt in range(16):
        p = ps[tt % 2]
        bank = (tt // 2) % 4
        pv = p.s(bank)[:, bank * 512:bank * 512 + 176]
        for kc in range(32):
            S.mm(pv, xnT.s(tt)[:, kc, tt * 128:(tt + 1) * 128], wbig.v(wbig.h[:, kc, 0:176]),
                 start=(kc == 0), stop=(kc == 31))
        g_ = gt[tt % 2]
        pg = p.s(bank)[:, bank * 512:bank * 512 + 16]
        pw = p.s(bank)[:, bank * 512 + 16:bank * 512 + 48]
        pk = p.s(bank)[:, bank * 512 + 48:bank * 512 + 176]
        S.copy("act", g_[:, 0:16], pg)
        S.act(g_[:, 16:48], pw, AF.Copy, scale=1.0 / 64.0)
        S.dma(C.GT.v(C.GT.h[tt * 128:(tt + 1) * 128, :], tt, tt + 1), g_[:, 0:16])
        S.dma(C.WI.v(C.WI.h[tt * 128:(tt + 1) * 128, :], tt, tt + 1), g_[:, 16:48])
        s4 = st4[tt % 2]
        S.memset("dve", s4[:, :], 0.0)
        S.act(kf[tt % 2][:, :], pk, AF.Copy, accum=s4[:, 0:1])
        S.act(junk[tt % 2][:, 0:128], pk, AF.Square, accum=s4[:, 1:2])
        S.ts("dve", s4[:, 2:3], s4[:, 0:1], 1.0 / 128, ALU.mult)
        S.tt("dve", s4[:, 3:4], s4[:, 2:3], s4[:, 2:3], ALU.mult)
        S.stt("dve", s4[:, 4:5], s4[:, 1:2], 1.0 / 128, s4[:, 3:4], ALU.mult, ALU.subtract)
        S.rsqrt(s4[:, 5:6], s4[:, 4:5], 1.0, EPS)
        S.ts("dve", kf[tt % 2][:, :], kf[tt % 2][:, :], s4[:, 2:3], ALU.subtract, s4[:, 5:6], ALU.mult)
        S.tt("dve", kf[tt % 2][:, :], kf[tt % 2][:, :], gk[:, :], ALU.mult)
        S.tt("dve", kb[tt % 2][:, :], kf[tt % 2][:, :], bk[:, :], ALU.add)
        S.dma(C.KTM.v(C.KTM.h[tt * 128:(tt + 1) * 128, :], tt, tt + 1), kb[tt % 2][:, :])


import math
LNSC = math.log(512.0 ** -0.5)


def phase_mlstm(S, C, l, heads=range(8), stop=99):
    with ExitStack() as st:
        sb = lambda n, shp, dt=F32, nseg=1: S.sbuf(n, shp, dt, nseg=nseg, stack=st)
        G = sb("G", [128, 16, 16])
        biB = sb("biB", [128, 16, 8])
        bfB = sb("bfB", [128, 16, 8])
        S.dma(G[:, :, :], C.GT.v(C.GT.h[:, :].rearrange("(c p) g -> p c g", p=128)))
        S.dma(biB[:, :, :], C.biB.v(C.biB.h[l]))
        S.dma(bfB[:, :, :], C.bfB.v(C.bfB.h[l]))
        ppt = sb("pptm", [128, 32, 8])
        S.dma(ppt[:, :, :], C.pp.v(C.pp.h[l]))
        li = sb("li", [128, 128]); lf = sb("lf", [128, 128]); bsb = sb("bsb", [128, 128]); BL = sb("BL", [128, 128])
        kA = sb("kA", [128, 128]); wS = sb("wS", [128, 128]); qA = sb("qA", [128, 128]); eBL = sb("eBL", [128, 128])
        wSb = sb("wSb", [128, 128], BF16)
        d1 = sb("d1", [128, 128])
        v3 = lambda t: t.v(t.h[:, :].rearrange("p (c h) -> p c h", h=8))
        psA = S.psum("psA", [128, 512], F32, stack=st)
        psB = S.psum("psB", [128, 512], F32, stack=st)
        pS = S.psum("pS", [128, 512], F32, stack=st)
        pN = S.psum("pN", [128, 512], F32, stack=st)
        pDU = S.psum("pDU", [128, 512], F32, nseg=1, stack=st)
        pU = [S.psum("pU", [128, 512], F32, stack=st) for _ in range(2)]
        pT = S.psum("pTm", [128, 512], F32, stack=st)
        S.tt("dve", v3(li), G.v(G.h[:, :, 0:8]), biB[:, :, :], ALU.add)
        S.tt("dve", v3(lf), G.v(G.h[:, :, 8:16]), bfB[:, :, :], ALU.add)
        S.act(lf[:, :], lf[:, :], AF.Exp, scale=-1.0)
        S.act(lf[:, :], lf[:, :], AF.Ln, bias=1.0)
        S.ts("dve", lf[:, :], lf[:, :], -1.0, ALU.mult)
        S.mm(psA[:, 0:128], C.utf[:, :], lf[:, :])
        S.mm(psB[:, 0:128], C.onesf[:, :], lf[:, :])
        S.copy("act", bsb[:, :], psA[:, 0:128])
        S.copy("act", BL[:, :], psB[:, 0:128])
        S.tt("dve", d1[:, :], li[:, :], bsb[:, :], ALU.subtract)
        S.act(kA[:, :], d1[:, :], AF.Exp, bias=LNSC)
        S.tt("dve", d1[:, :], d1[:, :], BL[:, :], ALU.add)
        S.act(wS[:, :], d1[:, :], AF.Exp, bias=LNSC)
        S.copy("dve", wSb[:, :], wS[:, :])
        S.act(qA[:, :], bsb[:, :], AF.Exp)
        S.act(eBL[:, :], BL[:, :], AF.Exp)

        if stop <= 1:
            S.barrier()
            return
        xcT = sb("xcT", [128, 4, T], BF16); xmT = sb("xmT", [128, 4, T], BF16)
        qT = sb("qT", [128, 4, T], BF16, nseg=4); kT = sb("kT", [128, 4, T], BF16, nseg=4)
        som = sb("som", [128, 4, T], BF16); szm = sb("szm", [128, 4, T], BF16)
        xcs = sb("xcs", [128, 4, T], BF16); ymT = sb("ymT", [128, 4, T], BF16, nseg=16)
        wq = sb("wq", [128, 4, 128], BF16); wk = sb("wk", [128, 4, 128], BF16); wv = sb("wv", [128, 4, 128], BF16)
        ktm = [sb("ktm", [128, 512], BF16) for _ in range(2)]
        vtm = [sb("vtm", [128, 512], BF16) for _ in range(2)]
        v3t = [sb("v3t", [128, 520], BF16) for _ in range(2)]
        at = [sb("at", [128, 128], BF16) for _ in range(2)]
        hnt = [sb("hnt", [128, 512], BF16) for _ in range(2)]
        yt = [sb("yt", [128, 4, 128]) for _ in range(2)]
        small = [sb("small", [128, 16]) for _ in range(2)]
        junk = sb("junkm", [128, 512], BF16)
        Cst = sb("Cst", [128, 4, 512], F32, nseg=4)
        Cb = sb("Cb", [128, 4, 520], BF16, nseg=5)
        nst = sb("nst", [128, 4])

        for h in heads:
            rows = lambda Tl: Tl.v(Tl.h[h * 512:(h + 1) * 512, :].rearrange("(j p) t -> p j t", p=128), h * 4, h * 4 + 4)
            S.dma(xcT[:, :, :], rows(C.XC))
            S.dma(xmT[:, :, :], rows(C.XM))
            S.dma(som[:, :, :], rows(C.SOM))
            S.dma(szm[:, :, :], rows(C.SZM))
            for (wt, wsrc) in ((wq, C.wq_bd), (wk, C.wk_bd), (wv, C.wv_bd)):
                S.dma(wt[:, :, :], wsrc.v(wsrc.h[l][h * 4:(h + 1) * 4].rearrange("j p o -> p j o")), eng="pool")
            n = 0
            for (dst, wt) in ((qT, wq), (kT, wk)):
                for j in range(4):
                    for tt in range(4):
                        p = psA if n % 2 == 0 else psB
                        n += 1
                        S.mm(p[:, :], wt[:, j, :], xcT[:, j, tt * 512:(tt + 1) * 512])
                        S.copy("act", dst.s(tt)[:, j, tt * 512:(tt + 1) * 512], p[:, :])
            for j in range(4):
                fc = h * 4 + j
                S.act(som[:, j, :], som[:, j, :], AF.Copy, scale=ppt[:, fc, 5:6])
                S.act(xcs[:, j, :], xcT[:, j, :], AF.Copy, scale=ppt[:, fc, 6:7])
            for c in range(16 if stop > 2.05 else 0):
                tc = slice(c * 128, (c + 1) * 128)
                col = c * 8 + h
                cs = slice(col, col + 1)
                b2 = c % 2
                tseg = c // 4
                for j in range(4):
                    S.mm(psA[:, j * 128:(j + 1) * 128], xcT[:, j, tc], wk[:, j, :])
                S.copy("act", ktm[b2][:, :], psA[:, :])
                for j in range(4):
                    S.mm(psB[:, j * 128:(j + 1) * 128], xmT[:, j, tc], wv[:, j, :])
                S.copy("act", vtm[b2][:, :], psB[:, :])
                if stop <= 2.1:
                    continue
                S.act(v3t[b2][:, 0:512], psB[:, :], AF.Copy, scale=wS[:, cs])
                if stop <= 2.2:
                    continue
                S.copy("dve", v3t[b2][:, 512:513], wS[:, cs])
                if stop <= 2.3:
                    continue
                for j in range(4):
                    S.mm(pS[:, 0:128], kT.s(tseg)[:, j, tc], qT.s(tseg)[:, j, tc], start=(j == 0), stop=(j == 3))
                if stop <= 2.4:
                    continue
                S.stt("dve", at[b2][:, :], pS[:, 0:128], kA[:, cs], C.utf[:, :], ALU.mult, ALU.mult)
                if stop <= 3:
                    continue
                S.mm(pN[:, :], at[b2][:, :], vtm[b2][:, :], start=True, stop=(c == 0))
                if c > 0:
                    for j in range(4):
                        S.mm(pN[:, :], qT.s(tseg)[:, j, tc], Cb.s(j)[:, j, 0:512], start=False, stop=(j == 3))
                S.mm(pDU[:, 0:1], at[b2][:, :], C.onesb[:, 0:1], start=True, stop=(c == 0))
                if c > 0:
                    for j in range(4):
                        S.mm(pDU[:, 0:1], qT.s(tseg)[:, j, tc], Cb.s(4)[:, j, 512:513], start=False, stop=(j == 3))
                sm = small[b2]
                k_ = lambda i: sm[:, i:i + 1]
                S.ts("dve", k_(0), pDU[:, 0:1], qA[:, cs], ALU.mult)
                S.ts("dve", k_(1), k_(0), -1.0, ALU.mult)
                S.tt("dve", k_(2), k_(0), k_(1), ALU.max)
                S.ts("dve", k_(2), k_(2), 1.0, ALU.max)
                S.op("dve", lambda e, o=k_(3), i=k_(2): e.reciprocal(o.ap, i.ap), reads=[k_(2)], writes=[k_(3)])
                S.tt("dve", k_(4), k_(3), qA[:, cs], ALU.mult)
                S.memset("dve", sm[:, 5:7], 0.0)
                S.act(junk[:, :], pN[:, :], AF.Copy, accum=k_(5))
                S.act(junk[:, :], pN[:, :], AF.Square, accum=k_(6))
                S.ts("dve", k_(7), k_(5), 1.0 / 512, ALU.mult)
                S.tt("dve", k_(8), k_(7), k_(7), ALU.mult)
                S.stt("dve", k_(9), k_(6), 1.0 / 512, k_(8), ALU.mult, ALU.subtract)
                S.tt("dve", k_(10), k_(4), k_(4), ALU.mult)
                S.tt("dve", k_(11), k_(9), k_(10), ALU.mult)
                S.rsqrt(k_(12), k_(11), 1.0, EPS)
                S.tt("dve", k_(13), k_(12), k_(4), ALU.mult)
                S.ts("dve", hnt[b2][:, :], pN[:, :], k_(7), ALU.subtract, k_(13), ALU.mult)
                if stop <= 4:
                    continue
                for j in range(4):
                    S.mm(pT[:, j * 128:(j + 1) * 128], hnt[b2][:, j * 128:(j + 1) * 128], C.ident[:, :])
                pTv = pT.v(pT.h[:, 0:512].rearrange("p (a b) -> p a b", a=4))
                S.tt("dve", yt[b2][:, :, :], pTv, som[:, :, tc], ALU.mult)
                S.tt("dve", yt[b2][:, :, :], yt[b2][:, :, :], xcs[:, :, tc], ALU.add)
                S.tt("dve", ymT.s(c)[:, :, tc], yt[b2][:, :, :], szm[:, :, tc], ALU.mult)
                if c < 15 and stop > 5:
                    for j in range(4):
                        pu = pU[j % 2]
                        S.mm(pu[:, :], ktm[b2][:, j * 128:(j + 1) * 128], v3t[b2][:, 0:512])
                        S.mm(pDU[:, 8 + j:9 + j], ktm[b2][:, j * 128:(j + 1) * 128], v3t[b2][:, 512:513])
                        if c == 0:
                            S.copy("act", Cst.s(j)[:, j, :], pu[:, :])
                        else:
                            S.stt("dve", Cst.s(j)[:, j, :], Cst.s(j)[:, j, :], eBL[:, cs], pu[:, :], ALU.mult, ALU.add)
                        S.copy("act", Cb.s(j)[:, j, 0:512], Cst.s(j)[:, j, :])
                    if c == 0:
                        S.copy("dve", nst[:, :], pDU[:, 8:12])
                    else:
                        S.stt("dve", nst[:, :], nst[:, :], eBL[:, cs], pDU[:, 8:12], ALU.mult, ALU.add)
                    S.copy("dve", Cb.s(4)[:, :, 512], nst[:, :])
            S.dma(rows(C.YM), ymT[:, :, :])
        S.barrier()


def phase_dsa(S, C, l, qbs=range(16), groups=range(8)):
    qbs = list(qbs)
    with ExitStack() as st:
        sb = lambda n, shp, dt=F32, nseg=1: S.sbuf(n, shp, dt, nseg=nseg, stack=st)
        c_tm = sb("c_tm", [128, 16, 512], BF16, nseg=16)
        cT = sb("cT", [128, 4, T], BF16, nseg=16)
        kidxT = sb("kidxT", [128, T], BF16, nseg=16)
        wuk = sb("wuk", [128, 32, 512], BF16)
        wuv = sb("wuv", [128, 32, 4, 128], BF16)
        wi_all = sb("wi_all", [128, 16, 32])
        S.dma(c_tm[:, :, :], C.CTM.v(C.CTM.h[:, :].rearrange("(b p) c -> p b c", p=128)))
        S.dma(wi_all[:, :, :], C.WI.v(C.WI.h[:, :].rearrange("(b p) h -> p b h", p=128)))
        S.dma(wuk[:, :, :], C.w_ukT.v(C.w_ukT.h[l].rearrange("h d c -> d h c")), eng="pool")
        for hh_ in range(4):
            S.dma(wuv[:, hh_ * 8:(hh_ + 1) * 8, :, :],
                  C.w_uv.v(C.w_uv.h[l][hh_ * 8:(hh_ + 1) * 8].rearrange("h (cc p) d -> p h cc d", p=128)), eng="pool")
        with ExitStack() as st2:
            kidx_tm = S.sbuf("kidx_tm", [128, 16, 128], BF16, stack=st2)
            pX = [S.psum("pX", [128, 512], F32, stack=st2) for _ in range(2)]
            S.dma(kidx_tm[:, :, :], C.KTM.v(C.KTM.h[:, :].rearrange("(b p) i -> p b i", p=128)))
            for b in range(16):
                p = pX[b % 2]
                for cc in range(4):
                    S.mm(p[:, cc * 128:(cc + 1) * 128], c_tm.s(b)[:, b, cc * 128:(cc + 1) * 128], C.ident[:, :])
                S.copy("act", cT.s(b)[:, :, b * 128:(b + 1) * 128], p.v(p.h[:, 0:512].rearrange("p (a b) -> p a b", a=4)))
            for b4 in range(4):
                p = pX[b4 % 2]
                for k in range(4):
                    b = b4 * 4 + k
                    S.mm(p[:, k * 128:(k + 1) * 128], kidx_tm[:, b, :], C.ident[:, :])
                S.copy("act", kidxT.s(b4 * 4, b4 * 4 + 4)[:, b4 * 512:(b4 + 1) * 512], p[:, 0:512])
            S.barrier()
        pG = [S.psum("pG", [128, 512], F32, stack=st) for _ in range(3)]
        gi = [0]

        def nextp():
            gi[0] += 1
            return pG[gi[0] % 3]
        pO = [S.psum("pO", [128, 512], F32, stack=st) for _ in range(4)]
        pSum = S.psum("pSum", [128, 512], F32, stack=st)
        qiT = [sb("qiT", [128, 32, 128], BF16) for _ in range(2)]
        qaT = [sb("qaT", [128, 32, 128], BF16) for _ in range(2)]
        szaT = [sb("szaT", [128, 32, 128], BF16, nseg=8) for _ in range(2)]
        score = sb("score", [128, T])
        work = sb("work", [128, T])
        maskb = sb("maskb", [128, T], BF16)
        maskT = [sb("maskT", [128, 16, 128], BF16) for _ in range(2)]
        rt = [sb("rt", [128, 512]) for _ in range(2)]
        m8 = sb("m8", [128, 8])
        qlat = [sb("qlat", [128, 4, 512], BF16) for _ in range(2)]
        et = [sb("et", [128, 512], BF16) for _ in range(3)]
        ptt = [sb("ptt", [128, 512], BF16) for _ in range(3)]
        rs = sb("rs", [128, 512])
        olat = sb("olat", [128, 4, 512], BF16)
        SC = 128.0 ** -0.5
        cnt = [0, 0]
        col = lambda Tl, qb: Tl.v(Tl.h[:, qb * 128:(qb + 1) * 128].rearrange("(h i) t -> i h t", i=128))

        def load_A(qb, k):
            S.dma(qiT[k % 2][:, :, :], col(C.QI, qb))

        def load_B(qb, k):
            S.dma(qaT[k % 2][:, :, :], col(C.QA, qb))
            S.dma(szaT[k % 2][:, :, :], col(C.SZA, qb))

        def gen_A(qb, k):
            Sc = (qb + 1) * 128
            qi_ = qiT[k % 2]
            mT = maskT[k % 2]
            for pc in range((Sc + 511) // 512):
                w = min(512, Sc - pc * 512)
                sv = score[:, pc * 512:pc * 512 + w]
                for h in range(32):
                    p = nextp()
                    r = rt[cnt[0] % 2]
                    cnt[0] += 1
                    S.mm(p[:, 0:w], qi_[:, h, :], kidxT.s(pc * 4, pc * 4 + (w + 127) // 128)[:, pc * 512:pc * 512 + w])
                    S.act(r[:, 0:w], p[:, 0:w], AF.Relu)
                    if h == 0:
                        S.ts("dve", sv, r[:, 0:w], wi_all[:, qb, h:h + 1], ALU.mult)
                    else:
                        S.stt("dve", sv, r[:, 0:w], wi_all[:, qb, h:h + 1], sv, ALU.mult, ALU.add)
                    yield
            S.tt("dve", score[:, qb * 128:(qb + 1) * 128], score[:, qb * 128:(qb + 1) * 128], C.negtri[:, :], ALU.add)
            if qb >= 2:
                S.copy("act", work[:, 0:Sc], score[:, 0:Sc])
                yield
                for it in range(32):
                    S.op("dve", lambda e, Sc=Sc: e.max(out=m8.h[:, :], in_=work.h[:, 0:Sc]), reads=[work[:, :]], writes=[m8[:, :]])
                    if it < 31:
                        S.op("dve", lambda e, Sc=Sc: e.match_replace(out=work.h[:, 0:Sc], in_to_replace=m8.h[:, :],
                                                                   in_values=work.h[:, 0:Sc], imm_value=NEG),
                             reads=[work[:, :], m8[:, :]], writes=[work[:, :]])
                    yield
                S.ts("dve", maskb[:, 0:Sc], score[:, 0:Sc], m8[:, 7:8], ALU.is_ge)
            else:
                S.ts("dve", maskb[:, 0:Sc], score[:, 0:Sc], -1.0e29, ALU.is_gt)
            yield
            for b4 in range((qb + 4) // 4):
                nb = min(4, qb + 1 - b4 * 4)
                pM = nextp()
                for k in range(nb):
                    b = b4 * 4 + k
                    S.mm(pM[:, k * 128:(k + 1) * 128], maskb[:, b * 128:(b + 1) * 128], C.ident[:, :])
                S.copy("act", mT[:, b4 * 4:b4 * 4 + nb, :],
                       pM.v(pM.h[:, 0:nb * 128].rearrange("p (a b) -> p a b", a=nb)))
                yield

        def gen_B(qb, k):
            qa_ = qaT[k % 2]
            sz_ = szaT[k % 2]
            mT = maskT[k % 2]
            for g in groups:
                ql = qlat[g % 2]
                for cc in range(4):
                    pM = nextp()
                    for hh in range(4):
                        h = g * 4 + hh
                        S.mm(pM[:, hh * 128:(hh + 1) * 128], wuk[:, h, cc * 128:(cc + 1) * 128], qa_[:, h, :])
                    S.copy("act", ql[:, cc, :], pM[:, :])
                yield
                for b in range(qb + 1):
                    pa = nextp()
                    e_ = et[cnt[1] % 3]
                    pt_ = ptt[cnt[1] % 3]
                    cnt[1] += 1
                    for cc in range(4):
                        S.mm(pa[:, :], cT.s(b)[:, cc, b * 128:(b + 1) * 128], ql[:, cc, :], start=(cc == 0), stop=(cc == 3))
                    S.act(e_[:, :], pa[:, :], AF.Exp, scale=SC)
                    mb = mT.v(mT.h[:, b:b + 1, :].to_broadcast([128, 4, 128]))
                    S.tt("dve", pt_.v(pt_.h[:, :].rearrange("p (a b) -> p a b", a=4)),
                         e_.v(e_.h[:, :].rearrange("p (a b) -> p a b", a=4)), mb, ALU.mult)
                    for cc in range(4):
                        S.mm(pO[cc][:, :], c_tm.s(b)[:, b, cc * 128:(cc + 1) * 128], pt_[:, :], start=(b == 0), stop=(b == qb))
                    S.mm(pSum[:, :], C.onesb[:, :], pt_[:, :], start=(b == 0), stop=(b == qb))
                    yield
                S.op("dve", lambda e: e.reciprocal(rs.h[:, :], pSum.h[:, :]), reads=[pSum[:, :]], writes=[rs[:, :]])
                for cc in range(4):
                    S.tt("dve", olat[:, cc, :], pO[cc][:, :], rs[:, :], ALU.mult)
                yield
                pM = nextp()
                for hh in range(4):
                    h = g * 4 + hh
                    for cc in range(4):
                        S.mm(pM[:, hh * 128:(hh + 1) * 128], wuv[:, h, cc, :], olat[:, cc, hh * 128:(hh + 1) * 128],
                             start=(cc == 0), stop=(cc == 3))
                S.tt("dve", sz_.s(g)[:, g * 4:(g + 1) * 4, :], pM.v(pM.h[:, :].rearrange("p (a b) -> p a b", a=4)),
                     sz_.s(g)[:, g * 4:(g + 1) * 4, :], ALU.mult)
                yield
            S.dma(C.YA.v(C.YA.h[:, qb * 128:(qb + 1) * 128].rearrange("(h d) t -> d h t", d=128)), sz_[:, :, :])

        def n_units_A(qb):
            Sc = (qb + 1) * 128
            return 32 * ((Sc + 511) // 512) + (34 if qb >= 2 else 1) + (qb + 4) // 4

        def n_units_B(qb):
            return len(list(groups)) * (qb + 4)

        load_A(qbs[0], 0)
        if len(qbs) > 1:
            load_A(qbs[1], 1)
        load_B(qbs[0], 0)
        for _ in gen_A(qbs[0], 0):
            pass
        for i, qb in enumerate(qbs):
            nxt = qbs[i + 1] if i + 1 < len(qbs) else None
            if nxt is not None:
                load_B(nxt, i + 1)
            if i + 2 < len(qbs):
                load_A(qbs[i + 2], i + 2)
            gA = gen_A(nxt, i + 1) if nxt is not None else None
            ratio = (n_units_A(nxt) / float(n_units_B(qb))) if nxt is not None else 0.0
            acc = 0.0
            first = True
            for _ in gen_B(qb, i):
                if gA is not None:
                    acc += ratio
                    while acc >= 1.0 and gA is not None:
                        acc -= 1.0
                        try:
                            next(gA)
                        except StopIteration:
                            gA = None
            if gA is not None:
                for _ in gA:
                    pass
        S.barrier()


def phase_out(S, C, l, xsrc, xdst):
    with ExitStack() as st:
        sb = lambda n, shp, dt=F32, nseg=1: S.sbuf(n, shp, dt, nseg=nseg, stack=st)
        ymT = sb("ymTo", [128, 32, 1024], BF16)
        yaT = sb("yaTo", [128, 32, 1024], BF16)
        wm = [sb("wm", [128, 32, 128], BF16) for _ in range(2)]
        wa = [sb("wa", [128, 32, 128], BF16) for _ in range(2)]
        sgm = [sb("sgm", [128, 1024], BF16) for _ in range(2)]
        sga = [sb("sga", [128, 1024], BF16) for _ in range(2)]
        t1 = [sb("t1", [128, 1024]) for _ in range(2)]
        t2 = [sb("t2", [128, 1024]) for _ in range(2)]
        mg = [sb("mg", [128, 1024], BF16) for _ in range(2)]
        pm = [S.psum("pm", [128, 1024], F32, nseg=2, stack=st) for _ in range(2)]
        pa = [S.psum("pa", [128, 1024], F32, nseg=2, stack=st) for _ in range(2)]
        for half in range(2):
            th = slice(half * 1024, (half + 1) * 1024)
            S.dma(ymT[:, :, :], C.YM.v(C.YM.h[:, th].rearrange("(kc p) t -> p kc t", p=128)))
            S.dma(yaT[:, :, :], C.YA.v(C.YA.h[:, th].rearrange("(kc p) t -> p kc t", p=128)))
            for fs in range(32):
                i = fs % 2
                fsl = slice(fs * 128, (fs + 1) * 128)
                S.dma(wm[i][:, :, :], C.w_bm.v(C.w_bm.h[l, fs]), eng="pool")
                S.dma(wa[i][:, :, :], C.w_ba.v(C.w_ba.h[l, fs]), eng="pool")
                S.dma(sgm[i][:, :], C.SGM.v(C.SGM.h[fsl, th], fs, fs + 1))
                S.dma(sga[i][:, :], C.SGA.v(C.SGA.h[fsl, th], fs, fs + 1))
                for (pp_, w_, y_) in ((pm[i], wm[i], ymT), (pa[i], wa[i], yaT)):
                    for tt in range(2):
                        for kc in range(32):
                            S.mm(pp_.s(tt)[:, tt * 512:(tt + 1) * 512], w_[:, kc, :], y_[:, kc, tt * 512:(tt + 1) * 512],
                                 start=(kc == 0), stop=(kc == 31))
                S.tt("dve", t1[i][:, :], pm[i][:, :], sgm[i][:, :], ALU.mult)
                S.tt("dve", t2[i][:, :], pa[i][:, :], sga[i][:, :], ALU.mult)
                S.tt("dve", mg[i][:, :], t1[i][:, :], t2[i][:, :], ALU.add)
                S.dma(C.MG.v(C.MG.h[fsl, th], fs, fs + 1), mg[i][:, :])
        S.barrier()
    with ExitStack() as st:
        sb = lambda n, shp, dt=F32, nseg=1: S.sbuf(n, shp, dt, nseg=nseg, stack=st)
        mgT = sb("mgT", [128, 32, 1024], BF16)
        wo = [sb("wo", [128, 32, 512], BF16) for _ in range(2)]
        xo = [sb("xo", [128, 512]) for _ in range(3)]
        po = [S.psum("po", [128, 512], F32, stack=st) for _ in range(4)]
        n = 0
        for half in range(2):
            th = slice(half * 1024, (half + 1) * 1024)
            S.dma(mgT[:, :, :], C.MG.v(C.MG.h[:, th].rearrange("(kc p) t -> p kc t", p=128)))
            for fs in range(8):
                fsl = slice(fs * 512, (fs + 1) * 512)
                w_ = wo[fs % 2]
                S.dma(w_[:, :, :], C.w_out.v(C.w_out.h[l, fs]), eng="pool")
                for tt in range(8):
                    tok = half * 1024 + tt * 128
                    seg = tok // 128
                    x_ = xo[n % 3]
                    p_ = po[n % 4]
                    n += 1
                    S.dma(x_[:, :], xsrc.v(xsrc.h[tok:tok + 128, fsl], seg, seg + 1))
                    for kc in range(32):
                        S.mm(p_[:, :], mgT[:, kc, tt * 128:(tt + 1) * 128], w_[:, kc, :], start=(kc == 0), stop=(kc == 31))
                    S.tt("dve", x_[:, :], p_[:, :], x_[:, :], ALU.add)
                    S.dma(xdst.v(xdst.h[tok:tok + 128, fsl], seg, seg + 1), x_[:, :])
        S.barrier()


def phase_final(S, C, xsrc):
    with ExitStack() as st:
        xt = [S.sbuf("xtf", [128, D], F32, stack=st) for _ in range(2)]
        junk = S.sbuf("junkf", [128, D], BF16, stack=st)
        gB = S.sbuf("gBf", [128, D], F32, stack=st)
        ss = [S.sbuf("ssf", [128, 16], F32, stack=st) for _ in range(2)]
        S.dma(gB[:, :], C.gfB[:, :])
        for tt in range(16):
            x_ = xt[tt % 2]
            s_ = ss[tt % 2]
            S.dma(x_[:, :], xsrc.v(xsrc.h[tt * 128:(tt + 1) * 128, :], tt, tt + 1))
            S.memset("dve", s_[:, :], 0.0)
            S.act(junk[:, :], x_[:, :], AF.Square, accum=s_[:, 0:1])
            S.rsqrt(s_[:, 1:2], s_[:, 0:1], 1.0 / D, EPS)
            S.stt("dve", x_[:, :], x_[:, :], s_[:, 1:2], gB[:, :], ALU.mult, ALU.mult)
            S.dma(C.out.v(C.out.h[tt * 128:(tt + 1) * 128, :], tt, tt + 1), x_[:, :])
        S.barrier()
def _bd(w):
    out = np.zeros((2, 32, 128, 128), np.float32)
    wr = w.reshape(2, 32, 32, 4, 4)
    for g in range(32):
        out[:, :, 4 * g:4 * g + 4, 4 * g:4 * g + 4] = wr[:, :, g]
    return out


def prep_shared(inp):
    f = lambda a: np.ascontiguousarray(np.asarray(a, dtype=np.float32))
    sh = {}
    w_in = f(inp["w_in"])
    sh["w_in"] = w_in
    sh["wmisc"] = f(np.concatenate([w_in[:, :, 12288:12304], w_in[:, :, 25232:25264], w_in[:, :, 25104:25232]], axis=2))
    slab = lambda w, fw: f(np.asarray(w, np.float32).reshape(2, 32, 128, D // fw, fw).transpose(0, 3, 2, 1, 4))
    sh["w_bm"] = slab(inp["w_bm"], 128); sh["w_ba"] = slab(inp["w_ba"], 128); sh["w_out"] = slab(inp["w_out"], 512)
    sh["w_ukT"] = f(np.transpose(np.asarray(inp["w_uk"]), (0, 1, 3, 2)))
    sh["w_uv"] = f(inp["w_uv"])
    sh["wq_bd"] = _bd(np.asarray(inp["w_q_m"], np.float32))
    sh["wk_bd"] = _bd(np.asarray(inp["w_k_m"], np.float32))
    sh["wv_bd"] = _bd(np.asarray(inp["w_v_m"], np.float32))
    pp = np.zeros((2, 128, 32, 8), np.float32)
    cw = np.asarray(inp["conv_w"], np.float32)
    for k in range(4):
        pp[:, :, :, k] = cw[:, k].reshape(2, 32, 128).transpose(0, 2, 1)
    pp[:, :, :, 4] = np.asarray(inp["conv_b"], np.float32).reshape(2, 32, 128).transpose(0, 2, 1)
    pp[:, :, :, 5] = np.asarray(inp["g_head_m"], np.float32).reshape(2, 32, 128).transpose(0, 2, 1)
    pp[:, :, :, 6] = np.asarray(inp["skip_m"], np.float32).reshape(2, 32, 128).transpose(0, 2, 1)
    sh["pp"] = pp
    bc = lambda v, shape: f(np.broadcast_to(np.asarray(v, np.float32), shape))
    sh["gnB"] = bc(np.asarray(inp["g_norm"])[:, None, :], (2, 128, 4096))
    sh["gfB"] = bc(np.asarray(inp["g_final"])[None, :], (128, 4096))
    sh["biB"] = bc(np.asarray(inp["b_i"])[:, None, None, :], (2, 128, 16, 8))
    sh["bfB"] = bc(np.asarray(inp["b_f"])[:, None, None, :], (2, 128, 16, 8))
    sh["gckvB"] = bc(np.asarray(inp["g_ckv"])[:, None, :], (2, 128, 512))
    sh["gkB"] = bc(np.asarray(inp["g_kidx"])[:, None, :], (2, 128, 128))
    sh["bkB"] = bc(np.asarray(inp["b_kidx"])[:, None, :], (2, 128, 128))
    return sh


_CACHE = {}


def build_program():
    nc = bass.Bass("TRN2", target_bir_lowering=False)
    S = Sched(nc)
    C = declare(S)
    consts(S, C)
    xs = [C.x, C.X1, C.X2]
    for l in range(2):
        with ExitStack() as st:
            xnT = S.sbuf("xnT", [128, 32, 2048], BF16, nseg=16, stack=st)
            phase_norm(S, C, l, xs[l], xnT)
            phase_proj(S, C, l, xnT)
        phase_mlstm(S, C, l)
        phase_dsa(S, C, l)
        phase_out(S, C, l, xs[l], xs[l + 1])
    phase_final(S, C, xs[2])
    S.finish()
    S.emit()
    return nc


def kernel(**inputs):
    n = 8
    if "nc" not in _CACHE:
        _CACHE["nc"] = build_program()
    nc = _CACHE["nc"]
    shared = prep_shared(inputs)
    x = np.asarray(inputs["x"], dtype=np.float32)
    in_maps = []
    for c in range(n):
        m = dict(shared)
        m["x"] = np.ascontiguousarray(x[c])
        in_maps.append(m)
    res = run_bass_kernel_spmd(nc, in_maps, core_ids=list(range(n)))
    out = np.stack([np.asarray(res.results[c]["out"], dtype=np.float32) for c in range(n)], axis=0)
    return out
```
